# Optimizing a Trainium2 kernel written in Bass

```python
import math
import jax, jax.numpy as jnp
from jax import lax
import numpy as np

D_MODEL = 2048
BATCH = 8
SEQ = 2048
DEPTH = 2

CTX_LEN = 256
GRID_W = 64
HEAD_DIM = 128
N_HEADS = D_MODEL // HEAD_DIM
BRANCH_W = N_HEADS * HEAD_DIM
MLA_HEADS = N_HEADS
MLA_NOPE = 128
MLA_ROPE = 64
MLA_V = 128
KV_RANK = D_MODEL // 4
NA_HEADS = N_HEADS
NA_DIM = HEAD_DIM
NA_KR_MAX = 8
NA_KC = 16
GQA_HEADS = N_HEADS
GQA_KV_HEADS = N_HEADS // 4
GQA_GROUP = GQA_HEADS // GQA_KV_HEADS
GQA_DIM = HEAD_DIM
D_FF = 256 * ((8 * D_MODEL // 3 + 255) // 256)
CONV_W = 3
Q_BLOCK = 128
ROPE_THETA = 10000.0
ADA_EPS = 1e-6
POST_EPS = 1e-5
RMS_EPS = 1e-6
NEG_INF = -1e30
ALPHA = (2 * DEPTH) ** 0.25
BETA = (8 * DEPTH) ** -0.25
MLA_SCALE = (MLA_NOPE + MLA_ROPE) ** -0.5
NA_SCALE = NA_DIM ** -0.5
GQA_SCALE = GQA_DIM ** -0.5
SPLITS = (MLA_HEADS * (MLA_NOPE + MLA_ROPE),
          KV_RANK,
          MLA_ROPE,
          3 * NA_HEADS * NA_DIM,
          GQA_HEADS * GQA_DIM,
          2 * GQA_KV_HEADS * GQA_DIM,
          3 * D_MODEL)
SPLIT_IDX = tuple(int(v) for v in np.cumsum(SPLITS)[:-1])
N_IN = int(sum(SPLITS))

kernel_name = "hybrid_mla_na_gqa_convffn_deepnorm_dit"


def _layernorm(x, eps, g=None, b=None):
    xf = x.astype(jnp.float32)
    mu = jnp.mean(xf, -1, keepdims=True)
    var = jnp.mean(jnp.square(xf - mu), -1, keepdims=True)
    y = (xf - mu) * lax.rsqrt(var + eps)
    if g is not None:
        y = y * g.astype(jnp.float32) + b.astype(jnp.float32)
    return y.astype(x.dtype)


def _rmsnorm(x, g):
    xf = x.astype(jnp.float32)
    y = xf * lax.rsqrt(jnp.mean(xf * xf, -1, keepdims=True) + RMS_EPS) * g.astype(jnp.float32)
    return y.astype(x.dtype)


def _rope_1d(x, pos):
    half = x.shape[-1] // 2
    freqs = ROPE_THETA ** (-jnp.arange(half, dtype=jnp.float32) / half)
    ang = pos.astype(jnp.float32)[:, None] * freqs[None, :]
    cos = jnp.cos(ang)[None, :, None, :]
    sin = jnp.sin(ang)[None, :, None, :]
    xf = x.astype(jnp.float32)
    x1, x2 = xf[..., :half], xf[..., half:]
    return jnp.concatenate([x1 * cos - x2 * sin, x1 * sin + x2 * cos], -1).astype(x.dtype)


def _rope_2d(x, rows, cols):
    half = x.shape[-1] // 2
    return jnp.concatenate([_rope_1d(x[..., :half], rows), _rope_1d(x[..., half:], cols)], -1)


def _sdpa(q, k, v, scale):
    s = jnp.einsum('bqhgd,bkhd->bhgqk', q, k).astype(jnp.float32) * scale
    p = jax.nn.softmax(s, axis=-1).astype(v.dtype)
    return jnp.einsum('bhgqk,bkhe->bqhge', p, v)


def _blocked_sdpa(q, k, v, scale):
    B, S = q.shape[:2]
    nb = S // Q_BLOCK
    qb = jnp.moveaxis(q.reshape(B, nb, Q_BLOCK, *q.shape[2:]), 1, 0)
    out = lax.map(lambda qi: _sdpa(qi, k, v, scale), qb)
    return jnp.moveaxis(out, 0, 1).reshape(B, S, *out.shape[3:])


def _mla_query(mq, rows, cols):
    B, T = mq.shape[:2]
    q = mq.reshape(B, T, MLA_HEADS, MLA_NOPE + MLA_ROPE)
    if rows is not None:
        q = jnp.concatenate([q[..., :MLA_NOPE], _rope_2d(q[..., MLA_NOPE:], rows, cols)], -1)
    return q[:, :, :, None, :]


def _mla_kv(ckv, kr, kv_norm, w_ukv, rows, cols):
    B, T = ckv.shape[:2]
    c = _rmsnorm(ckv, kv_norm)
    kv = jnp.einsum('btr,rf->btf', c, w_ukv).reshape(B, T, MLA_HEADS, MLA_NOPE + MLA_V)
    k_nope, v = kv[..., :MLA_NOPE], kv[..., MLA_NOPE:]
    k_rope = kr[:, :, None, :]
    if rows is not None:
        k_rope = _rope_2d(k_rope, rows, cols)
    k = jnp.concatenate([k_nope, jnp.broadcast_to(k_rope, (B, T, MLA_HEADS, MLA_ROPE))], -1)
    return k, v


def _gqa_qkv(gq, gkv, q_norm, k_norm, rows, cols):
    B, T = gq.shape[:2]
    q = _rmsnorm(gq.reshape(B, T, GQA_HEADS, GQA_DIM), q_norm)
    kv = gkv.reshape(B, T, 2, GQA_KV_HEADS, GQA_DIM)
    k = _rmsnorm(kv[:, :, 0], k_norm)
    v = kv[:, :, 1]
    if rows is not None:
        q = _rope_2d(q, rows, cols)
        k = _rope_2d(k, rows, cols)
    return q.reshape(B, T, GQA_KV_HEADS, GQA_GROUP, GQA_DIM), k, v


def _na_latent(q, k, v, k_ctx, v_ctx, rpb):
    B, S, H, d = q.shape
    rows_n = S // GRID_W
    kr = min(NA_KR_MAX, rows_n)
    qg = q.reshape(B, rows_n, GRID_W, H, d)
    kg = k.reshape(B, rows_n, GRID_W, H, d)
    vg = v.reshape(B, rows_n, GRID_W, H, d)
    col = jnp.arange(GRID_W)
    cs = jnp.clip(col - NA_KC // 2, 0, GRID_W - NA_KC)
    col_valid = (col[None, :] >= cs[:, None]) & (col[None, :] < cs[:, None] + NA_KC)
    mask = jnp.broadcast_to(col_valid[:, None, :], (GRID_W, kr, GRID_W)).reshape(GRID_W, kr * GRID_W)
    dc_idx = jnp.clip(col[None, :] - col[:, None] + NA_KC - 1, 0, 2 * NA_KC - 2)
    n_lat = kr * GRID_W

    def row(r):
        rs = jnp.clip(r - kr // 2, 0, rows_n - kr)
        q_r = lax.dynamic_index_in_dim(qg, r, axis=1, keepdims=False)
        k_r = lax.dynamic_slice_in_dim(kg, rs, kr, axis=1).reshape(B, n_lat, H, d)
        v_r = lax.dynamic_slice_in_dim(vg, rs, kr, axis=1).reshape(B, n_lat, H, d)
        dr_idx = rs + jnp.arange(kr) - r + NA_KR_MAX - 1
        bias = jnp.take(rpb[:, dr_idx], dc_idx, axis=2)
        bias = jnp.transpose(bias, (0, 2, 1, 3)).reshape(H, GRID_W, n_lat).astype(jnp.float32)
        s_lat = jnp.einsum('bqhd,bkhd->bhqk', q_r, k_r).astype(jnp.float32) * NA_SCALE + bias
        s_lat = jnp.where(mask, s_lat, NEG_INF)
        s_ctx = jnp.einsum('bqhd,bkhd->bhqk', q_r, k_ctx).astype(jnp.float32) * NA_SCALE
        p = jax.nn.softmax(jnp.concatenate([s_lat, s_ctx], -1), axis=-1).astype(v.dtype)
        return (jnp.einsum('bhqk,bkhd->bqhd', p[..., :n_lat], v_r)
                + jnp.einsum('bhqk,bkhd->bqhd', p[..., n_lat:], v_ctx))

    out = lax.map(row, jnp.arange(rows_n))
    return jnp.moveaxis(out, 0, 1).reshape(B, S, H, d)


def _merge(ys, gates, w_branch, w_out):
    gs = jnp.split(gates, 3, axis=-1)
    acc = sum(jax.nn.sigmoid(g) * (y @ w_branch[i]) for i, (y, g) in enumerate(zip(ys, gs)))
    return acc @ w_out


def _token_mixers(h_lat, h_ctx, lp, rows, cols, with_ctx):
    B, S, _ = h_lat.shape
    L = h_ctx.shape[1]
    p_lat = h_lat @ lp['w_in']
    p_ctx = h_ctx @ lp['w_in']
    mq_l, ckv_l, kr_l, na_l, gq_l, gkv_l, gate_l = jnp.split(p_lat, SPLIT_IDX, axis=-1)
    mq_c, ckv_c, kr_c, na_c, gq_c, gkv_c, gate_c = jnp.split(p_ctx, SPLIT_IDX, axis=-1)

    qa_l = _mla_query(mq_l, rows, cols)
    ka_l, va_l = _mla_kv(ckv_l, kr_l, lp['mla_kv_norm'], lp['w_mla_ukv'], rows, cols)
    ka_c, va_c = _mla_kv(ckv_c, kr_c, lp['mla_kv_norm'], lp['w_mla_ukv'], None, None)
    ya_l = _blocked_sdpa(qa_l, jnp.concatenate([ka_l, ka_c], 1), jnp.concatenate([va_l, va_c], 1),
                         MLA_SCALE).reshape(B, S, BRANCH_W)

    qkv_l = na_l.reshape(B, S, 3, NA_HEADS, NA_DIM)
    qkv_c = na_c.reshape(B, L, 3, NA_HEADS, NA_DIM)
    yb_l = _na_latent(qkv_l[:, :, 0], qkv_l[:, :, 1], qkv_l[:, :, 2],
                      qkv_c[:, :, 1], qkv_c[:, :, 2], lp['na_rpb']).reshape(B, S, BRANCH_W)

    qc_l, kc_l, vc_l = _gqa_qkv(gq_l, gkv_l, lp['gqa_q_norm'], lp['gqa_k_norm'], rows, cols)
    qc_c, kc_c, vc_c = _gqa_qkv(gq_c, gkv_c, lp['gqa_q_norm'], lp['gqa_k_norm'], None, None)
    yc_l = _blocked_sdpa(qc_l, jnp.concatenate([kc_l, kc_c], 1), jnp.concatenate([vc_l, vc_c], 1),
                         GQA_SCALE).reshape(B, S, BRANCH_W)

    y_lat = _merge((ya_l, yb_l, yc_l), gate_l, lp['w_branch'], lp['w_out'])
    if not with_ctx:
        return y_lat, None
    ya_c = _sdpa(_mla_query(mq_c, None, None), ka_c, va_c, MLA_SCALE).reshape(B, L, BRANCH_W)
    yb_c = _sdpa(qkv_c[:, :, 0][:, :, :, None, :], qkv_c[:, :, 1], qkv_c[:, :, 2],
                 NA_SCALE).reshape(B, L, BRANCH_W)
    yc_c = _sdpa(qc_c, kc_c, vc_c, GQA_SCALE).reshape(B, L, BRANCH_W)
    y_ctx = _merge((ya_c, yb_c, yc_c), gate_c, lp['w_branch'], lp['w_out'])
    return y_lat, y_ctx


def _conv_ffn(h, w_up, conv_w, conv_b, w_down):
    u = h @ w_up
    gate, val = u[..., :D_FF], u[..., D_FF:]
    gp = jnp.pad(gate, ((0, 0), (1, 1), (0, 0)))
    gate = gp[:, :-2] * conv_w[0] + gp[:, 1:-1] * conv_w[1] + gp[:, 2:] * conv_w[2] + conv_b
    return (jax.nn.silu(gate) * val) @ w_down


def _modulate(x, shift, scale):
    return _layernorm(x, ADA_EPS) * (1 + scale) + shift


def setup_inputs(seed: int = 0) -> dict:
    key = jax.random.key(seed)
    ks = jax.random.split(key, 24)
    f32 = jnp.float32

    def nrm(k, shape, scale):
        return jax.random.normal(k, shape, f32) * scale

    L = DEPTH
    return {
        'x': nrm(ks[0], (BATCH, SEQ, D_MODEL), 1.0),
        'c': nrm(ks[1], (BATCH, D_MODEL), 1.0),
        'ctx': nrm(ks[2], (BATCH, CTX_LEN, D_MODEL), 1.0),
        'c_ctx': nrm(ks[3], (D_MODEL,), 1.0),
        'w_ada': nrm(ks[4], (L, D_MODEL, 6 * D_MODEL), 0.5 * D_MODEL ** -0.5),
        'b_ada': nrm(ks[5], (L, 6 * D_MODEL), 0.02),
        'w_in': nrm(ks[6], (L, D_MODEL, N_IN), D_MODEL ** -0.5),
        'mla_kv_norm': 1.0 + nrm(ks[7], (L, KV_RANK), 0.1),
        'w_mla_ukv': nrm(ks[8], (L, KV_RANK, MLA_HEADS * (MLA_NOPE + MLA_V)), KV_RANK ** -0.5),
        'gqa_q_norm': 1.0 + nrm(ks[9], (L, GQA_DIM), 0.1),
        'gqa_k_norm': 1.0 + nrm(ks[10], (L, GQA_DIM), 0.1),
        'na_rpb': nrm(ks[11], (L, NA_HEADS, 2 * NA_KR_MAX - 1, 2 * NA_KC - 1), 0.05),
        'w_branch': nrm(ks[12], (L, 3, BRANCH_W, D_MODEL), BETA * BRANCH_W ** -0.5),
        'w_out': nrm(ks[13], (L, D_MODEL, D_MODEL), BETA * D_MODEL ** -0.5),
        'ln_a_g': 1.0 + nrm(ks[14], (L, D_MODEL), 0.1),
        'ln_a_b': nrm(ks[15], (L, D_MODEL), 0.02),
        'w_up': nrm(ks[16], (L, D_MODEL, 2 * D_FF), D_MODEL ** -0.5),
        'conv_w': nrm(ks[17], (L, CONV_W, D_FF), CONV_W ** -0.5),
        'conv_b': nrm(ks[18], (L, D_FF), 0.02),
        'w_down': nrm(ks[19], (L, D_FF, D_MODEL), BETA * D_FF ** -0.5),
        'ln_f_g': 1.0 + nrm(ks[20], (L, D_MODEL), 0.1),
        'ln_f_b': nrm(ks[21], (L, D_MODEL), 0.02),
    }


def reference(x, c, ctx, c_ctx, w_ada, b_ada, w_in, mla_kv_norm, w_mla_ukv, gqa_q_norm, gqa_k_norm,
              na_rpb, w_branch, w_out, ln_a_g, ln_a_b, w_up, conv_w, conv_b, w_down, ln_f_g, ln_f_b):
    S = x.shape[1]
    t = jnp.arange(S)
    rows, cols = t // GRID_W, t % GRID_W
    x_lat, x_ctx = x, ctx
    for l in range(DEPTH):
        with_ctx = l < DEPTH - 1
        lp = {'w_in': w_in[l], 'mla_kv_norm': mla_kv_norm[l], 'w_mla_ukv': w_mla_ukv[l],
              'gqa_q_norm': gqa_q_norm[l], 'gqa_k_norm': gqa_k_norm[l], 'na_rpb': na_rpb[l],
              'w_branch': w_branch[l], 'w_out': w_out[l]}
        m_lat = (jax.nn.silu(c) @ w_ada[l] + b_ada[l]).reshape(c.shape[0], 1, 6, D_MODEL)
        m_ctx = (jax.nn.silu(c_ctx) @ w_ada[l] + b_ada[l]).reshape(1, 1, 6, D_MODEL)
        h_lat = _modulate(x_lat, m_lat[..., 0, :], m_lat[..., 1, :])
        h_ctx = _modulate(x_ctx, m_ctx[..., 0, :], m_ctx[..., 1, :])
        y_lat, y_ctx = _token_mixers(h_lat, h_ctx, lp, rows, cols, with_ctx)
        x_lat = _layernorm(ALPHA * x_lat + m_lat[..., 2, :] * y_lat, POST_EPS, ln_a_g[l], ln_a_b[l])
        h_lat = _modulate(x_lat, m_lat[..., 3, :], m_lat[..., 4, :])
        f_lat = _conv_ffn(h_lat, w_up[l], conv_w[l], conv_b[l], w_down[l])
        x_lat = _layernorm(ALPHA * x_lat + m_lat[..., 5, :] * f_lat, POST_EPS, ln_f_g[l], ln_f_b[l])
        if with_ctx:
            x_ctx = _layernorm(ALPHA * x_ctx + m_ctx[..., 2, :] * y_ctx, POST_EPS, ln_a_g[l], ln_a_b[l])
            h_ctx = _modulate(x_ctx, m_ctx[..., 3, :], m_ctx[..., 4, :])
            f_ctx = _conv_ffn(h_ctx, w_up[l], conv_w[l], conv_b[l], w_down[l])
            x_ctx = _layernorm(ALPHA * x_ctx + m_ctx[..., 5, :] * f_ctx, POST_EPS, ln_f_g[l], ln_f_b[l])
    return x_lat
```

```python
import numpy as np
from contextlib import ExitStack
import concourse.bass as bass
import concourse.mybir as mybir
from concourse.bass_utils import run_bass_kernel_spmd

F32 = mybir.dt.float32
BF16 = mybir.dt.bfloat16
AF = mybir.ActivationFunctionType
ALU = mybir.AluOpType

D = 2048
S = 2048
CL = 256
T = S + CL
NT = T // 128
NLT = S // 128
L = 2
H = 16
DFF = 5632
NFC = DFF // 128
N_IN = 19008
O_MQ, O_CKV, O_KR, O_NA, O_GQ, O_GKV, O_GATE = 0, 3072, 3584, 3648, 9792, 11840, 12864
ALPHA = (2 * L) ** 0.25
MLA_SCALE = 192 ** -0.5
NA_SCALE = 128 ** -0.5
GQA_SCALE = 128 ** -0.5
ADA_EPS, POST_EPS, RMS_EPS = 1e-6, 1e-5, 1e-6
NEG = -1.0e4


class DSem:
    def __init__(self, h, key):
        self.h, self.key, self.cnt = h, key, 0


class Eng:
    def __init__(self, name, sem):
        self.name, self.sem, self.cnt, self.ops, self.waited = name, sem, 0, [], {}

    def wait(self, *evs):
        for ev in evs:
            if ev is None:
                continue
            if isinstance(ev, list):
                self.wait(*ev)
                continue
            key, sem, val = ev
            if self.waited.get(key, 0) >= val:
                continue
            self.waited[key] = val
            self.ops.append(lambda e, sem=sem, val=val: e.wait_ge(sem, val))

    def do(self, fn, sig=False):
        if sig:
            self.cnt += 1
            n, sem = self.cnt, self.sem
            self.ops.append(lambda e: fn(e).then_inc(sem, 1))
            return (self.name, sem, n)
        self.ops.append(fn)
        return None

    def dma(self, dsem, out, in_):
        dsem.cnt += 16
        n, h = dsem.cnt, dsem.h
        self.ops.append(lambda e: e.dma_start(out=out, in_=in_).then_inc(h, 16))
        return (dsem.key, h, n)


class Slot:
    def __init__(self, ap, dsem=None):
        self.ap, self.dsem = ap, dsem
        self.ready = None
        self.readers = []

    def free_evs(self):
        r = self.readers
        self.readers = []
        return r


class Ring:
    def __init__(self, slots):
        self.slots, self.i = slots, 0

    def next(self):
        s = self.slots[self.i % len(self.slots)]
        self.i += 1
        return s


class K:
    def __init__(self, dbg=(), stop_after=None):
        self.dbg, self.stop_after = set(dbg), stop_after
        self.branches = (0, 1, 2)
        self.nc = nc = bass.Bass("TRN2", target_bir_lowering=False)
        self.es = ExitStack()
        self.ds_pool, self.ds_i = [], 0
        self.uid = 0
        self.engs = {}
        for n in ("pe", "act", "dve", "pool", "sp"):
            self.engs[n] = Eng(n, self.es.enter_context(nc.semaphore("sem_" + n)))
        self.pe, self.act, self.dve, self.pool, self.sp = (self.engs[n] for n in ("pe", "act", "dve", "pool", "sp"))
        self.dram = {}
        self.pending = []

    def inp(self, name, shape, dt=F32):
        t = self.nc.dram_tensor(name, list(shape), dt, kind="ExternalInput")
        self.dram[name] = t
        return t

    def scr(self, name, shape, dt=BF16):
        if name in self.dbg:
            t = self.nc.dram_tensor(name, list(shape), dt, kind="ExternalOutput")
        else:
            t = self.nc.dram_tensor(name, list(shape), dt)
        self.dram[name] = t
        return t

    def dsem(self):
        if self.ds_i >= len(self.ds_pool):
            n = len(self.ds_pool)
            self.ds_pool.append(DSem(self.es.enter_context(self.nc.semaphore("ds%d" % n)), "ds%d" % n))
        d = self.ds_pool[self.ds_i]
        self.ds_i += 1
        return d

    def sb(self, es, name, shape, dt):
        self.uid += 1
        return es.enter_context(self.nc.sbuf_tensor("%s_%d" % (name, self.uid), list(shape), dt))

    def ps(self, es, name, shape, dt):
        self.uid += 1
        return es.enter_context(self.nc.psum_tensor("%s_%d" % (name, self.uid), list(shape), dt))

    def flush(self, waiter=None):
        w = waiter or self.sp
        w.wait(self.pending)
        self.pending = []
        with self.nc.Block() as block:
            ops = {n: list(e.ops) for n, e in self.engs.items()}

            @block.tensor
            def _(t):
                for f in ops["pe"]:
                    f(t)

            @block.scalar
            def _(a):
                for f in ops["act"]:
                    f(a)

            @block.vector
            def _(v):
                for f in ops["dve"]:
                    f(v)

            @block.gpsimd
            def _(g):
                for f in ops["pool"]:
                    f(g)

            @block.sync
            def _(s):
                for f in ops["sp"]:
                    f(s)
        for e in self.engs.values():
            e.ops = []
        self.ds_i = 0

    def store(self, eng, slot, out, in_, after):
        eng.wait(after)
        ev = eng.dma(slot.dsem, out, in_)
        slot.readers.append(ev)
        self.pending.append(ev)
        return ev


def bc_ap(t, off, n, parts=128):
    return bass.AP(t, off, [[0, parts], [1, n]])


def phase_ada(k, l):
    pe, act, dve, pool, sp = k.pe, k.act, k.dve, k.pool, k.sp
    with ExitStack() as es:
        csb = k.sb(es, "ada_c", [128, 32], F32)
        scT = k.sb(es, "ada_sc", [128, 16, 2], BF16)
        bsb = k.sb(es, "ada_b", [2, 12288], F32)
        msb = k.sb(es, "ada_m", [2, 12288], F32)
        ws = Ring([Slot(k.sb(es, "ada_w%d" % i, [128, 16, 512], BF16), k.dsem()) for i in range(2)])
        pb = Ring([Slot(k.ps(es, "ada_p%d" % i, [128, 512], F32)) for i in range(2)])
        d0 = k.dsem()
        sp.dma(d0, csb[:], k.dram["cT"].ap().rearrange("p k j -> p (k j)"))
        e2 = sp.dma(d0, bsb[:], bc_ap(k.dram["b_ada"], l * 12288, 12288, parts=2))
        act.wait(e2)
        ev_sc = act.do(lambda a: a.activation(out=scT[:].rearrange("p k j -> p (k j)"), in_=csb[:], func=AF.Silu), sig=True)
        wada = k.dram["w_ada"].ap()

        def load(n):
            s = ws.next()
            pool.wait(s.free_evs())
            s.ready = pool.dma(s.dsem, s.ap[:], wada[l, :, n * 512:(n + 1) * 512].rearrange("(k p) n -> p k n", p=128))
            return s

        def mm(b, cur, kk):
            return lambda t: t.matmul(b.ap[0:2, :], lhsT=scT[:, kk, :], rhs=cur.ap[:, kk, :], start=(kk == 0), stop=(kk == 15))

        def addb(b, n):
            return lambda v: v.tensor_tensor(out=msb[:, n * 512:(n + 1) * 512], in0=b.ap[0:2, :], in1=bsb[:, n * 512:(n + 1) * 512], op=ALU.add)

        nxt = load(0)
        ev2 = None
        for n in range(24):
            cur = nxt
            if n + 1 < 24:
                nxt = load(n + 1)
            b = pb.next()
            pe.wait(cur.ready, ev_sc, b.free_evs())
            for kk in range(16):
                ev = pe.do(mm(b, cur, kk), sig=(kk == 15))
            cur.readers.append(ev)
            dve.wait(ev, e2)
            ev2 = dve.do(addb(b, n), sig=True)
            b.readers.append(ev2)
        st = Slot(None, k.dsem())
        k.store(sp, st, k.dram["mvec"].ap()[l], msb[:], ev2)
        k.flush()


def ln_stats(k, xin_ap, mv, rstd, nmr, eps, stats):
    dve, act = k.dve, k.act
    for c in range(4):
        ev = dve.do(lambda v, c=c: v.bn_stats(out=stats[:, c, :], in_=xin_ap[:, c * 512:(c + 1) * 512]), sig=(c == 3))
    dve.wait(ev)
    ev = dve.do(lambda v: v.bn_aggr(out=mv, in_=stats), sig=True)
    act.wait(ev)
    ev = act.do(lambda a: a.activation(out=rstd, in_=mv[:, 1:2], func=AF.Sqrt, bias=eps, scale=1.0), sig=True)
    dve.wait(ev)
    ev = dve.do(lambda v: v.reciprocal(out=rstd, in_=rstd), sig=True)
    dve.wait(ev)
    return dve.do(lambda v: v.tensor_scalar(out=nmr, in0=mv[:, 0:1], scalar1=-1.0, scalar2=rstd, op0=ALU.mult, op1=ALU.mult), sig=True)


def phase_mod(k, l, sub, src, hT, ident, ntiles):
    pe, act, dve, pool, sp = k.pe, k.act, k.dve, k.pool, k.sp
    mv_t = k.dram["mvec"]
    with ExitStack() as es:
        bcs = {}
        d0 = k.dsem()
        evb = None
        for j, nm in ((0, "lat"), (1, "ctx")):
            sh = k.sb(es, "mod_sh" + nm, [128, 2048], F32)
            s1 = k.sb(es, "mod_s1" + nm, [128, 2048], F32)
            base = (l * 2 + j) * 12288 + sub * 3 * 2048
            sp.dma(d0, sh[:], bc_ap(mv_t, base, 2048))
            evb = sp.dma(d0, s1[:], bc_ap(mv_t, base + 2048, 2048))
            bcs[j] = (sh, s1)
        dve.wait(evb)
        for j in (0, 1):
            s1 = bcs[j][1]
            evb2 = dve.do(lambda v, s1=s1: v.tensor_scalar(out=s1[:], in0=s1[:], scalar1=1.0, scalar2=None, op0=ALU.add), sig=True)
        xr = Ring([Slot(k.sb(es, "mod_x%d" % i, [128, 2048], F32), k.dsem()) for i in range(2)])
        xn = Ring([Slot(k.sb(es, "mod_xn%d" % i, [128, 2048], F32)) for i in range(2)])
        hb = Ring([Slot(k.sb(es, "mod_hb%d" % i, [128, 2048], BF16)) for i in range(2)])
        sm = Ring([Slot(k.sb(es, "mod_sm%d" % i, [128, 32], F32)) for i in range(2)])
        pt = Ring([Slot(k.ps(es, "mod_pt%d" % i, [128, 512], BF16)) for i in range(4)])

        def load(i):
            s = xr.next()
            sp.wait(s.free_evs())
            s.ready = sp.dma(s.dsem, s.ap[:], src(i))
            return s

        nxt = load(0)
        for i in range(ntiles):
            cur = nxt
            if i + 1 < ntiles:
                nxt = load(i + 1)
            j = 0 if i < NLT else 1
            sh, s1 = bcs[j]
            smt = sm.next()
            st = smt.ap
            stats = st[:, 0:24].rearrange("p (c s) -> p c s", s=6)
            mv, rstd, nmr = st[:, 24:26], st[:, 26:27], st[:, 27:28]
            dve.wait(cur.ready, smt.free_evs())
            ev = ln_stats(k, cur.ap, mv, rstd, nmr, ADA_EPS, stats)
            xs = xn.next()
            act.wait(ev, xs.free_evs())
            ev = act.do(lambda a, xs=xs, cur=cur, rstd=rstd, nmr=nmr: a.activation(out=xs.ap[:], in_=cur.ap[:], func=AF.Identity, bias=nmr, scale=rstd), sig=True)
            cur.readers.append(ev)
            smt.readers.append(ev)
            dve.wait(ev, evb2)
            ev = dve.do(lambda v, xs=xs, s1=s1: v.tensor_tensor(out=xs.ap[:], in0=xs.ap[:], in1=s1[:], op=ALU.mult), sig=True)
            hs = hb.next()
            pool.wait(ev, hs.free_evs())
            ev = pool.do(lambda g, xs=xs, hs=hs, sh=sh: g.tensor_tensor(out=hs.ap[:], in0=xs.ap[:], in1=sh[:], op=ALU.add), sig=True)
            xs.readers.append(ev)
            for jj in range(4):
                p = pt.next()
                pe.wait(ev, p.free_evs())
                for a in range(4):
                    kk = 4 * jj + a
                    evp = pe.do(lambda t, p=p, hs=hs, kk=kk, a=a: t.transpose(out=p.ap[:, a * 128:(a + 1) * 128], in_=hs.ap[:, kk * 128:(kk + 1) * 128], identity=ident[:]), sig=(a == 3))
                eng = act if jj % 2 == 0 else dve
                eng.wait(evp)
                if eng is act:
                    evc = eng.do(lambda a_, p=p, jj=jj, i=i: a_.copy(out=hT[:, 4 * jj:4 * jj + 4, i * 128:(i + 1) * 128], in_=p.ap[:].rearrange("p (a b) -> p a b", a=4)), sig=True)
                else:
                    evc = eng.do(lambda v, p=p, jj=jj, i=i: v.tensor_copy(out=hT[:, 4 * jj:4 * jj + 4, i * 128:(i + 1) * 128], in_=p.ap[:].rearrange("p (a b) -> p a b", a=4)), sig=True)
                p.readers.append(evc)
            hs.readers.append(evp)
        k.flush()


def declare(k):
    k.inp("x", [S, D])
    k.inp("ctx", [CL, D])
    k.inp("cT", [128, 16, 2])
    k.inp("w_ada", [L, D, 6 * D])
    k.inp("b_ada", [L, 6 * D])
    k.inp("w_in", [L, D, N_IN])
    k.inp("mla_kv_norm", [L, 512])
    k.inp("w_mla_ukv", [L, 512, 4096])
    k.inp("gqa_q_norm", [L, 128])
    k.inp("gqa_k_norm", [L, 128])
    k.inp("na_bias", [L, H, 128, 1024])
    k.inp("w_branch", [L, 3, D, D])
    k.inp("w_out", [L, D, D])
    k.inp("ln_a_g", [L, D])
    k.inp("ln_a_b", [L, D])
    k.inp("w_up", [L, D, 2 * DFF])
    k.inp("convp", [L, 128, NFC, 4])
    k.inp("w_down", [L, DFF, D])
    k.inp("ln_f_g", [L, D])
    k.inp("ln_f_b", [L, D])
    k.inp("ropeM", [2, 128, NLT, 64])
    k.inp("ropeG", [2, 128, NLT, 128])
    k.out = k.nc.dram_tensor("out", [S, D], F32, kind="ExternalOutput")
    k.scr("mvec", [L, 2, 6 * D], F32)
    k.scr("xres", [T, D], F32)
    k.scr("fout", [T, D], F32)
    k.scr("hT_dbg", [D, T], BF16)
    k.scr("qa_nT", [H, 128, T])
    k.scr("qa_rT", [H // 2, 128, T])
    k.scr("ka_nT", [H, 128, T])
    k.scr("ka_rT", [128, T])
    k.scr("cgT", [512, T])
    k.scr("va", [T, D])
    k.scr("qbT", [H, 128, T])
    k.scr("kbT", [H, 128, T])
    k.scr("vb", [T, D])
    k.scr("qcT", [H, 128, T])
    k.scr("kcT", [4, 128, T])
    k.scr("vc", [T, 512])
    k.scr("gateT", [3, D, T])
    k.scr("yT", [3, H, 128, T])
    k.scr("accT", [D, T])
    k.scr("hidT", [DFF, T])
    k.scr("rstd_dbg", [128, NT], F32)


def make_ident(k, es):
    ident = k.sb(es, "ident", [128, 128], BF16)
    k.pool.do(lambda g: g.memset(ident[:], 0.0))
    k.pool.do(lambda g: g.affine_select(out=ident[:], in_=ident[:], pattern=[[-1, 128]], compare_op=ALU.not_equal, fill=1.0, base=0, channel_multiplier=1))
    return ident


def build(dbg=(), stop_after=None):
    k = K(dbg, stop_after)
    declare(k)
    sp = k.sp
    with ExitStack() as es:
        ident = make_ident(k, es)
        k.flush()
        for l in range(L):
            phase_ada(k, l)
        if stop_after == "ada":
            return k
        xin, cin, xres = k.dram["x"].ap(), k.dram["ctx"].ap(), k.dram["xres"].ap()
        for l in range(L):
            ntq = NT if l < L - 1 else NLT
            if l == 0:
                src = lambda i: (xin[i * 128:(i + 1) * 128, :] if i < NLT else cin[(i - NLT) * 128:(i - NLT + 1) * 128, :])
            else:
                src = lambda i: xres[i * 128:(i + 1) * 128, :]
            with ExitStack() as es2:
                hT = k.sb(es2, "hT", [128, 16, T], BF16)
                phase_mod(k, l, 0, src, hT, ident, NT)
                if "hT_dbg" in k.dbg and l == 0:
                    st = Slot(None, k.dsem())
                    k.store(sp, st, k.dram["hT_dbg"].ap().rearrange("(k p) t -> p k t", p=128), hT[:], None)
                    k.flush()
                if stop_after == "mod":
                    return k
                rms_all = k.sb(es2, "rms_all", [128, NT], F32)
                rstd_all = k.sb(es2, "rstd_all", [128, NT], F32)
                phase_win(k, l, hT, ident, ntq, rms_all, rstd_all)
                if stop_after == "win":
                    return k
                phase_ukv(k, l, rstd_all)
                if stop_after == "ukv":
                    return k
                for br in k.branches:
                    phase_attn(k, l, br, ident, ntq, rstd_all)
                if stop_after == "attn":
                    return k
            phase_merge(k, l, ntq)
            if stop_after == "merge":
                return k
            phase_tm_out(k, l, "out", ntq)
            phase_resid(k, l, 0, src, ntq, False)
            if stop_after == "mixer":
                return k
            srcx = lambda i: xres[i * 128:(i + 1) * 128, :]
            with ExitStack() as es2:
                hT = k.sb(es2, "hT", [128, 16, T], BF16)
                phase_mod(k, l, 1, srcx, hT, ident, ntq)
                phase_ffn_up(k, l, hT, ntq)
            if stop_after == "ffnup":
                return k
            phase_tm_out(k, l, "down", ntq)
            phase_resid(k, l, 1, srcx, ntq, l == L - 1)
            if stop_after == "layer%d" % l:
                return k
    return k


def host_inputs(inputs):
    f32 = np.float32
    common = {}
    for n in ("w_ada", "b_ada", "w_in", "mla_kv_norm", "w_mla_ukv", "gqa_q_norm", "gqa_k_norm", "w_branch",
              "w_out", "ln_a_g", "ln_a_b", "w_up", "w_down", "ln_f_g", "ln_f_b"):
        common[n] = np.ascontiguousarray(inputs[n], dtype=f32)
    cw = np.asarray(inputs["conv_w"], f32)
    cb = np.asarray(inputs["conv_b"], f32)
    cp = np.concatenate([cw, cb[:, None, :]], axis=1)
    common["convp"] = np.ascontiguousarray(cp.reshape(L, 4, NFC, 128).transpose(0, 3, 2, 1))
    rpb = np.asarray(inputs["na_rpb"], f32)
    a_ = np.arange(2)[:, None, None, None]
    kc = np.arange(64)[None, :, None, None]
    t_ = np.arange(16)[None, None, :, None]
    qc = np.arange(64)[None, None, None, :]
    dr = a_ - t_ + 7
    cs = np.clip(qc - 8, 0, 48)
    valid = (np.abs(dr) <= 7) & (kc >= cs) & (kc < cs + 16)
    valid = np.broadcast_to(valid, (2, 64, 16, 64))
    dri = np.broadcast_to(np.clip(dr + 7, 0, 14), (2, 64, 16, 64))
    dci = np.broadcast_to(np.clip(kc - qc + 15, 0, 30), (2, 64, 16, 64))
    nb = rpb[:, :, dri, dci]
    nb = np.where(valid[None, None], nb, f32(NEG)).astype(f32)
    common["na_bias"] = np.ascontiguousarray(nb.reshape(L, H, 128, 1024))
    tok = np.arange(S)
    rows, cols = (tok // 64).astype(f32), (tok % 64).astype(f32)

    def tables(quarter):
        fr = (f32(10000.0) ** (-np.arange(quarter, dtype=f32) / f32(quarter))).astype(f32)
        ar, ac = rows[:, None] * fr[None, :], cols[:, None] * fr[None, :]
        C = np.concatenate([np.cos(ar), np.cos(ar), np.cos(ac), np.cos(ac)], 1)
        Sg = np.concatenate([-np.sin(ar), np.sin(ar), -np.sin(ac), np.sin(ac)], 1)
        tb = np.stack([C, Sg], 0).astype(f32)
        return np.ascontiguousarray(tb.reshape(2, NLT, 128, 4 * quarter).transpose(0, 2, 1, 3))
    common["ropeM"] = tables(16)
    common["ropeG"] = tables(32)
    x = np.asarray(inputs["x"], f32)
    ctx = np.asarray(inputs["ctx"], f32)
    c = np.asarray(inputs["c"], f32)
    cc = np.asarray(inputs["c_ctx"], f32)
    maps = []
    for b in range(8):
        m = dict(common)
        m["x"] = np.ascontiguousarray(x[b])
        m["ctx"] = np.ascontiguousarray(ctx[b])
        c2 = np.stack([c[b], cc], 0)
        m["cT"] = np.ascontiguousarray(c2.reshape(2, 16, 128).transpose(2, 1, 0))
        maps.append(m)
    return maps


def kernel(**inputs):
    k = build()
    maps = host_inputs(inputs)
    res = run_bass_kernel_spmd(k.nc, maps, core_ids=list(range(8)))
    return np.stack([np.asarray(r["out"], np.float32) for r in res.results], 0)


def tok_groups(ntok):
    g, t0 = [], 0
    while t0 < ntok:
        n = min(512, ntok - t0)
        g.append((t0, n))
        t0 += n
    return g


class Proj:
    def __init__(self, k, es, A, KC, wcols, nslots, nbanks, pfx):
        self.k, self.A, self.KC = k, A, KC
        self.ws = Ring([Slot(k.sb(es, pfx + "_w%d" % i, [128, KC, wcols], BF16), k.dsem()) for i in range(nslots)])
        self.pb = Ring([Slot(k.ps(es, pfx + "_p%d" % i, [128, 512], F32)) for i in range(nbanks)])
        self.items = []

    def add(self, mode, src_fn, run_fn):
        self.items.append((mode, src_fn, run_fn))

    def load(self, idx):
        _, src_fn, _ = self.items[idx]
        s = self.ws.next()
        self.k.pool.wait(s.free_evs())
        pairs = src_fn(s)
        if isinstance(pairs, tuple):
            pairs = [pairs]
        for dst, src in pairs:
            s.ready = self.k.pool.dma(s.dsem, dst, src)
        return s

    def run(self):
        n = len(self.items)
        depth = len(self.ws.slots) - 1
        loaded = []
        for j in range(min(depth, n)):
            loaded.append(self.load(j))
        for idx in range(n):
            if idx + depth < n:
                loaded.append(self.load(idx + depth))
            self.items[idx][2](loaded[idx])
        self.items = []

    def tm_tile(self, slab, width, i, extra_wait=None):
        pe = self.k.pe
        b = self.pb.next()
        pe.wait(slab.ready, b.free_evs(), extra_wait)
        A, KC = self.A, self.KC
        for kk in range(KC):
            ev = pe.do(lambda t, b=b, kk=kk: t.matmul(b.ap[:, 0:width], lhsT=A[:, kk, i * 128:(i + 1) * 128], rhs=slab.ap[:, kk, 0:width],
                                                     start=(kk == 0), stop=(kk == KC - 1)), sig=(kk == KC - 1))
        slab.readers.append(ev)
        return b, ev

    def fm_group(self, slab, c, g0, n, kcs=None, extra_wait=None):
        pe = self.k.pe
        b = self.pb.next()
        pe.wait(slab.ready, b.free_evs(), extra_wait)
        A = self.A
        kcs = list(range(self.KC)) if kcs is None else kcs
        for j, kk in enumerate(kcs):
            ev = pe.do(lambda t, b=b, kk=kk, j=j: t.matmul(b.ap[:, 0:n], lhsT=slab.ap[:, j, c * 128:(c + 1) * 128], rhs=A[:, kk, g0:g0 + n],
                                                           start=(j == 0), stop=(j == len(kcs) - 1)), sig=(j == len(kcs) - 1))
        slab.readers.append(ev)
        return b, ev


def evac_copy(k, eng, out, in_, func=None, scale=None):
    if eng is k.act:
        f = func if func is not None else (AF.Copy if scale is None else AF.Identity)
        if scale is None:
            return eng.do(lambda a: a.activation(out=out, in_=in_, func=f), sig=True)
        return eng.do(lambda a: a.activation(out=out, in_=in_, func=f, scale=scale), sig=True)
    assert func is None
    if scale is None:
        return eng.do(lambda v: v.tensor_copy(out=out, in_=in_), sig=True)
    return eng.do(lambda v: v.tensor_scalar(out=out, in0=in_, scalar1=scale, scalar2=None, op0=ALU.mult), sig=True)


class FMOut:
    def __init__(self, k, es, nslots, pfx):
        self.k = k
        self.ring = Ring([Slot(k.sb(es, pfx + "_fs%d" % i, [128, T], BF16), k.dsem()) for i in range(nslots)])
        self.cur, self.evs, self.tog = None, [], 0

    def evac(self, b, ev, g0, n, first, last, dst, ntok, func=None):
        k = self.k
        if first:
            self.cur = self.ring.next()
            self.evs = []
            self.fe = self.cur.free_evs()
        eng = k.act if (func is not None or self.tog % 2 == 0) else k.dve
        self.tog += 1
        eng.wait(ev, self.fe)
        e2 = evac_copy(k, eng, self.cur.ap[:, g0:g0 + n], b.ap[:, 0:n], func=func)
        b.readers.append(e2)
        self.evs.append(e2)
        if last:
            k.store(k.sp, self.cur, dst[:, 0:ntok], self.cur.ap[:, 0:ntok], self.evs)


class TOut:
    def __init__(self, k, es, ident, nslots, pfx):
        self.k, self.ident = k, ident
        self.ring = Ring([Slot(k.sb(es, pfx + "_ts%d" % i, [128, 4, T], BF16), k.dsem()) for i in range(nslots)])
        self.pt = Ring([Slot(k.ps(es, pfx + "_tp%d" % i, [128, 512], BF16)) for i in range(2)])
        self.tog = 0

    def begin(self):
        self.cur = self.ring.next()
        self.fe = self.cur.free_evs()
        self.evs = []

    def tile(self, o_slot, o_ev, nblk, i):
        k, pe, ident = self.k, self.k.pe, self.ident
        p = self.pt.next()
        pe.wait(o_ev, p.free_evs())
        for j in range(nblk):
            evp = pe.do(lambda t, j=j, p=p: t.transpose(out=p.ap[:, j * 128:(j + 1) * 128], in_=o_slot.ap[:, j * 128:(j + 1) * 128], identity=ident[:]), sig=(j == nblk - 1))
        o_slot.readers.append(evp)
        eng = k.act if self.tog % 2 == 0 else k.dve
        self.tog += 1
        eng.wait(evp, self.fe)
        e2 = evac_copy(k, eng, self.cur.ap[:, 0:nblk, i * 128:(i + 1) * 128], p.ap[:, 0:nblk * 128].rearrange("p (a b) -> p a b", b=128))
        p.readers.append(e2)
        self.evs.append(e2)

    def finish(self, dsts, ntok):
        k = self.k
        for j, d in enumerate(dsts):
            k.store(k.sp, self.cur, d[:, 0:ntok], self.cur.ap[:, j, 0:ntok], self.evs)


def tab_bc(tab, i, nh, Dh, sub=None):
    pstep = NLT * Dh
    if sub is None:
        return bass.AP(tab, i * Dh, [[pstep, 128], [0, nh], [1, Dh]])
    b, q = sub
    return bass.AP(tab, i * Dh + b * q, [[pstep, 128], [0, nh], [2 * q, 2], [1, q]])


def rope_emit(k, x_ap, x_ev, nh, Dh, tabC, tabS, i, t1, t2, o_ap, o_free, sc=None):
    dve, pool = k.dve, k.pool
    q = Dh // 4
    W = nh * Dh
    xv = x_ap.rearrange("p (h a b q) -> p h a b q", h=nh, a=2, b=2)
    t1v = t1.ap[:, 0:W].rearrange("p (h d) -> p h d", h=nh)
    t2v = t2.ap[:, 0:W].rearrange("p (h a b q) -> p h a b q", h=nh, a=2, b=2)
    dve.wait(x_ev, t1.free_evs(), t2.free_evs())
    if sc is None:
        dve.do(lambda v: v.tensor_tensor(out=t1v, in0=x_ap.rearrange("p (h d) -> p h d", h=nh), in1=tab_bc(tabC, i, nh, Dh), op=ALU.mult))
        dve.do(lambda v: v.tensor_tensor(out=t2v[:, :, :, 0, :], in0=xv[:, :, :, 1, :], in1=tab_bc(tabS, i, nh, Dh, (0, q)), op=ALU.mult))
        ev = dve.do(lambda v: v.tensor_tensor(out=t2v[:, :, :, 1, :], in0=xv[:, :, :, 0, :], in1=tab_bc(tabS, i, nh, Dh, (1, q)), op=ALU.mult), sig=True)
    else:
        assert nh == 1
        Cv = bass.AP(tabC, i * Dh, [[NLT * Dh, 128], [1, Dh]])
        dve.do(lambda v: v.scalar_tensor_tensor(out=t1.ap[:, 0:W], in0=x_ap, scalar=sc, in1=Cv, op0=ALU.mult, op1=ALU.mult))
        for b in (0, 1):
            Sv = bass.AP(tabS, i * Dh + b * q, [[NLT * Dh, 128], [2 * q, 2], [1, q]])
            ev = dve.do(lambda v, b=b, Sv=Sv: v.scalar_tensor_tensor(out=t2v[:, 0, :, b, :], in0=xv[:, 0, :, 1 - b, :], scalar=sc, in1=Sv, op0=ALU.mult, op1=ALU.mult), sig=(b == 1))
    pool.wait(ev, o_free)
    ev2 = pool.do(lambda g: g.tensor_tensor(out=o_ap, in0=t1.ap[:, 0:W], in1=t2.ap[:, 0:W], op=ALU.add), sig=True)
    t1.readers.append(ev2)
    t2.readers.append(ev2)
    return ev2, [ev]


def phase_win(k, l, hT, ident, ntq, rms_all, rstd_all):
    pe, act, dve, pool, sp = k.pe, k.act, k.dve, k.pool, k.sp
    W = k.dram["w_in"].ap()[l]
    ntok_q = ntq * 128
    with ExitStack() as es:
        pj = Proj(k, es, hT, 16, 512, 2, 4, "win")
        fmo = FMOut(k, es, 2, "win")
        tout = TOut(k, es, ident, 2, "win")
        vst = Ring([Slot(k.sb(es, "win_vst%d" % i, [128, 512], BF16), k.dsem()) for i in range(2)])
        t1r = Ring([Slot(k.sb(es, "win_t1%d" % i, [128, 512], F32)) for i in range(2)])
        t2r = Ring([Slot(k.sb(es, "win_t2%d" % i, [128, 512], F32)) for i in range(2)])
        xgr = Ring([Slot(k.sb(es, "win_xg%d" % i, [128, 512], F32)) for i in range(2)])
        orr = Ring([Slot(k.sb(es, "win_o%d" % i, [128, 512], BF16)) for i in range(2)])
        smr = Ring([Slot(k.sb(es, "win_sm%d" % i, [128, 16], F32)) for i in range(2)])
        junk = k.sb(es, "win_junk", [128, 512], BF16)
        rM = [k.sb(es, "win_rM%d" % j, [128, NLT, 64], F32) for j in range(2)]
        rG = [k.sb(es, "win_rG%d" % j, [128, NLT, 128], F32) for j in range(2)]
        g512 = k.sb(es, "win_g512", [128, 512], F32)
        gqk = [k.sb(es, "win_gqk%d" % j, [128, 128], F32) for j in range(2)]
        d0 = k.dsem()
        for j in range(2):
            sp.dma(d0, rM[j][:], k.dram["ropeM"].ap()[j])
            sp.dma(d0, rG[j][:], k.dram["ropeG"].ap()[j])
        sp.dma(d0, g512[:], bc_ap(k.dram["mla_kv_norm"], l * 512, 512))
        sp.dma(d0, gqk[0][:], bc_ap(k.dram["gqa_q_norm"], l * 128, 128))
        ev_tab = sp.dma(d0, gqk[1][:], bc_ap(k.dram["gqa_k_norm"], l * 128, 128))
        for e in (act, dve, pool):
            e.wait(ev_tab)

        def src2d(c0, w):
            return lambda s: (s.ap[:, :, 0:w], W[:, c0:c0 + w].rearrange("(k p) n -> p k n", p=128))

        def src_mq(h0, nh, d0_, dw):
            v = W[:, 0:3072].rearrange("(k p) (h d) -> p k h d", p=128, d=192)
            return lambda s: [(s.ap[:, :, hh * dw:(hh + 1) * dw], v[:, :, h0 + hh, d0_:d0_ + dw]) for hh in range(nh)]

        def fm_run(nchunks, dsts, ntok, func=None):
            groups = tok_groups(ntok)

            def run(slab):
                for c in range(nchunks):
                    for gi, (g0, n) in enumerate(groups):
                        b, ev = pj.fm_group(slab, c, g0, n)
                        fmo.evac(b, ev, g0, n, gi == 0, gi == len(groups) - 1, dsts[c], ntok, func=func)
            return run

        def v_run(dst, width, ntiles):
            def run(slab):
                for i in range(ntiles):
                    b, ev = pj.tm_tile(slab, width, i)
                    st = vst.next()
                    act.wait(ev, st.free_evs())
                    e2 = evac_copy(k, act, st.ap[:, 0:width], b.ap[:, 0:width])
                    b.readers.append(e2)
                    k.store(sp, st, dst[i * 128:(i + 1) * 128, :], st.ap[:, 0:width], e2)
            return run

        def mqrope_run(h0):
            def run(slab):
                tout.begin()
                for i in range(ntq):
                    b, ev = pj.tm_tile(slab, 512, i)
                    o = orr.next()
                    if i < NLT:
                        e2, xr = rope_emit(k, b.ap[:, 0:512], ev, 8, 64, rM[0], rM[1], i, t1r.next(), t2r.next(), o.ap[:, 0:512], o.free_evs())
                        b.readers.extend(xr)
                    else:
                        act.wait(ev, o.free_evs())
                        e2 = evac_copy(k, act, o.ap[:, 0:512], b.ap[:, 0:512])
                        b.readers.append(e2)
                    tout.tile(o, e2, 4, i)
                tout.finish([k.dram["qa_rT"].ap()[h0 // 2 + j] for j in range(4)], ntok_q)
            return run

        def ckv_run(slab):
            tout.begin()
            for i in range(NT):
                b, ev = pj.tm_tile(slab, 512, i)
                sm = smr.next()
                act.wait(ev, sm.free_evs())
                e1 = act.do(lambda a, b=b, sm=sm: a.activation(out=junk[:], in_=b.ap[:, 0:512], func=AF.Square, accum_out=sm.ap[:, 0:1]), sig=True)
                act.wait(e1)
                e1 = act.do(lambda a, sm=sm, i=i: a.activation(out=rms_all[:, i:i + 1], in_=sm.ap[:, 0:1], func=AF.Sqrt, bias=RMS_EPS, scale=1.0 / 512), sig=True)
                sm.readers.append(e1)
                dve.wait(e1)
                e3 = dve.do(lambda v, i=i: v.reciprocal(out=rstd_all[:, i:i + 1], in_=rms_all[:, i:i + 1]), sig=True)
                o = orr.next()
                dve.wait(o.free_evs())
                e2 = dve.do(lambda v, b=b, o=o: v.tensor_tensor(out=o.ap[:, 0:512], in0=b.ap[:, 0:512], in1=g512[:], op=ALU.mult), sig=True)
                b.readers.extend([e1, e2])
                tout.tile(o, e2, 4, i)
            k.ev_rstd = e3
            tout.finish([k.dram["cgT"].ap()[j * 128:(j + 1) * 128, :] for j in range(4)], T)

        def kr_run(slab):
            tout.begin()
            for i in range(NT):
                b, ev = pj.tm_tile(slab, 64, i)
                o = orr.next()
                sc = rms_all[:, i:i + 1]
                dve.wait(k.ev_rstd)
                if i < NLT:
                    e2, xr = rope_emit(k, b.ap[:, 0:64], ev, 1, 64, rM[0], rM[1], i, t1r.next(), t2r.next(), o.ap[:, 0:64], o.free_evs(), sc=sc)
                    b.readers.extend(xr)
                    pool.wait(e2)
                    e2 = pool.do(lambda g, o=o: g.tensor_copy(out=o.ap[:, 64:128], in_=o.ap[:, 0:64]), sig=True)
                else:
                    dve.wait(ev, o.free_evs())
                    dve.do(lambda v, b=b, o=o, sc=sc: v.tensor_scalar(out=o.ap[:, 0:64], in0=b.ap[:, 0:64], scalar1=sc, scalar2=None, op0=ALU.mult))
                    e2 = dve.do(lambda v, b=b, o=o, sc=sc: v.tensor_scalar(out=o.ap[:, 64:128], in0=b.ap[:, 0:64], scalar1=sc, scalar2=None, op0=ALU.mult), sig=True)
                    b.readers.append(e2)
                tout.tile(o, e2, 1, i)
            tout.finish([k.dram["ka_rT"].ap()], T)

        def gqa_run(gvec, dsts, ntiles):
            def run(slab):
                tout.begin()
                for i in range(ntiles):
                    b, ev = pj.tm_tile(slab, 512, i)
                    sm = smr.next()
                    act.wait(ev, sm.free_evs())
                    for hh in range(4):
                        e1 = act.do(lambda a, b=b, sm=sm, hh=hh: a.activation(out=junk[:, 0:128], in_=b.ap[:, hh * 128:(hh + 1) * 128], func=AF.Square, accum_out=sm.ap[:, hh:hh + 1]), sig=(hh == 3))
                    act.wait(e1)
                    e1 = act.do(lambda a, sm=sm: a.activation(out=sm.ap[:, 4:8], in_=sm.ap[:, 0:4], func=AF.Sqrt, bias=RMS_EPS, scale=1.0 / 128), sig=True)
                    dve.wait(e1)
                    e3 = dve.do(lambda v, sm=sm: v.reciprocal(out=sm.ap[:, 8:12], in_=sm.ap[:, 4:8]), sig=True)
                    xg = xgr.next()
                    dve.wait(e3, xg.free_evs())
                    for hh in range(4):
                        e4 = dve.do(lambda v, b=b, sm=sm, hh=hh, xg=xg: v.scalar_tensor_tensor(out=xg.ap[:, hh * 128:(hh + 1) * 128], in0=b.ap[:, hh * 128:(hh + 1) * 128],
                                                                                              scalar=sm.ap[:, 8 + hh:9 + hh], in1=gvec[:], op0=ALU.mult, op1=ALU.mult), sig=(hh == 3))
                    b.readers.extend([e1, e4])
                    sm.readers.append(e4)
                    o = orr.next()
                    if i < NLT:
                        e2, xr = rope_emit(k, xg.ap[:, 0:512], e4, 4, 128, rG[0], rG[1], i, t1r.next(), t2r.next(), o.ap[:, 0:512], o.free_evs())
                        xg.readers.extend(xr)
                    else:
                        pool.wait(e4, o.free_evs())
                        e2 = pool.do(lambda g, xg=xg, o=o: g.tensor_copy(out=o.ap[:, 0:512], in_=xg.ap[:, 0:512]), sig=True)
                        xg.readers.append(e2)
                    tout.tile(o, e2, 4, i)
                tout.finish(dsts, ntiles * 128)
            return run

        D_ = k.dram
        pj.add("tm", src2d(O_CKV, 512), ckv_run)
        pj.add("tm", src2d(O_KR, 64), kr_run)
        for h0 in (0, 8):
            pj.add("tm", src_mq(h0, 8, 128, 64), mqrope_run(h0))
        for h0 in range(0, H, 4):
            pj.add("fm", src_mq(h0, 4, 0, 128), fm_run(4, [D_["qa_nT"].ap()[h0 + j] for j in range(4)], ntok_q))
        for h0 in range(0, H, 4):
            pj.add("fm", src2d(O_NA + h0 * 128, 512), fm_run(4, [D_["qbT"].ap()[h0 + j] for j in range(4)], ntok_q))
        for h0 in range(0, H, 4):
            pj.add("fm", src2d(O_NA + 2048 + h0 * 128, 512), fm_run(4, [D_["kbT"].ap()[h0 + j] for j in range(4)], T))
        for c0 in range(0, 2048, 512):
            pj.add("tm", src2d(O_NA + 4096 + c0, 512), v_run(D_["vb"].ap()[:, c0:c0 + 512], 512, NT))
        for h0 in range(0, H, 4):
            pj.add("tm", src2d(O_GQ + h0 * 128, 512), gqa_run(gqk[0], [D_["qcT"].ap()[h0 + j] for j in range(4)], ntq))
        pj.add("tm", src2d(O_GKV, 512), gqa_run(gqk[1], [D_["kcT"].ap()[j] for j in range(4)], NT))
        pj.add("tm", src2d(O_GKV + 512, 512), v_run(D_["vc"].ap(), 512, NT))
        for br in range(3):
            for c0 in range(0, 2048, 512):
                pj.add("fm", src2d(O_GATE + br * 2048 + c0, 512),
                       fm_run(4, [D_["gateT"].ap()[br, c0 + j * 128:c0 + (j + 1) * 128, :] for j in range(4)], ntok_q, func=AF.Sigmoid))
        pj.run()
        if "rstd_dbg" in k.dbg:
            st = Slot(None, k.dsem())
            k.store(sp, st, k.dram["rstd_dbg"].ap(), rstd_all[:], k.ev_rstd)
        k.flush()


def phase_ukv(k, l, rstd_all):
    pe, act, dve, pool, sp = k.pe, k.act, k.dve, k.pool, k.sp
    with ExitStack() as es:
        A2 = k.sb(es, "ukv_A", [128, 4, T], BF16)
        Wu = k.sb(es, "ukv_W", [128, 4, 4096], BF16)
        pb = Ring([Slot(k.ps(es, "ukv_p%d" % i, [128, 512], F32)) for i in range(4)])
        fmo = FMOut(k, es, 2, "ukv")
        vst = Ring([Slot(k.sb(es, "ukv_vst%d" % i, [128, 512], BF16), k.dsem()) for i in range(2)])
        d0, d1 = k.dsem(), k.dsem()
        evA = sp.dma(d0, A2[:], k.dram["cgT"].ap().rearrange("(k p) t -> p k t", p=128))
        evW = pool.dma(d1, Wu[:], k.dram["w_mla_ukv"].ap()[l].rearrange("(k p) n -> p k n", p=128))
        groups = tok_groups(T)
        for h in range(H):
            for gi, (g0, n) in enumerate(groups):
                b = pb.next()
                pe.wait(evA, evW, b.free_evs())
                for kk in range(4):
                    ev = pe.do(lambda t, b=b, kk=kk, h=h, g0=g0, n=n: t.matmul(b.ap[:, 0:n], lhsT=Wu[:, kk, h * 256:h * 256 + 128], rhs=A2[:, kk, g0:g0 + n],
                                                                              start=(kk == 0), stop=(kk == 3)), sig=(kk == 3))
                fmo.evac(b, ev, g0, n, gi == 0, gi == len(groups) - 1, k.dram["ka_nT"].ap()[h], T)
        for i in range(NT):
            for hg in range(4):
                b = pb.next()
                pe.wait(b.free_evs())
                for hh in range(4):
                    c0 = (hg * 4 + hh) * 256 + 128
                    for kk in range(4):
                        ev = pe.do(lambda t, b=b, kk=kk, hh=hh, c0=c0, i=i: t.matmul(b.ap[:, hh * 128:(hh + 1) * 128], lhsT=A2[:, kk, i * 128:(i + 1) * 128], rhs=Wu[:, kk, c0:c0 + 128],
                                                                                     start=(kk == 0), stop=(kk == 3)), sig=(kk == 3 and hh == 3))
                st = vst.next()
                act.wait(ev, st.free_evs())
                e2 = evac_copy(k, act, st.ap[:, :], b.ap[:, :], scale=rstd_all[:, i:i + 1])
                b.readers.append(e2)
                k.store(sp, st, k.dram["va"].ap()[i * 128:(i + 1) * 128, hg * 512:(hg + 1) * 512], st.ap[:, :], e2)
        k.flush()


def na_window(j):
    out = []
    for i in range(NLT):
        inv, anyv = [], False
        for a in range(2):
            for b in range(2):
                r = 2 * j + b
                rs = min(max(r - 4, 0), 24)
                if rs <= 2 * i + a < rs + 8:
                    anyv = True
                else:
                    inv.append((a, b))
        if anyv:
            out.append((i, 7 - 2 * (i - j), inv))
    return out


def phase_attn(k, l, br, ident, ntq, rstd_all):
    pe, act, dve, pool, sp = k.pe, k.act, k.dve, k.pool, k.sp
    D_ = k.dram
    ntok_q = ntq * 128
    mla, na = br == 0, br == 1
    with ExitStack() as es:
        slots = []
        for i in range(2):
            s = Slot(None, k.dsem())
            s.QN = k.sb(es, "at_qn%d" % i, [128, T], BF16)
            s.KN = k.sb(es, "at_kn%d" % i, [128, T], BF16)
            s.V = k.sb(es, "at_v%d" % i, [128, NT, 129], BF16)
            if mla:
                s.QR = k.sb(es, "at_qr%d" % i, [128, T], BF16)
            if na:
                s.EEraw = k.sb(es, "at_eer%d" % i, [128, 1024], F32)
                s.EE = k.sb(es, "at_ee%d" % i, [128, 1024], F32)
            slots.append(s)
        ev_init = None
        for s in slots:
            ev_init = pool.do(lambda g, s=s: g.memset(s.V[:, :, 128:129], 1.0), sig=True)
        sp.wait(ev_init)
        hring = Ring(slots)
        ev_kr = None
        if mla:
            KRb = k.sb(es, "at_krb", [128, T], BF16)
            sc_all = k.sb(es, "at_sc", [128, NT], F32)
            dk = k.dsem()
            ev_kr = sp.dma(dk, KRb[:], D_["ka_rT"].ap())
            ev_sc = dve.do(lambda v: v.tensor_scalar(out=sc_all[:], in0=rstd_all[:], scalar1=MLA_SCALE, scalar2=None, op0=ALU.mult), sig=True)
            act.wait(ev_sc)
        Sb = Ring([Slot(k.ps(es, "at_s%d" % i, [128, 512], F32)) for i in range(2)])
        Yb = Ring([Slot([k.ps(es, "at_y%d_%d" % (i, j), [128, 512], F32) for j in range(2)]) for i in range(2)])
        Tb = Ring([Slot(k.ps(es, "at_t%d" % i, [128, 512], BF16)) for i in range(2)])
        PT = Ring([Slot(k.sb(es, "at_pt%d" % i, [128, 512], BF16)) for i in range(3)])
        PR = Ring([Slot(k.sb(es, "at_pr%d" % i, [128, 128], F32)) for i in range(2)]) if na else None
        Yn = Ring([Slot(k.sb(es, "at_yn%d" % i, [128, 4, 128], BF16)) for i in range(2)])
        Rc = Ring([Slot(k.sb(es, "at_rc%d" % i, [128, 4], F32)) for i in range(2)])
        YT = Ring([Slot(k.sb(es, "at_yt%d" % i, [128, T], BF16), k.dsem()) for i in range(2)])

        def load_head(h):
            s = hring.next()
            sp.wait(s.free_evs())
            if br == 0:
                base = (h % 2) * 64
                sp.dma(s.dsem, s.QN[:, 0:ntok_q], D_["qa_nT"].ap()[h][:, 0:ntok_q])
                sp.dma(s.dsem, s.QR[base:base + 64, 0:ntok_q], D_["qa_rT"].ap()[h // 2][base:base + 64, 0:ntok_q])
                sp.dma(s.dsem, s.KN[:], D_["ka_nT"].ap()[h])
                vsrc = D_["va"].ap()[:, h * 128:(h + 1) * 128]
            elif br == 1:
                sp.dma(s.dsem, s.QN[:, 0:ntok_q], D_["qbT"].ap()[h][:, 0:ntok_q])
                sp.dma(s.dsem, s.KN[:], D_["kbT"].ap()[h])
                sp.dma(s.dsem, s.EEraw[:], D_["na_bias"].ap()[l, h])
                vsrc = D_["vb"].ap()[:, h * 128:(h + 1) * 128]
            else:
                sp.dma(s.dsem, s.QN[:, 0:ntok_q], D_["qcT"].ap()[h][:, 0:ntok_q])
                sp.dma(s.dsem, s.KN[:], D_["kcT"].ap()[h // 4])
                vsrc = D_["vc"].ap()[:, (h // 4) * 128:(h // 4 + 1) * 128]
            s.ready = sp.dma(s.dsem, s.V[:, :, 0:128], vsrc.rearrange("(i p) e -> p i e", p=128))
            s.ee_ev = None
            if na:
                act.wait(s.ready)
                s.ee_ev = act.do(lambda a, s=s: a.activation(out=s.EE[:], in_=s.EEraw[:], func=AF.Exp), sig=True)
            return s

        def head_groups():
            gs = []
            if na:
                for j in range(NLT):
                    kts = [(i, t0, inv) for (i, t0, inv) in na_window(j)] + [(16, None, None), (17, None, None)]
                    gs.append((j * 128, 128, kts))
            else:
                for g in range(4):
                    gs.append((g * 512, 512, [(i, None, None) for i in range(NT)]))
            if ntq == NT:
                gs.append((S, 256, [(16, None, None), (17, None, None)]))
            return gs

        steps = []
        for h in range(H):
            for gidx, (q0, n, kts) in enumerate(head_groups()):
                for ki, (kt, t0, inv) in enumerate(kts):
                    steps.append(dict(h=h, g=gidx, q0=q0, n=n, kt=kt, t0=t0, inv=inv, first=(ki == 0), last=(ki == len(kts) - 1),
                                      lastg=(gidx == len(head_groups()) - 1)))
        hs = {}
        state = dict(Y=None, yt=None)
        scale_c = NA_SCALE if na else GQA_SCALE

        def emit_qk(st):
            h = st["h"]
            if h not in hs:
                hs[h] = load_head(h)
            s = hs[h]
            sb_ = Sb.next()
            st["S"] = sb_
            q0, n, kt = st["q0"], st["n"], st["kt"]
            pe.wait(s.ready, ev_kr, sb_.free_evs())
            if mla:
                base = (h % 2) * 64
                pe.do(lambda t: t.matmul(sb_.ap[:, 0:n], lhsT=s.KN[:, kt * 128:(kt + 1) * 128], rhs=s.QN[:, q0:q0 + n], start=True, stop=False))
                st["qk_ev"] = pe.do(lambda t: t.matmul(sb_.ap[:, 0:n], lhsT=KRb[base:base + 64, kt * 128:(kt + 1) * 128], rhs=s.QR[base:base + 64, q0:q0 + n], start=False, stop=True), sig=True)
            else:
                st["qk_ev"] = pe.do(lambda t: t.matmul(sb_.ap[:, 0:n], lhsT=s.KN[:, kt * 128:(kt + 1) * 128], rhs=s.QN[:, q0:q0 + n], start=True, stop=True), sig=True)

        def emit_exp(st):
            s, sb_, n, kt = hs[st["h"]], st["S"], st["n"], st["kt"]
            pt = PT.next()
            st["PT"] = pt
            if st["t0"] is None:
                act.wait(st["qk_ev"], pt.free_evs())
                sc = sc_all[:, kt:kt + 1] if mla else scale_c
                e = act.do(lambda a: a.activation(out=pt.ap[:, 0:n], in_=sb_.ap[:, 0:n], func=AF.Exp, scale=sc), sig=True)
                sb_.readers.append(e)
                st["pt_ev"] = e
            else:
                pr = PR.next()
                act.wait(st["qk_ev"], pr.free_evs())
                e = act.do(lambda a: a.activation(out=pr.ap[:, :], in_=sb_.ap[:, 0:128], func=AF.Exp, scale=scale_c), sig=True)
                sb_.readers.append(e)
                t0 = st["t0"]
                dve.wait(e, s.ee_ev, pt.free_evs())
                e2 = dve.do(lambda v: v.tensor_tensor(out=pt.ap[:, 0:128], in0=pr.ap[:, :], in1=s.EE[:, t0 * 64:(t0 + 2) * 64], op=ALU.mult), sig=True)
                pr.readers.append(e2)
                if st["inv"]:
                    dve.wait(e2)
                    for (a, b) in st["inv"]:
                        e2 = dve.do(lambda v, a=a, b=b: v.memset(pt.ap[a * 64:(a + 1) * 64, b * 64:(b + 1) * 64], 0.0), sig=True)
                st["pt_ev"] = e2

        def emit_pv(st):
            s, n, kt, pt = hs[st["h"]], st["n"], st["kt"], st["PT"]
            if st["first"]:
                y = Yb.next()
                state["Y"] = y
                pe.wait(y.free_evs())
            y = state["Y"]
            pe.wait(st["pt_ev"])
            nqs = n // 128
            for qs in range(nqs):
                yap = y.ap[qs // 2][:, (qs % 2) * 256:(qs % 2) * 256 + 129]
                ev = pe.do(lambda t, qs=qs, yap=yap: t.matmul(yap, lhsT=pt.ap[:, qs * 128:(qs + 1) * 128], rhs=s.V[:, kt, :], start=(st["first"] and qs % 2 == 0), stop=st["last"]), sig=(qs == nqs - 1))
            pt.readers.append(ev)
            st["pv_ev"] = ev
            if st["lastg"] and st["last"]:
                s.readers.append(ev)

        def epilogue(st):
            y, n, q0, h = state["Y"], st["n"], st["q0"], st["h"]
            nqs = n // 128
            rc, yn = Rc.next(), Yn.next()
            dve.wait(st["pv_ev"], rc.free_evs(), yn.free_evs())
            for qs in range(nqs):
                e = dve.do(lambda v, qs=qs: v.reciprocal(out=rc.ap[:, qs:qs + 1], in_=y.ap[qs // 2][:, (qs % 2) * 256 + 128:(qs % 2) * 256 + 129]), sig=(qs == nqs - 1))
            dve.wait(e)
            for qs in range(nqs):
                e = dve.do(lambda v, qs=qs: v.tensor_scalar(out=yn.ap[:, qs, :], in0=y.ap[qs // 2][:, (qs % 2) * 256:(qs % 2) * 256 + 128], scalar1=rc.ap[:, qs:qs + 1], scalar2=None, op0=ALU.mult), sig=(qs == nqs - 1))
            y.readers.append(e)
            rc.readers.append(e)
            if st["g"] == 0:
                state["yt"] = YT.next()
                state["yt_fe"] = state["yt"].free_evs()
                state["yt_evs"] = []
            yt, yt_fe, yt_evs = state["yt"], state["yt_fe"], state["yt_evs"]
            lastg = st["lastg"]

            def pe_part():
                tb = Tb.next()
                pe.wait(e, tb.free_evs())
                for qs in range(nqs):
                    ep = pe.do(lambda t, qs=qs: t.transpose(out=tb.ap[:, qs * 128:(qs + 1) * 128], in_=yn.ap[:, qs, :], identity=ident[:]), sig=(qs == nqs - 1))
                yn.readers.append(ep)
                dve.wait(ep, yt_fe)
                ec = dve.do(lambda v: v.tensor_copy(out=yt.ap[:, q0:q0 + n], in_=tb.ap[:, 0:n]), sig=True)
                tb.readers.append(ec)
                yt_evs.append(ec)
                if lastg:
                    k.store(sp, yt, D_["yT"].ap()[br, h][:, 0:ntok_q], yt.ap[:, 0:ntok_q], list(yt_evs))
            return pe_part

        deferred = []
        emit_qk(steps[0])
        for si, st in enumerate(steps):
            emit_exp(st)
            if si + 1 < len(steps):
                nh = steps[si + 1]["h"]
                if nh not in hs and nh + 0 < H:
                    pass
                emit_qk(steps[si + 1])
            emit_pv(st)
            deferred = [(c - 1, f) for (c, f) in deferred]
            for c, f in [d for d in deferred if d[0] <= 0]:
                f()
            deferred = [d for d in deferred if d[0] > 0]
            if st["last"]:
                deferred.append((2, epilogue(st)))
            if st["first"] and st["g"] == 0 and st["h"] + 1 < H and (st["h"] + 1) not in hs:
                hs[st["h"] + 1] = load_head(st["h"] + 1)
        for c, f in deferred:
            f()
        k.flush()


def phase_merge(k, l, ntq):
    pe, act, dve, pool, sp = k.pe, k.act, k.dve, k.pool, k.sp
    D_ = k.dram
    ngt = ntq // 2
    ntg = ngt * 128
    WB = D_["w_branch"].ap()[l]
    with ExitStack() as es:
        A3 = k.sb(es, "mg_A", [128, 48, ntg], BF16)
        ws = Ring([Slot(k.sb(es, "mg_w%d" % i, [128, 3, 16, 256], BF16), k.dsem()) for i in range(2)])
        gs = Ring([Slot(k.sb(es, "mg_g%d" % i, [128, 3, ntg], BF16), k.dsem()) for i in range(2)])
        accs = Ring([Slot(k.sb(es, "mg_acc%d" % i, [128, ntg], F32)) for i in range(2)])
        tmps = Ring([Slot(k.sb(es, "mg_tmp%d" % i, [128, 512], F32)) for i in range(3)])
        outs = Ring([Slot(k.sb(es, "mg_o%d" % i, [128, ntg], BF16), k.dsem()) for i in range(2)])
        pb = Ring([Slot(k.ps(es, "mg_p%d" % i, [128, 512], F32)) for i in range(6)])
        dA = k.dsem()
        a_readers = []
        groups = tok_groups(ntg)

        def wload(cs):
            s = ws.next()
            pool.wait(s.free_evs())
            for i in range(3):
                s.ready = pool.dma(s.dsem, s.ap[:, i, :, :], WB[i][:, cs * 256:(cs + 1) * 256].rearrange("(k p) n -> p k n", p=128))
            return s

        for tg in range(2):
            tok0 = tg * ntg
            sp.wait(a_readers)
            a_readers = []
            for i in range(3):
                evA = sp.dma(dA, A3[:, i * 16:(i + 1) * 16, :], D_["yT"].ap()[i][:, :, tok0:tok0 + ntg].rearrange("h p t -> p h t"))
            nxt = wload(0)
            for cs in range(8):
                cur = nxt
                if cs + 1 < 8:
                    nxt = wload(cs + 1)
                for cc in range(2):
                    c = cs * 2 + cc
                    g = gs.next()
                    sp.wait(g.free_evs())
                    g.ready = sp.dma(g.dsem, g.ap[:], D_["gateT"].ap()[:, c * 128:(c + 1) * 128, tok0:tok0 + ntg].rearrange("i p t -> p i t"))
                    acc, o = accs.next(), outs.next()
                    acc_fe, o_fe = acc.free_evs(), o.free_evs()
                    last = {}
                    oevs = []
                    for i in range(3):
                        for (g0, n) in groups:
                            b = pb.next()
                            pe.wait(cur.ready, evA, b.free_evs())
                            for hc in range(16):
                                ev = pe.do(lambda t, b=b, i=i, hc=hc, cc=cc, g0=g0, n=n, cur=cur: t.matmul(b.ap[:, 0:n], lhsT=cur.ap[:, i, hc, cc * 128:(cc + 1) * 128], rhs=A3[:, i * 16 + hc, g0:g0 + n],
                                                                                                          start=(hc == 0), stop=(hc == 15)), sig=(hc == 15))
                            cur.readers.append(ev)
                            a_readers.append(ev)
                            if i == 0:
                                dve.wait(ev, g.ready, acc_fe)
                                e = dve.do(lambda v, b=b, g=g, acc=acc, g0=g0, n=n: v.tensor_tensor(out=acc.ap[:, g0:g0 + n], in0=b.ap[:, 0:n], in1=g.ap[:, 0, g0:g0 + n], op=ALU.mult), sig=True)
                                b.readers.append(e)
                                last[g0] = e
                            else:
                                tmp = tmps.next()
                                dve.wait(ev, g.ready, tmp.free_evs())
                                e = dve.do(lambda v, b=b, g=g, tmp=tmp, i=i, g0=g0, n=n: v.tensor_tensor(out=tmp.ap[:, 0:n], in0=b.ap[:, 0:n], in1=g.ap[:, i, g0:g0 + n], op=ALU.mult), sig=True)
                                b.readers.append(e)
                                pool.wait(e, last[g0])
                                if i == 1:
                                    e2 = pool.do(lambda p_, acc=acc, tmp=tmp, g0=g0, n=n: p_.tensor_tensor(out=acc.ap[:, g0:g0 + n], in0=acc.ap[:, g0:g0 + n], in1=tmp.ap[:, 0:n], op=ALU.add), sig=True)
                                    last[g0] = e2
                                else:
                                    pool.wait(o_fe)
                                    e2 = pool.do(lambda p_, acc=acc, tmp=tmp, o=o, g0=g0, n=n: p_.tensor_tensor(out=o.ap[:, g0:g0 + n], in0=acc.ap[:, g0:g0 + n], in1=tmp.ap[:, 0:n], op=ALU.add), sig=True)
                                    oevs.append(e2)
                                    acc.readers.append(e2)
                                tmp.readers.append(e2)
                    g.readers.append(e)
                    k.store(sp, o, D_["accT"].ap()[c * 128:(c + 1) * 128, tok0:tok0 + ntg], o.ap[:, :], oevs)
        k.flush()


def phase_tm_out(k, l, which, ntq):
    pe, act, dve, pool, sp = k.pe, k.act, k.dve, k.pool, k.sp
    D_ = k.dram
    with ExitStack() as es:
        if which == "out":
            KC, wc, Wd, ngroups, ngt = 16, 512, D_["w_out"].ap()[l], 1, ntq
            Asrc = D_["accT"].ap()
        else:
            KC, wc, Wd, ngroups, ngt = NFC, 256, D_["w_down"].ap()[l], 2, ntq // 2
            Asrc = D_["hidT"].ap()
        ntg = ngt * 128
        A = k.sb(es, "to_A", [128, KC, ntg], BF16)
        pj = Proj(k, es, A, KC, wc, 2, 4, "to")
        stg = Ring([Slot(k.sb(es, "to_st%d" % i, [128, 512], F32), k.dsem()) for i in range(3)])
        dA = k.dsem()
        tog = [0]
        a_readers = []
        for tg in range(ngroups):
            tok0 = tg * ntg
            sp.wait(a_readers)
            evA = sp.dma(dA, A[:], Asrc[:, tok0:tok0 + ntg].rearrange("(k p) t -> p k t", p=128))

            def mk_run(c0, evA=evA, tok0=tok0):
                def run(slab):
                    for i in range(ngt):
                        b, ev = pj.tm_tile(slab, wc, i, extra_wait=evA)
                        a_readers.append(ev)
                        st = stg.next()
                        eng = act if tog[0] % 2 == 0 else dve
                        tog[0] += 1
                        eng.wait(ev, st.free_evs())
                        e2 = evac_copy(k, eng, st.ap[:, 0:wc], b.ap[:, 0:wc])
                        b.readers.append(e2)
                        k.store(sp, st, D_["fout"].ap()[tok0 + i * 128:tok0 + (i + 1) * 128, c0:c0 + wc], st.ap[:, 0:wc], e2)
                return run

            for c0 in range(0, D, wc):
                pj.add("tm", (lambda s, c0=c0: (s.ap[:, :, 0:wc], Wd[:, c0:c0 + wc].rearrange("(k p) n -> p k n", p=128))), mk_run(c0))
            pj.run()
        k.flush()


def phase_resid(k, l, sub, src, ntiles, final):
    pe, act, dve, pool, sp = k.pe, k.act, k.dve, k.pool, k.sp
    D_ = k.dram
    mv_t = D_["mvec"]
    with ExitStack() as es:
        d0 = k.dsem()
        gate = {}
        for j, nm in ((0, "lat"), (1, "ctx")):
            gate[j] = k.sb(es, "rs_gate" + nm, [128, 2048], F32)
            sp.dma(d0, gate[j][:], bc_ap(mv_t, (l * 2 + j) * 12288 + (2 + 3 * sub) * 2048, 2048))
        lng = k.sb(es, "rs_lng", [128, 2048], F32)
        lnb = k.sb(es, "rs_lnb", [128, 2048], F32)
        sp.dma(d0, lng[:], bc_ap(D_["ln_f_g" if sub else "ln_a_g"], l * 2048, 2048))
        evc = sp.dma(d0, lnb[:], bc_ap(D_["ln_f_b" if sub else "ln_a_b"], l * 2048, 2048))
        xr = Ring([Slot(k.sb(es, "rs_x%d" % i, [128, 2048], F32), k.dsem()) for i in range(2)])
        fr = Ring([Slot(k.sb(es, "rs_f%d" % i, [128, 2048], F32), k.dsem()) for i in range(2)])
        orr = Ring([Slot(k.sb(es, "rs_o%d" % i, [128, 2048], F32), k.dsem()) for i in range(2)])
        sm = Ring([Slot(k.sb(es, "rs_sm%d" % i, [128, 32], F32)) for i in range(2)])
        fout = D_["fout"].ap()

        def load(i):
            xs, fs = xr.next(), fr.next()
            sp.wait(xs.free_evs(), fs.free_evs())
            xs.ready = sp.dma(xs.dsem, xs.ap[:], src(i))
            fs.ready = sp.dma(fs.dsem, fs.ap[:], fout[i * 128:(i + 1) * 128, :])
            return xs, fs

        nxt = load(0)
        for i in range(ntiles):
            xs, fs = nxt
            if i + 1 < ntiles:
                nxt = load(i + 1)
            gt = gate[0 if i < NLT else 1]
            dve.wait(xs.ready, fs.ready, evc)
            e = dve.do(lambda v, fs=fs, gt=gt: v.tensor_tensor(out=fs.ap[:], in0=fs.ap[:], in1=gt[:], op=ALU.mult), sig=True)
            dve.wait(e)
            e = dve.do(lambda v, xs=xs, fs=fs: v.scalar_tensor_tensor(out=xs.ap[:], in0=xs.ap[:], scalar=ALPHA, in1=fs.ap[:], op0=ALU.mult, op1=ALU.add), sig=True)
            smt = sm.next()
            st = smt.ap
            stats = st[:, 0:24].rearrange("p (c s) -> p c s", s=6)
            mv, rstd, nmr = st[:, 24:26], st[:, 26:27], st[:, 27:28]
            dve.wait(e, smt.free_evs())
            e = ln_stats(k, xs.ap, mv, rstd, nmr, POST_EPS, stats)
            act.wait(e)
            e = act.do(lambda a, xs=xs, fs=fs, rstd=rstd, nmr=nmr: a.activation(out=fs.ap[:], in_=xs.ap[:], func=AF.Identity, bias=nmr, scale=rstd), sig=True)
            xs.readers.append(e)
            smt.readers.append(e)
            dve.wait(e)
            e = dve.do(lambda v, fs=fs: v.tensor_tensor(out=fs.ap[:], in0=fs.ap[:], in1=lng[:], op=ALU.mult), sig=True)
            os_ = orr.next()
            pool.wait(e, os_.free_evs())
            e = pool.do(lambda g, fs=fs, os_=os_: g.tensor_tensor(out=os_.ap[:], in0=fs.ap[:], in1=lnb[:], op=ALU.add), sig=True)
            fs.readers.append(e)
            if final:
                if i < NLT:
                    k.store(sp, os_, k.out.ap()[i * 128:(i + 1) * 128, :], os_.ap[:], e)
            else:
                k.store(sp, os_, D_["xres"].ap()[i * 128:(i + 1) * 128, :], os_.ap[:], e)
        k.flush()


def phase_ffn_up(k, l, hT, ntq):
    pe, act, dve, pool, sp = k.pe, k.act, k.dve, k.pool, k.sp
    D_ = k.dram
    Wu = D_["w_up"].ap()[l]
    ntok = ntq * 128
    has_ctx = ntq == NT
    GW = T + 4
    with ExitStack() as es:
        pj = Proj(k, es, hT, 16, 512, 2, 4, "fu")
        Gb = Ring([Slot(k.sb(es, "fu_g%d" % i, [128, GW], F32)) for i in range(2)])
        Vb = Ring([Slot(k.sb(es, "fu_v%d" % i, [128, T], F32)) for i in range(2)])
        Cb = Ring([Slot(k.sb(es, "fu_c%d" % i, [128, T], F32)) for i in range(2)])
        Ho = Ring([Slot(k.sb(es, "fu_h%d" % i, [128, T], BF16), k.dsem()) for i in range(2)])
        cp = k.sb(es, "fu_cp", [128, NFC, 4], F32)
        d0 = k.dsem()
        evcp = sp.dma(d0, cp[:], D_["convp"].ap()[l])
        ev_ms = None
        for s in Gb.slots:
            ev_ms = pool.do(lambda g, s=s: g.memset(s.ap[:], 0.0), sig=True)
        groups = tok_groups(ntok)
        segs = [(0, S, 0)] + ([(S, CL, S + 2)] if has_ctx else [])

        def gcol(g0):
            return 1 + g0 if g0 < S else S + 3 + (g0 - S)

        def mk_run(sp_):
            def run(slab):
                for cc in range(2):
                    ch = sp_ * 2 + cc
                    gb, vb, cb, ho = Gb.next(), Vb.next(), Cb.next(), Ho.next()
                    gfe, vfe = gb.free_evs(), vb.free_evs()
                    gevs, vevs = [], []
                    for (g0, n) in groups:
                        b, ev = pj.fm_group(slab, cc, g0, n)
                        act.wait(ev, gfe, ev_ms)
                        e = evac_copy(k, act, gb.ap[:, gcol(g0):gcol(g0) + n], b.ap[:, 0:n])
                        b.readers.append(e)
                        gevs.append(e)
                    for (g0, n) in groups:
                        b, ev = pj.fm_group(slab, 2 + cc, g0, n)
                        act.wait(ev, vfe)
                        e = evac_copy(k, act, vb.ap[:, g0:g0 + n], b.ap[:, 0:n])
                        b.readers.append(e)
                        vevs.append(e)
                    w = [cp[:, ch, j:j + 1] for j in range(4)]
                    dve.wait(gevs, evcp, cb.free_evs())
                    for (t0, n, c0) in segs:
                        e = dve.do(lambda v, gb=gb, cb=cb, t0=t0, n=n, c0=c0, w=w: v.tensor_scalar(out=cb.ap[:, t0:t0 + n], in0=gb.ap[:, c0:c0 + n], scalar1=w[0], scalar2=w[3], op0=ALU.mult, op1=ALU.add), sig=True)
                    for j in (1, 2):
                        dve.wait(e)
                        for (t0, n, c0) in segs:
                            e = dve.do(lambda v, gb=gb, cb=cb, t0=t0, n=n, c0=c0, w=w, j=j: v.scalar_tensor_tensor(out=cb.ap[:, t0:t0 + n], in0=gb.ap[:, c0 + j:c0 + j + n], scalar=w[j], in1=cb.ap[:, t0:t0 + n],
                                                                                                                     op0=ALU.mult, op1=ALU.add), sig=True)
                    gb.readers.append(e)
                    act.wait(e)
                    e = act.do(lambda a, cb=cb: a.activation(out=cb.ap[:, 0:ntok], in_=cb.ap[:, 0:ntok], func=AF.Silu), sig=True)
                    pool.wait(e, vevs, ho.free_evs())
                    e = pool.do(lambda g, cb=cb, vb=vb, ho=ho: g.tensor_tensor(out=ho.ap[:, 0:ntok], in0=cb.ap[:, 0:ntok], in1=vb.ap[:, 0:ntok], op=ALU.mult), sig=True)
                    cb.readers.append(e)
                    vb.readers.append(e)
                    k.store(sp, ho, D_["hidT"].ap()[ch * 128:(ch + 1) * 128, 0:ntok], ho.ap[:, 0:ntok], e)
            return run

        for sp_ in range(NFC // 2):
            pj.add("fm", (lambda s, sp_=sp_: [(s.ap[:, :, 0:256], Wu[:, sp_ * 256:(sp_ + 1) * 256].rearrange("(k p) n -> p k n", p=128)),
                                              (s.ap[:, :, 256:512], Wu[:, DFF + sp_ * 256:DFF + (sp_ + 1) * 256].rearrange("(k p) n -> p k n", p=128))]), mk_run(sp_))
        pj.run()
        k.flush()
```

```python
import numpy as np
from contextlib import ExitStack
import concourse.bass as bass
import concourse.mybir as mybir
from concourse.bass_utils import run_bass_kernel_spmd

F32 = mybir.dt.float32
BF16 = mybir.dt.bfloat16
AF = mybir.ActivationFunctionType
ALU = mybir.AluOpType

D = 2048
S = 2048
CL = 256
T = S + CL
NT = T // 128
NLT = S // 128
L = 2
H = 16
DFF = 5632
NFC = DFF // 128
N_IN = 19008
O_MQ, O_CKV, O_KR, O_NA, O_GQ, O_GKV, O_GATE = 0, 3072, 3584, 3648, 9792, 11840, 12864
ALPHA = (2 * L) ** 0.25
MLA_SCALE = 192 ** -0.5
NA_SCALE = 128 ** -0.5
GQA_SCALE = 128 ** -0.5
ADA_EPS, POST_EPS, RMS_EPS = 1e-6, 1e-5, 1e-6
NEG = -1.0e4


class DSem:
    def __init__(self, h, key):
        self.h, self.key, self.cnt = h, key, 0


class Eng:
    def __init__(self, name, sem):
        self.name, self.sem, self.cnt, self.ops, self.waited = name, sem, 0, [], {}

    def wait(self, *evs):
        for ev in evs:
            if ev is None:
                continue
            if isinstance(ev, list):
                self.wait(*ev)
                continue
            key, sem, val = ev
            if self.waited.get(key, 0) >= val:
                continue
            self.waited[key] = val
            self.ops.append(lambda e, sem=sem, val=val: e.wait_ge(sem, val))

    def do(self, fn, sig=False):
        if sig:
            self.cnt += 1
            n, sem = self.cnt, self.sem
            self.ops.append(lambda e: fn(e).then_inc(sem, 1))
            return (self.name, sem, n)
        self.ops.append(fn)
        return None

    def dma(self, dsem, out, in_):
        dsem.cnt += 16
        n, h = dsem.cnt, dsem.h
        self.ops.append(lambda e: e.dma_start(out=out, in_=in_).then_inc(h, 16))
        return (dsem.key, h, n)


class Slot:
    def __init__(self, ap, dsem=None):
        self.ap, self.dsem = ap, dsem
        self.ready = None
        self.readers = []

    def free_evs(self):
        r = self.readers
        self.readers = []
        return r


class Ring:
    def __init__(self, slots):
        self.slots, self.i = slots, 0

    def next(self):
        s = self.slots[self.i % len(self.slots)]
        self.i += 1
        return s


class K:
    def __init__(self, dbg=(), stop_after=None):
        self.dbg, self.stop_after = set(dbg), stop_after
        self.branches = (0, 1, 2)
        self.nc = nc = bass.Bass("TRN2", target_bir_lowering=False)
        self.es = ExitStack()
        self.ds_pool, self.ds_i = [], 0
        self.uid = 0
        self.engs = {}
        for n in ("pe", "act", "dve", "pool", "sp"):
            self.engs[n] = Eng(n, self.es.enter_context(nc.semaphore("sem_" + n)))
        self.pe, self.act, self.dve, self.pool, self.sp = (self.engs[n] for n in ("pe", "act", "dve", "pool", "sp"))
        self.dram = {}
        self.pending = []

    def inp(self, name, shape, dt=F32):
        t = self.nc.dram_tensor(name, list(shape), dt, kind="ExternalInput")
        self.dram[name] = t
        return t

    def scr(self, name, shape, dt=BF16):
        if name in self.dbg:
            t = self.nc.dram_tensor(name, list(shape), dt, kind="ExternalOutput")
        else:
            t = self.nc.dram_tensor(name, list(shape), dt)
        self.dram[name] = t
        return t

    def dsem(self):
        if self.ds_i >= len(self.ds_pool):
            n = len(self.ds_pool)
            self.ds_pool.append(DSem(self.es.enter_context(self.nc.semaphore("ds%d" % n)), "ds%d" % n))
        d = self.ds_pool[self.ds_i]
        self.ds_i += 1
        return d

    def sb(self, es, name, shape, dt):
        self.uid += 1
        return es.enter_context(self.nc.sbuf_tensor("%s_%d" % (name, self.uid), list(shape), dt))

    def ps(self, es, name, shape, dt):
        self.uid += 1
        return es.enter_context(self.nc.psum_tensor("%s_%d" % (name, self.uid), list(shape), dt))

    def flush(self, waiter=None):
        w = waiter or self.sp
        w.wait(self.pending)
        self.pending = []
        with self.nc.Block() as block:
            ops = {n: list(e.ops) for n, e in self.engs.items()}

            @block.tensor
            def _(t):
                for f in ops["pe"]:
                    f(t)

            @block.scalar
            def _(a):
                for f in ops["act"]:
                    f(a)

            @block.vector
            def _(v):
                for f in ops["dve"]:
                    f(v)

            @block.gpsimd
            def _(g):
                for f in ops["pool"]:
                    f(g)

            @block.sync
            def _(s):
                for f in ops["sp"]:
                    f(s)
        for e in self.engs.values():
            e.ops = []
        self.ds_i = 0

    def store(self, eng, slot, out, in_, after):
        eng.wait(after)
        ev = eng.dma(slot.dsem, out, in_)
        slot.readers.append(ev)
        self.pending.append(ev)
        return ev


def bc_ap(t, off, n, parts=128):
    return bass.AP(t, off, [[0, parts], [1, n]])


def phase_ada(k, l):
    pe, act, dve, pool, sp = k.pe, k.act, k.dve, k.pool, k.sp
    with ExitStack() as es:
        csb = k.sb(es, "ada_c", [128, 32], F32)
        scT = k.sb(es, "ada_sc", [128, 16, 2], BF16)
        bsb = k.sb(es, "ada_b", [2, 12288], F32)
        msb = k.sb(es, "ada_m", [2, 12288], F32)
        ws = Ring([Slot(k.sb(es, "ada_w%d" % i, [128, 16, 512], BF16), k.dsem()) for i in range(2)])
        pb = Ring([Slot(k.ps(es, "ada_p%d" % i, [128, 512], F32)) for i in range(2)])
        d0 = k.dsem()
        sp.dma(d0, csb[:], k.dram["cT"].ap().rearrange("p k j -> p (k j)"))
        e2 = sp.dma(d0, bsb[:], bc_ap(k.dram["b_ada"], l * 12288, 12288, parts=2))
        act.wait(e2)
        ev_sc = act.do(lambda a: a.activation(out=scT[:].rearrange("p k j -> p (k j)"), in_=csb[:], func=AF.Silu), sig=True)
        wada = k.dram["w_ada"].ap()

        def load(n):
            s = ws.next()
            pool.wait(s.free_evs())
            s.ready = pool.dma(s.dsem, s.ap[:], wada[l, :, n * 512:(n + 1) * 512].rearrange("(k p) n -> p k n", p=128))
            return s

        def mm(b, cur, kk):
            return lambda t: t.matmul(b.ap[0:2, :], lhsT=scT[:, kk, :], rhs=cur.ap[:, kk, :], start=(kk == 0), stop=(kk == 15))

        def addb(b, n):
            return lambda v: v.tensor_tensor(out=msb[:, n * 512:(n + 1) * 512], in0=b.ap[0:2, :], in1=bsb[:, n * 512:(n + 1) * 512], op=ALU.add)

        nxt = load(0)
        ev2 = None
        for n in range(24):
            cur = nxt
            if n + 1 < 24:
                nxt = load(n + 1)
            b = pb.next()
            pe.wait(cur.ready, ev_sc, b.free_evs())
            for kk in range(16):
                ev = pe.do(mm(b, cur, kk), sig=(kk == 15))
            cur.readers.append(ev)
            dve.wait(ev, e2)
            ev2 = dve.do(addb(b, n), sig=True)
            b.readers.append(ev2)
        st = Slot(None, k.dsem())
        k.store(sp, st, k.dram["mvec"].ap()[l], msb[:], ev2)
        k.flush()


def ln_stats(k, xin_ap, mv, rstd, nmr, eps, stats):
    dve, act = k.dve, k.act
    for c in range(4):
        ev = dve.do(lambda v, c=c: v.bn_stats(out=stats[:, c, :], in_=xin_ap[:, c * 512:(c + 1) * 512]), sig=(c == 3))
    dve.wait(ev)
    ev = dve.do(lambda v: v.bn_aggr(out=mv, in_=stats), sig=True)
    act.wait(ev)
    ev = act.do(lambda a: a.activation(out=rstd, in_=mv[:, 1:2], func=AF.Sqrt, bias=eps, scale=1.0), sig=True)
    dve.wait(ev)
    ev = dve.do(lambda v: v.reciprocal(out=rstd, in_=rstd), sig=True)
    dve.wait(ev)
    return dve.do(lambda v: v.tensor_scalar(out=nmr, in0=mv[:, 0:1], scalar1=-1.0, scalar2=rstd, op0=ALU.mult, op1=ALU.mult), sig=True)


def phase_mod(k, l, sub, src, hT, ident, ntiles):
    pe, act, dve, pool, sp = k.pe, k.act, k.dve, k.pool, k.sp
    mv_t = k.dram["mvec"]
    with ExitStack() as es:
        bcs = {}
        d0 = k.dsem()
        evb = None
        for j, nm in ((0, "lat"), (1, "ctx")):
            sh = k.sb(es, "mod_sh" + nm, [128, 2048], F32)
            s1 = k.sb(es, "mod_s1" + nm, [128, 2048], F32)
            base = (l * 2 + j) * 12288 + sub * 3 * 2048
            sp.dma(d0, sh[:], bc_ap(mv_t, base, 2048))
            evb = sp.dma(d0, s1[:], bc_ap(mv_t, base + 2048, 2048))
            bcs[j] = (sh, s1)
        dve.wait(evb)
        for j in (0, 1):
            s1 = bcs[j][1]
            evb2 = dve.do(lambda v, s1=s1: v.tensor_scalar(out=s1[:], in0=s1[:], scalar1=1.0, scalar2=None, op0=ALU.add), sig=True)
        xr = Ring([Slot(k.sb(es, "mod_x%d" % i, [128, 2048], F32), k.dsem()) for i in range(2)])
        xn = Ring([Slot(k.sb(es, "mod_xn%d" % i, [128, 2048], F32)) for i in range(2)])
        hb = Ring([Slot(k.sb(es, "mod_hb%d" % i, [128, 2048], BF16)) for i in range(2)])
        sm = Ring([Slot(k.sb(es, "mod_sm%d" % i, [128, 32], F32)) for i in range(2)])
        pt = Ring([Slot(k.ps(es, "mod_pt%d" % i, [128, 512], BF16)) for i in range(4)])

        def load(i):
            s = xr.next()
            sp.wait(s.free_evs())
            s.ready = sp.dma(s.dsem, s.ap[:], src(i))
            return s

        nxt = load(0)
        for i in range(ntiles):
            cur = nxt
            if i + 1 < ntiles:
                nxt = load(i + 1)
            j = 0 if i < NLT else 1
            sh, s1 = bcs[j]
            smt = sm.next()
            st = smt.ap
            stats = st[:, 0:24].rearrange("p (c s) -> p c s", s=6)
            mv, rstd, nmr = st[:, 24:26], st[:, 26:27], st[:, 27:28]
            dve.wait(cur.ready, smt.free_evs())
            ev = ln_stats(k, cur.ap, mv, rstd, nmr, ADA_EPS, stats)
            xs = xn.next()
            act.wait(ev, xs.free_evs())
            ev = act.do(lambda a, xs=xs, cur=cur, rstd=rstd, nmr=nmr: a.activation(out=xs.ap[:], in_=cur.ap[:], func=AF.Identity, bias=nmr, scale=rstd), sig=True)
            cur.readers.append(ev)
            smt.readers.append(ev)
            dve.wait(ev, evb2)
            ev = dve.do(lambda v, xs=xs, s1=s1: v.tensor_tensor(out=xs.ap[:], in0=xs.ap[:], in1=s1[:], op=ALU.mult), sig=True)
            hs = hb.next()
            pool.wait(ev, hs.free_evs())
            ev = pool.do(lambda g, xs=xs, hs=hs, sh=sh: g.tensor_tensor(out=hs.ap[:], in0=xs.ap[:], in1=sh[:], op=ALU.add), sig=True)
            xs.readers.append(ev)
            for jj in range(4):
                p = pt.next()
                pe.wait(ev, p.free_evs())
                for a in range(4):
                    kk = 4 * jj + a
                    evp = pe.do(lambda t, p=p, hs=hs, kk=kk, a=a: t.transpose(out=p.ap[:, a * 128:(a + 1) * 128], in_=hs.ap[:, kk * 128:(kk + 1) * 128], identity=ident[:]), sig=(a == 3))
                eng = act if jj % 2 == 0 else dve
                eng.wait(evp)
                if eng is act:
                    evc = eng.do(lambda a_, p=p, jj=jj, i=i: a_.copy(out=hT[:, 4 * jj:4 * jj + 4, i * 128:(i + 1) * 128], in_=p.ap[:].rearrange("p (a b) -> p a b", a=4)), sig=True)
                else:
                    evc = eng.do(lambda v, p=p, jj=jj, i=i: v.tensor_copy(out=hT[:, 4 * jj:4 * jj + 4, i * 128:(i + 1) * 128], in_=p.ap[:].rearrange("p (a b) -> p a b", a=4)), sig=True)
                p.readers.append(evc)
            hs.readers.append(evp)
        k.flush()


def declare(k):
    k.inp("x", [S, D])
    k.inp("ctx", [CL, D])
    k.inp("cT", [128, 16, 2])
    k.inp("w_ada", [L, D, 6 * D])
    k.inp("b_ada", [L, 6 * D])
    k.inp("w_in", [L, D, N_IN])
    k.inp("mla_kv_norm", [L, 512])
    k.inp("w_mla_ukv", [L, 512, 4096])
    k.inp("gqa_q_norm", [L, 128])
    k.inp("gqa_k_norm", [L, 128])
    k.inp("na_bias", [L, H, 128, 1024])
    k.inp("w_branch", [L, 3, D, D])
    k.inp("w_out", [L, D, D])
    k.inp("ln_a_g", [L, D])
    k.inp("ln_a_b", [L, D])
    k.inp("w_up", [L, D, 2 * DFF])
    k.inp("convp", [L, 128, NFC, 4])
    k.inp("w_down", [L, DFF, D])
    k.inp("ln_f_g", [L, D])
    k.inp("ln_f_b", [L, D])
    k.inp("ropeM", [2, 128, NLT, 64])
    k.inp("ropeG", [2, 128, NLT, 128])
    k.out = k.nc.dram_tensor("out", [S, D], F32, kind="ExternalOutput")
    k.scr("mvec", [L, 2, 6 * D], F32)
    k.scr("xres", [T, D], F32)
    k.scr("fout", [T, D], F32)
    k.scr("hT_dbg", [D, T], BF16)
    k.scr("qa_nT", [H, 128, T])
    k.scr("qa_rT", [H // 2, 128, T])
    k.scr("ka_nT", [H, 128, T])
    k.scr("ka_rT", [128, T])
    k.scr("cgT", [512, T])
    k.scr("va", [T, D])
    k.scr("qbT", [H, 128, T])
    k.scr("kbT", [H, 128, T])
    k.scr("vb", [T, D])
    k.scr("qcT", [H, 128, T])
    k.scr("kcT", [4, 128, T])
    k.scr("vc", [T, 512])
    k.scr("gateT", [3, D, T])
    k.scr("yT", [3, H, 128, T])
    k.scr("accT", [D, T])
    k.scr("hidT", [DFF, T])
    k.scr("rstd_dbg", [128, NT], F32)


def make_ident(k, es):
    ident = k.sb(es, "ident", [128, 128], BF16)
    k.pool.do(lambda g: g.memset(ident[:], 0.0))
    k.pool.do(lambda g: g.affine_select(out=ident[:], in_=ident[:], pattern=[[-1, 128]], compare_op=ALU.not_equal, fill=1.0, base=0, channel_multiplier=1))
    return ident


def build(dbg=(), stop_after=None):
    k = K(dbg, stop_after)
    declare(k)
    sp = k.sp
    with ExitStack() as es:
        ident = make_ident(k, es)
        k.flush()
        for l in range(L):
            phase_ada(k, l)
        if stop_after == "ada":
            return k
        xin, cin, xres = k.dram["x"].ap(), k.dram["ctx"].ap(), k.dram["xres"].ap()
        for l in range(L):
            ntq = NT if l < L - 1 else NLT
            if l == 0:
                src = lambda i: (xin[i * 128:(i + 1) * 128, :] if i < NLT else cin[(i - NLT) * 128:(i - NLT + 1) * 128, :])
            else:
                src = lambda i: xres[i * 128:(i + 1) * 128, :]
            with ExitStack() as es2:
                hT = k.sb(es2, "hT", [128, 16, T], BF16)
                phase_mod(k, l, 0, src, hT, ident, NT)
                if "hT_dbg" in k.dbg and l == 0:
                    st = Slot(None, k.dsem())
                    k.store(sp, st, k.dram["hT_dbg"].ap().rearrange("(k p) t -> p k t", p=128), hT[:], None)
                    k.flush()
                if stop_after == "mod":
                    return k
                rms_all = k.sb(es2, "rms_all", [128, NT], F32)
                rstd_all = k.sb(es2, "rstd_all", [128, NT], F32)
                phase_win(k, l, hT, ident, ntq, rms_all, rstd_all)
                if stop_after == "win":
                    return k
                phase_ukv(k, l, rstd_all)
                if stop_after == "ukv":
                    return k
                for br in k.branches:
                    phase_attn(k, l, br, ident, ntq, rstd_all)
                if stop_after == "attn":
                    return k
            phase_merge(k, l, ntq)
            if stop_after == "merge":
                return k
            phase_tm_out(k, l, "out", ntq)
            phase_resid(k, l, 0, src, ntq, False)
            if stop_after == "mixer":
                return k
            srcx = lambda i: xres[i * 128:(i + 1) * 128, :]
            with ExitStack() as es2:
                hT = k.sb(es2, "hT", [128, 16, T], BF16)
                phase_mod(k, l, 1, srcx, hT, ident, ntq)
                phase_ffn_up(k, l, hT, ntq)
            if stop_after == "ffnup":
                return k
            phase_tm_out(k, l, "down", ntq)
            phase_resid(k, l, 1, srcx, ntq, l == L - 1)
            if stop_after == "layer%d" % l:
                return k
    return k


def host_inputs(inputs):
    f32 = np.float32
    common = {}
    for n in ("w_ada", "b_ada", "w_in", "mla_kv_norm", "w_mla_ukv", "gqa_q_norm", "gqa_k_norm", "w_branch",
              "w_out", "ln_a_g", "ln_a_b", "w_up", "w_down", "ln_f_g", "ln_f_b"):
        common[n] = np.ascontiguousarray(inputs[n], dtype=f32)
    cw = np.asarray(inputs["conv_w"], f32)
    cb = np.asarray(inputs["conv_b"], f32)
    cp = np.concatenate([cw, cb[:, None, :]], axis=1)
    common["convp"] = np.ascontiguousarray(cp.reshape(L, 4, NFC, 128).transpose(0, 3, 2, 1))
    rpb = np.asarray(inputs["na_rpb"], f32)
    a_ = np.arange(2)[:, None, None, None]
    kc = np.arange(64)[None, :, None, None]
    t_ = np.arange(16)[None, None, :, None]
    qc = np.arange(64)[None, None, None, :]
    dr = a_ - t_ + 7
    cs = np.clip(qc - 8, 0, 48)
    valid = (np.abs(dr) <= 7) & (kc >= cs) & (kc < cs + 16)
    valid = np.broadcast_to(valid, (2, 64, 16, 64))
    dri = np.broadcast_to(np.clip(dr + 7, 0, 14), (2, 64, 16, 64))
    dci = np.broadcast_to(np.clip(kc - qc + 15, 0, 30), (2, 64, 16, 64))
    nb = rpb[:, :, dri, dci]
    nb = np.where(valid[None, None], nb, f32(NEG)).astype(f32)
    common["na_bias"] = np.ascontiguousarray(nb.reshape(L, H, 128, 1024))
    tok = np.arange(S)
    rows, cols = (tok // 64).astype(f32), (tok % 64).astype(f32)

    def tables(quarter):
        fr = (f32(10000.0) ** (-np.arange(quarter, dtype=f32) / f32(quarter))).astype(f32)
        ar, ac = rows[:, None] * fr[None, :], cols[:, None] * fr[None, :]
        C = np.concatenate([np.cos(ar), np.cos(ar), np.cos(ac), np.cos(ac)], 1)
        Sg = np.concatenate([-np.sin(ar), np.sin(ar), -np.sin(ac), np.sin(ac)], 1)
        tb = np.stack([C, Sg], 0).astype(f32)
        return np.ascontiguousarray(tb.reshape(2, NLT, 128, 4 * quarter).transpose(0, 2, 1, 3))
    common["ropeM"] = tables(16)
    common["ropeG"] = tables(32)
    x = np.asarray(inputs["x"], f32)
    ctx = np.asarray(inputs["ctx"], f32)
    c = np.asarray(inputs["c"], f32)
    cc = np.asarray(inputs["c_ctx"], f32)
    maps = []
    for b in range(8):
        m = dict(common)
        m["x"] = np.ascontiguousarray(x[b])
        m["ctx"] = np.ascontiguousarray(ctx[b])
        c2 = np.stack([c[b], cc], 0)
        m["cT"] = np.ascontiguousarray(c2.reshape(2, 16, 128).transpose(2, 1, 0))
        maps.append(m)
    return maps


def kernel(**inputs):
    k = build()
    maps = host_inputs(inputs)
    res = run_bass_kernel_spmd(k.nc, maps, core_ids=list(range(8)))
    return np.stack([np.asarray(r["out"], np.float32) for r in res.results], 0)


def tok_groups(ntok):
    g, t0 = [], 0
    while t0 < ntok:
        n = min(512, ntok - t0)
        g.append((t0, n))
        t0 += n
    return g


class Proj:
    def __init__(self, k, es, A, KC, wcols, nslots, nbanks, pfx):
        self.k, self.A, self.KC = k, A, KC
        self.ws = Ring([Slot(k.sb(es, pfx + "_w%d" % i, [128, KC, wcols], BF16), k.dsem()) for i in range(nslots)])
        self.pb = Ring([Slot(k.ps(es, pfx + "_p%d" % i, [128, 512], F32)) for i in range(nbanks)])
        self.items = []

    def add(self, mode, src_fn, run_fn):
        self.items.append((mode, src_fn, run_fn))

    def load(self, idx):
        _, src_fn, _ = self.items[idx]
        s = self.ws.next()
        self.k.pool.wait(s.free_evs())
        pairs = src_fn(s)
        if isinstance(pairs, tuple):
            pairs = [pairs]
        for dst, src in pairs:
            s.ready = self.k.pool.dma(s.dsem, dst, src)
        return s

    def run(self):
        n = len(self.items)
        depth = len(self.ws.slots) - 1
        loaded = []
        for j in range(min(depth, n)):
            loaded.append(self.load(j))
        for idx in range(n):
            if idx + depth < n:
                loaded.append(self.load(idx + depth))
            self.items[idx][2](loaded[idx])
        self.items = []

    def tm_tile(self, slab, width, i, extra_wait=None):
        pe = self.k.pe
        b = self.pb.next()
        pe.wait(slab.ready, b.free_evs(), extra_wait)
        A, KC = self.A, self.KC
        for kk in range(KC):
            ev = pe.do(lambda t, b=b, kk=kk: t.matmul(b.ap[:, 0:width], lhsT=A[:, kk, i * 128:(i + 1) * 128], rhs=slab.ap[:, kk, 0:width],
                                                     start=(kk == 0), stop=(kk == KC - 1)), sig=(kk == KC - 1))
        slab.readers.append(ev)
        return b, ev

    def fm_group(self, slab, c, g0, n, kcs=None, extra_wait=None):
        pe = self.k.pe
        b = self.pb.next()
        pe.wait(slab.ready, b.free_evs(), extra_wait)
        A = self.A
        kcs = list(range(self.KC)) if kcs is None else kcs
        for j, kk in enumerate(kcs):
            ev = pe.do(lambda t, b=b, kk=kk, j=j: t.matmul(b.ap[:, 0:n], lhsT=slab.ap[:, j, c * 128:(c + 1) * 128], rhs=A[:, kk, g0:g0 + n],
                                                           start=(j == 0), stop=(j == len(kcs) - 1)), sig=(j == len(kcs) - 1))
        slab.readers.append(ev)
        return b, ev


def evac_copy(k, eng, out, in_, func=None, scale=None):
    if eng is k.act:
        f = func if func is not None else (AF.Copy if scale is None else AF.Identity)
        if scale is None:
            return eng.do(lambda a: a.activation(out=out, in_=in_, func=f), sig=True)
        return eng.do(lambda a: a.activation(out=out, in_=in_, func=f, scale=scale), sig=True)
    assert func is None
    if scale is None:
        return eng.do(lambda v: v.tensor_copy(out=out, in_=in_), sig=True)
    return eng.do(lambda v: v.tensor_scalar(out=out, in0=in_, scalar1=scale, scalar2=None, op0=ALU.mult), sig=True)


class FMOut:
    def __init__(self, k, es, nslots, pfx):
        self.k = k
        self.ring = Ring([Slot(k.sb(es, pfx + "_fs%d" % i, [128, T], BF16), k.dsem()) for i in range(nslots)])
        self.cur, self.evs, self.tog = None, [], 0

    def evac(self, b, ev, g0, n, first, last, dst, ntok, func=None):
        k = self.k
        if first:
            self.cur = self.ring.next()
            self.evs = []
            self.fe = self.cur.free_evs()
        eng = k.act if (func is not None or self.tog % 2 == 0) else k.dve
        self.tog += 1
        eng.wait(ev, self.fe)
        e2 = evac_copy(k, eng, self.cur.ap[:, g0:g0 + n], b.ap[:, 0:n], func=func)
        b.readers.append(e2)
        self.evs.append(e2)
        if last:
            k.store(k.sp, self.cur, dst[:, 0:ntok], self.cur.ap[:, 0:ntok], self.evs)


class TOut:
    def __init__(self, k, es, ident, nslots, pfx):
        self.k, self.ident = k, ident
        self.ring = Ring([Slot(k.sb(es, pfx + "_ts%d" % i, [128, 4, T], BF16), k.dsem()) for i in range(nslots)])
        self.pt = Ring([Slot(k.ps(es, pfx + "_tp%d" % i, [128, 512], BF16)) for i in range(2)])
        self.tog = 0
        self.q = []
        self.DELAY = 2

    def begin(self):
        self.cur = self.ring.next()
        self.fe = self.cur.free_evs()
        self.evs = []

    def tile(self, o_slot, o_ev, nblk, i):
        self.q.append((o_slot, o_ev, nblk, i))
        while len(self.q) > self.DELAY:
            self._emit(*self.q.pop(0))

    def _emit(self, o_slot, o_ev, nblk, i):
        k, pe, ident = self.k, self.k.pe, self.ident
        p = self.pt.next()
        pe.wait(o_ev, p.free_evs())
        for j in range(nblk):
            evp = pe.do(lambda t, j=j, p=p: t.transpose(out=p.ap[:, j * 128:(j + 1) * 128], in_=o_slot.ap[:, j * 128:(j + 1) * 128], identity=ident[:]), sig=(j == nblk - 1))
        o_slot.readers.append(evp)
        eng = k.act if self.tog % 2 == 0 else k.dve
        self.tog += 1
        eng.wait(evp, self.fe)
        e2 = evac_copy(k, eng, self.cur.ap[:, 0:nblk, i * 128:(i + 1) * 128], p.ap[:, 0:nblk * 128].rearrange("p (a b) -> p a b", b=128))
        p.readers.append(e2)
        self.evs.append(e2)

    def finish(self, dsts, ntok):
        k = self.k
        while self.q:
            self._emit(*self.q.pop(0))
        for j, d in enumerate(dsts):
            k.store(k.sp, self.cur, d[:, 0:ntok], self.cur.ap[:, j, 0:ntok], self.evs)


def tab_bc(tab, i, nh, Dh, sub=None):
    pstep = NLT * Dh
    if sub is None:
        return bass.AP(tab, i * Dh, [[pstep, 128], [0, nh], [1, Dh]])
    b, q = sub
    return bass.AP(tab, i * Dh + b * q, [[pstep, 128], [0, nh], [2 * q, 2], [1, q]])


def rope_emit(k, x_ap, x_ev, nh, Dh, tabC, tabS, i, t1, t2, o_ap, o_free, sc=None):
    dve, pool = k.dve, k.pool
    q = Dh // 4
    W = nh * Dh
    xv = x_ap.rearrange("p (h a b q) -> p h a b q", h=nh, a=2, b=2)
    t1v = t1.ap[:, 0:W].rearrange("p (h d) -> p h d", h=nh)
    t2v = t2.ap[:, 0:W].rearrange("p (h a b q) -> p h a b q", h=nh, a=2, b=2)
    dve.wait(x_ev, t1.free_evs(), t2.free_evs())
    if sc is None:
        dve.do(lambda v: v.tensor_tensor(out=t1v, in0=x_ap.rearrange("p (h d) -> p h d", h=nh), in1=tab_bc(tabC, i, nh, Dh), op=ALU.mult))
        dve.do(lambda v: v.tensor_tensor(out=t2v[:, :, :, 0, :], in0=xv[:, :, :, 1, :], in1=tab_bc(tabS, i, nh, Dh, (0, q)), op=ALU.mult))
        ev = dve.do(lambda v: v.tensor_tensor(out=t2v[:, :, :, 1, :], in0=xv[:, :, :, 0, :], in1=tab_bc(tabS, i, nh, Dh, (1, q)), op=ALU.mult), sig=True)
    else:
        assert nh == 1
        Cv = bass.AP(tabC, i * Dh, [[NLT * Dh, 128], [1, Dh]])
        dve.do(lambda v: v.scalar_tensor_tensor(out=t1.ap[:, 0:W], in0=x_ap, scalar=sc, in1=Cv, op0=ALU.mult, op1=ALU.mult))
        for b in (0, 1):
            Sv = bass.AP(tabS, i * Dh + b * q, [[NLT * Dh, 128], [2 * q, 2], [1, q]])
            ev = dve.do(lambda v, b=b, Sv=Sv: v.scalar_tensor_tensor(out=t2v[:, 0, :, b, :], in0=xv[:, 0, :, 1 - b, :], scalar=sc, in1=Sv, op0=ALU.mult, op1=ALU.mult), sig=(b == 1))
    pool.wait(ev, o_free)
    ev2 = pool.do(lambda g: g.tensor_tensor(out=o_ap, in0=t1.ap[:, 0:W], in1=t2.ap[:, 0:W], op=ALU.add), sig=True)
    t1.readers.append(ev2)
    t2.readers.append(ev2)
    return ev2, [ev]


def phase_win(k, l, hT, ident, ntq, rms_all, rstd_all):
    pe, act, dve, pool, sp = k.pe, k.act, k.dve, k.pool, k.sp
    W = k.dram["w_in"].ap()[l]
    ntok_q = ntq * 128
    with ExitStack() as es:
        pj = Proj(k, es, hT, 16, 512, 2, 4, "win")
        fmo = FMOut(k, es, 2, "win")
        tout = TOut(k, es, ident, 2, "win")
        vst = Ring([Slot(k.sb(es, "win_vst%d" % i, [128, 512], BF16), k.dsem()) for i in range(2)])
        t1r = Ring([Slot(k.sb(es, "win_t1%d" % i, [128, 512], F32)) for i in range(2)])
        t2r = Ring([Slot(k.sb(es, "win_t2%d" % i, [128, 512], F32)) for i in range(2)])
        xgr = Ring([Slot(k.sb(es, "win_xg%d" % i, [128, 512], F32)) for i in range(3)])
        orr = Ring([Slot(k.sb(es, "win_o%d" % i, [128, 512], BF16)) for i in range(4)])
        smr = Ring([Slot(k.sb(es, "win_sm%d" % i, [128, 16], F32)) for i in range(4)])
        junk = k.sb(es, "win_junk", [128, 512], BF16)
        rM = [k.sb(es, "win_rM%d" % j, [128, NLT, 64], F32) for j in range(2)]
        rG = [k.sb(es, "win_rG%d" % j, [128, NLT, 128], F32) for j in range(2)]
        g512 = k.sb(es, "win_g512", [128, 512], F32)
        gqk = [k.sb(es, "win_gqk%d" % j, [128, 128], F32) for j in range(2)]
        d0 = k.dsem()
        for j in range(2):
            sp.dma(d0, rM[j][:], k.dram["ropeM"].ap()[j])
            sp.dma(d0, rG[j][:], k.dram["ropeG"].ap()[j])
        sp.dma(d0, g512[:], bc_ap(k.dram["mla_kv_norm"], l * 512, 512))
        sp.dma(d0, gqk[0][:], bc_ap(k.dram["gqa_q_norm"], l * 128, 128))
        ev_tab = sp.dma(d0, gqk[1][:], bc_ap(k.dram["gqa_k_norm"], l * 128, 128))
        for e in (act, dve, pool):
            e.wait(ev_tab)

        def src2d(c0, w):
            return lambda s: (s.ap[:, :, 0:w], W[:, c0:c0 + w].rearrange("(k p) n -> p k n", p=128))

        def src_mq(h0, nh, d0_, dw):
            v = W[:, 0:3072].rearrange("(k p) (h d) -> p k h d", p=128, d=192)
            return lambda s: [(s.ap[:, :, hh * dw:(hh + 1) * dw], v[:, :, h0 + hh, d0_:d0_ + dw]) for hh in range(nh)]

        def fm_run(nchunks, dsts, ntok, func=None):
            groups = tok_groups(ntok)

            def run(slab):
                for c in range(nchunks):
                    for gi, (g0, n) in enumerate(groups):
                        b, ev = pj.fm_group(slab, c, g0, n)
                        fmo.evac(b, ev, g0, n, gi == 0, gi == len(groups) - 1, dsts[c], ntok, func=func)
            return run

        def v_run(dst, width, ntiles):
            def run(slab):
                for i in range(ntiles):
                    b, ev = pj.tm_tile(slab, width, i)
                    st = vst.next()
                    act.wait(ev, st.free_evs())
                    e2 = evac_copy(k, act, st.ap[:, 0:width], b.ap[:, 0:width])
                    b.readers.append(e2)
                    k.store(sp, st, dst[i * 128:(i + 1) * 128, :], st.ap[:, 0:width], e2)
            return run

        def mqrope_run(h0):
            def run(slab):
                tout.begin()
                for i in range(ntq):
                    b, ev = pj.tm_tile(slab, 512, i)
                    o = orr.next()
                    if i < NLT:
                        e2, xr = rope_emit(k, b.ap[:, 0:512], ev, 8, 64, rM[0], rM[1], i, t1r.next(), t2r.next(), o.ap[:, 0:512], o.free_evs())
                        b.readers.extend(xr)
                    else:
                        act.wait(ev, o.free_evs())
                        e2 = evac_copy(k, act, o.ap[:, 0:512], b.ap[:, 0:512])
                        b.readers.append(e2)
                    tout.tile(o, e2, 4, i)
                tout.finish([k.dram["qa_rT"].ap()[h0 // 2 + j] for j in range(4)], ntok_q)
            return run

        def ckv_run(slab):
            tout.begin()
            for i in range(NT):
                b, ev = pj.tm_tile(slab, 512, i)
                sm = smr.next()
                act.wait(ev, sm.free_evs())
                e1 = act.do(lambda a, b=b, sm=sm: a.activation(out=junk[:], in_=b.ap[:, 0:512], func=AF.Square, accum_out=sm.ap[:, 0:1]), sig=True)
                act.wait(e1)
                e1 = act.do(lambda a, sm=sm, i=i: a.activation(out=rms_all[:, i:i + 1], in_=sm.ap[:, 0:1], func=AF.Sqrt, bias=RMS_EPS, scale=1.0 / 512), sig=True)
                sm.readers.append(e1)
                dve.wait(e1)
                e3 = dve.do(lambda v, i=i: v.reciprocal(out=rstd_all[:, i:i + 1], in_=rms_all[:, i:i + 1]), sig=True)
                o = orr.next()
                dve.wait(o.free_evs())
                e2 = dve.do(lambda v, b=b, o=o: v.tensor_tensor(out=o.ap[:, 0:512], in0=b.ap[:, 0:512], in1=g512[:], op=ALU.mult), sig=True)
                b.readers.extend([e1, e2])
                tout.tile(o, e2, 4, i)
            k.ev_rstd = e3
            tout.finish([k.dram["cgT"].ap()[j * 128:(j + 1) * 128, :] for j in range(4)], T)

        def kr_run(slab):
            tout.begin()
            for i in range(NT):
                b, ev = pj.tm_tile(slab, 64, i)
                o = orr.next()
                sc = rms_all[:, i:i + 1]
                dve.wait(k.ev_rstd)
                if i < NLT:
                    e2, xr = rope_emit(k, b.ap[:, 0:64], ev, 1, 64, rM[0], rM[1], i, t1r.next(), t2r.next(), o.ap[:, 0:64], o.free_evs(), sc=sc)
                    b.readers.extend(xr)
                    pool.wait(e2)
                    e2 = pool.do(lambda g, o=o: g.tensor_copy(out=o.ap[:, 64:128], in_=o.ap[:, 0:64]), sig=True)
                else:
                    dve.wait(ev, o.free_evs())
                    dve.do(lambda v, b=b, o=o, sc=sc: v.tensor_scalar(out=o.ap[:, 0:64], in0=b.ap[:, 0:64], scalar1=sc, scalar2=None, op0=ALU.mult))
                    e2 = dve.do(lambda v, b=b, o=o, sc=sc: v.tensor_scalar(out=o.ap[:, 64:128], in0=b.ap[:, 0:64], scalar1=sc, scalar2=None, op0=ALU.mult), sig=True)
                    b.readers.append(e2)
                tout.tile(o, e2, 1, i)
            tout.finish([k.dram["ka_rT"].ap()], T)

        def gqa_run(gvec, dsts, ntiles):
            def run(slab):
                tout.begin()
                for i in range(ntiles):
                    b, ev = pj.tm_tile(slab, 512, i)
                    sm = smr.next()
                    act.wait(ev, sm.free_evs())
                    for hh in range(4):
                        e1 = act.do(lambda a, b=b, sm=sm, hh=hh: a.activation(out=junk[:, 0:128], in_=b.ap[:, hh * 128:(hh + 1) * 128], func=AF.Square, accum_out=sm.ap[:, hh:hh + 1]), sig=(hh == 3))
                    act.wait(e1)
                    e1 = act.do(lambda a, sm=sm: a.activation(out=sm.ap[:, 4:8], in_=sm.ap[:, 0:4], func=AF.Sqrt, bias=RMS_EPS, scale=1.0 / 128), sig=True)
                    dve.wait(e1)
                    e3 = dve.do(lambda v, sm=sm: v.reciprocal(out=sm.ap[:, 8:12], in_=sm.ap[:, 4:8]), sig=True)
                    xg = xgr.next()
                    dve.wait(e3, xg.free_evs())
                    for hh in range(4):
                        e4 = dve.do(lambda v, b=b, sm=sm, hh=hh, xg=xg: v.scalar_tensor_tensor(out=xg.ap[:, hh * 128:(hh + 1) * 128], in0=b.ap[:, hh * 128:(hh + 1) * 128],
                                                                                              scalar=sm.ap[:, 8 + hh:9 + hh], in1=gvec[:], op0=ALU.mult, op1=ALU.mult), sig=(hh == 3))
                    b.readers.extend([e1, e4])
                    sm.readers.append(e4)
                    o = orr.next()
                    if i < NLT:
                        e2, xr = rope_emit(k, xg.ap[:, 0:512], e4, 4, 128, rG[0], rG[1], i, t1r.next(), t2r.next(), o.ap[:, 0:512], o.free_evs())
                        xg.readers.extend(xr)
                    else:
                        pool.wait(e4, o.free_evs())
                        e2 = pool.do(lambda g, xg=xg, o=o: g.tensor_copy(out=o.ap[:, 0:512], in_=xg.ap[:, 0:512]), sig=True)
                        xg.readers.append(e2)
                    tout.tile(o, e2, 4, i)
                tout.finish(dsts, ntiles * 128)
            return run

        D_ = k.dram
        pj.add("tm", src2d(O_CKV, 512), ckv_run)
        pj.add("tm", src2d(O_KR, 64), kr_run)
        for h0 in (0, 8):
            pj.add("tm", src_mq(h0, 8, 128, 64), mqrope_run(h0))
        for h0 in range(0, H, 4):
            pj.add("fm", src_mq(h0, 4, 0, 128), fm_run(4, [D_["qa_nT"].ap()[h0 + j] for j in range(4)], ntok_q))
        for h0 in range(0, H, 4):
            pj.add("fm", src2d(O_NA + h0 * 128, 512), fm_run(4, [D_["qbT"].ap()[h0 + j] for j in range(4)], ntok_q))
        for h0 in range(0, H, 4):
            pj.add("fm", src2d(O_NA + 2048 + h0 * 128, 512), fm_run(4, [D_["kbT"].ap()[h0 + j] for j in range(4)], T))
        for c0 in range(0, 2048, 512):
            pj.add("tm", src2d(O_NA + 4096 + c0, 512), v_run(D_["vb"].ap()[:, c0:c0 + 512], 512, NT))
        for h0 in range(0, H, 4):
            pj.add("tm", src2d(O_GQ + h0 * 128, 512), gqa_run(gqk[0], [D_["qcT"].ap()[h0 + j] for j in range(4)], ntq))
        pj.add("tm", src2d(O_GKV, 512), gqa_run(gqk[1], [D_["kcT"].ap()[j] for j in range(4)], NT))
        pj.add("tm", src2d(O_GKV + 512, 512), v_run(D_["vc"].ap(), 512, NT))
        for br in range(3):
            for c0 in range(0, 2048, 512):
                pj.add("fm", src2d(O_GATE + br * 2048 + c0, 512),
                       fm_run(4, [D_["gateT"].ap()[br, c0 + j * 128:c0 + (j + 1) * 128, :] for j in range(4)], ntok_q, func=AF.Sigmoid))
        pj.run()
        if "rstd_dbg" in k.dbg:
            st = Slot(None, k.dsem())
            k.store(sp, st, k.dram["rstd_dbg"].ap(), rstd_all[:], k.ev_rstd)
        k.flush()


def phase_ukv(k, l, rstd_all):
    pe, act, dve, pool, sp = k.pe, k.act, k.dve, k.pool, k.sp
    with ExitStack() as es:
        A2 = k.sb(es, "ukv_A", [128, 4, T], BF16)
        Wu = k.sb(es, "ukv_W", [128, 4, 4096], BF16)
        pb = Ring([Slot(k.ps(es, "ukv_p%d" % i, [128, 512], F32)) for i in range(4)])
        fmo = FMOut(k, es, 2, "ukv")
        vst = Ring([Slot(k.sb(es, "ukv_vst%d" % i, [128, 512], BF16), k.dsem()) for i in range(2)])
        d0, d1 = k.dsem(), k.dsem()
        evA = sp.dma(d0, A2[:], k.dram["cgT"].ap().rearrange("(k p) t -> p k t", p=128))
        evW = pool.dma(d1, Wu[:], k.dram["w_mla_ukv"].ap()[l].rearrange("(k p) n -> p k n", p=128))
        groups = tok_groups(T)
        for h in range(H):
            for gi, (g0, n) in enumerate(groups):
                b = pb.next()
                pe.wait(evA, evW, b.free_evs())
                for kk in range(4):
                    ev = pe.do(lambda t, b=b, kk=kk, h=h, g0=g0, n=n: t.matmul(b.ap[:, 0:n], lhsT=Wu[:, kk, h * 256:h * 256 + 128], rhs=A2[:, kk, g0:g0 + n],
                                                                              start=(kk == 0), stop=(kk == 3)), sig=(kk == 3))
                fmo.evac(b, ev, g0, n, gi == 0, gi == len(groups) - 1, k.dram["ka_nT"].ap()[h], T)
        for i in range(NT):
            for hg in range(4):
                b = pb.next()
                pe.wait(b.free_evs())
                for hh in range(4):
                    c0 = (hg * 4 + hh) * 256 + 128
                    for kk in range(4):
                        ev = pe.do(lambda t, b=b, kk=kk, hh=hh, c0=c0, i=i: t.matmul(b.ap[:, hh * 128:(hh + 1) * 128], lhsT=A2[:, kk, i * 128:(i + 1) * 128], rhs=Wu[:, kk, c0:c0 + 128],
                                                                                     start=(kk == 0), stop=(kk == 3)), sig=(kk == 3 and hh == 3))
                st = vst.next()
                act.wait(ev, st.free_evs())
                e2 = evac_copy(k, act, st.ap[:, :], b.ap[:, :], scale=rstd_all[:, i:i + 1])
                b.readers.append(e2)
                k.store(sp, st, k.dram["va"].ap()[i * 128:(i + 1) * 128, hg * 512:(hg + 1) * 512], st.ap[:, :], e2)
        k.flush()


def na_window(j):
    out = []
    for i in range(NLT):
        inv, anyv = [], False
        for a in range(2):
            for b in range(2):
                r = 2 * j + b
                rs = min(max(r - 4, 0), 24)
                if rs <= 2 * i + a < rs + 8:
                    anyv = True
                else:
                    inv.append((a, b))
        if anyv:
            out.append((i, 7 - 2 * (i - j), inv))
    return out


def phase_attn(k, l, br, ident, ntq, rstd_all):
    pe, act, dve, pool, sp = k.pe, k.act, k.dve, k.pool, k.sp
    D_ = k.dram
    ntok_q = ntq * 128
    mla, na = br == 0, br == 1
    with ExitStack() as es:
        slots = []
        for i in range(2):
            s = Slot(None, k.dsem())
            s.QN = k.sb(es, "at_qn%d" % i, [128, T], BF16)
            s.KN = k.sb(es, "at_kn%d" % i, [128, T], BF16)
            s.V = k.sb(es, "at_v%d" % i, [128, NT, 129], BF16)
            if mla:
                s.QR = k.sb(es, "at_qr%d" % i, [128, T], BF16)
            if na:
                s.EEraw = k.sb(es, "at_eer%d" % i, [128, 1024], F32)
                s.EE = k.sb(es, "at_ee%d" % i, [128, 1024], F32)
            slots.append(s)
        ev_init = None
        for s in slots:
            ev_init = pool.do(lambda g, s=s: g.memset(s.V[:, :, 128:129], 1.0), sig=True)
        sp.wait(ev_init)
        hring = Ring(slots)
        ev_kr = None
        if mla:
            KRb = k.sb(es, "at_krb", [128, T], BF16)
            sc_all = k.sb(es, "at_sc", [128, NT], F32)
            dk = k.dsem()
            ev_kr = sp.dma(dk, KRb[:], D_["ka_rT"].ap())
            ev_sc = dve.do(lambda v: v.tensor_scalar(out=sc_all[:], in0=rstd_all[:], scalar1=MLA_SCALE, scalar2=None, op0=ALU.mult), sig=True)
            act.wait(ev_sc)
        la = 3 if na else 2
        Sb = Ring([Slot(k.ps(es, "at_s%d" % i, [128, 512], F32)) for i in range(la + 1)])
        Yb = Ring([Slot([k.ps(es, "at_y%d_%d" % (i, j), [128, 512], F32) for j in range(1 if na else 2)]) for i in range(2)])
        Tb = Ring([Slot(k.ps(es, "at_t%d" % i, [128, 512], BF16)) for i in range(2 if na else 1)])
        PT = Ring([Slot(k.sb(es, "at_pt%d" % i, [128, 512], BF16)) for i in range(la + 2)])
        PR = Ring([Slot(k.sb(es, "at_pr%d" % i, [128, 128], F32)) for i in range(la + 1)]) if na else None
        Yn = Ring([Slot(k.sb(es, "at_yn%d" % i, [128, 4, 128], BF16)) for i in range(2)])
        Rc = Ring([Slot(k.sb(es, "at_rc%d" % i, [128, 4], F32)) for i in range(2)])
        YT = Ring([Slot(k.sb(es, "at_yt%d" % i, [128, T], BF16), k.dsem()) for i in range(2)])

        def load_head(h):
            s = hring.next()
            sp.wait(s.free_evs())
            if br == 0:
                base = (h % 2) * 64
                sp.dma(s.dsem, s.QN[:, 0:ntok_q], D_["qa_nT"].ap()[h][:, 0:ntok_q])
                sp.dma(s.dsem, s.QR[base:base + 64, 0:ntok_q], D_["qa_rT"].ap()[h // 2][base:base + 64, 0:ntok_q])
                sp.dma(s.dsem, s.KN[:], D_["ka_nT"].ap()[h])
                vsrc = D_["va"].ap()[:, h * 128:(h + 1) * 128]
            elif br == 1:
                sp.dma(s.dsem, s.QN[:, 0:ntok_q], D_["qbT"].ap()[h][:, 0:ntok_q])
                sp.dma(s.dsem, s.KN[:], D_["kbT"].ap()[h])
                sp.dma(s.dsem, s.EEraw[:], D_["na_bias"].ap()[l, h])
                vsrc = D_["vb"].ap()[:, h * 128:(h + 1) * 128]
            else:
                sp.dma(s.dsem, s.QN[:, 0:ntok_q], D_["qcT"].ap()[h][:, 0:ntok_q])
                sp.dma(s.dsem, s.KN[:], D_["kcT"].ap()[h // 4])
                vsrc = D_["vc"].ap()[:, (h // 4) * 128:(h // 4 + 1) * 128]
            s.ready = sp.dma(s.dsem, s.V[:, :, 0:128], vsrc.rearrange("(i p) e -> p i e", p=128))
            s.ee_ev = None
            if na:
                act.wait(s.ready)
                s.ee_ev = act.do(lambda a, s=s: a.activation(out=s.EE[:], in_=s.EEraw[:], func=AF.Exp), sig=True)
            return s

        def head_groups():
            gs = []
            if na:
                for j in range(NLT):
                    kts = [(i, t0, inv) for (i, t0, inv) in na_window(j)] + [(16, None, None), (17, None, None)]
                    gs.append((j * 128, 128, kts))
            else:
                for g in range(4):
                    gs.append((g * 512, 512, [(i, None, None) for i in range(NT)]))
            if ntq == NT:
                gs.append((S, 256, [(16, None, None), (17, None, None)]))
            return gs

        steps = []
        for h in range(H):
            for gidx, (q0, n, kts) in enumerate(head_groups()):
                for ki, (kt, t0, inv) in enumerate(kts):
                    steps.append(dict(h=h, g=gidx, q0=q0, n=n, kt=kt, t0=t0, inv=inv, first=(ki == 0), last=(ki == len(kts) - 1),
                                      lastg=(gidx == len(head_groups()) - 1)))
        hs = {}
        state = dict(Y=None, yt=None)
        scale_c = NA_SCALE if na else GQA_SCALE

        def emit_qk(st):
            h = st["h"]
            if h not in hs:
                hs[h] = load_head(h)
            s = hs[h]
            sb_ = Sb.next()
            st["S"] = sb_
            q0, n, kt = st["q0"], st["n"], st["kt"]
            pe.wait(s.ready, ev_kr, sb_.free_evs())
            if mla:
                base = (h % 2) * 64
                pe.do(lambda t: t.matmul(sb_.ap[:, 0:n], lhsT=s.KN[:, kt * 128:(kt + 1) * 128], rhs=s.QN[:, q0:q0 + n], start=True, stop=False))
                st["qk_ev"] = pe.do(lambda t: t.matmul(sb_.ap[:, 0:n], lhsT=KRb[base:base + 64, kt * 128:(kt + 1) * 128], rhs=s.QR[base:base + 64, q0:q0 + n], start=False, stop=True), sig=True)
            else:
                st["qk_ev"] = pe.do(lambda t: t.matmul(sb_.ap[:, 0:n], lhsT=s.KN[:, kt * 128:(kt + 1) * 128], rhs=s.QN[:, q0:q0 + n], start=True, stop=True), sig=True)

        def emit_exp(st):
            s, sb_, n, kt = hs[st["h"]], st["S"], st["n"], st["kt"]
            pt = PT.next()
            st["PT"] = pt
            if st["t0"] is None:
                act.wait(st["qk_ev"], pt.free_evs())
                sc = sc_all[:, kt:kt + 1] if mla else scale_c
                e = act.do(lambda a: a.activation(out=pt.ap[:, 0:n], in_=sb_.ap[:, 0:n], func=AF.Exp, scale=sc), sig=True)
                sb_.readers.append(e)
                st["pt_ev"] = e
            else:
                pr = PR.next()
                act.wait(st["qk_ev"], pr.free_evs())
                e = act.do(lambda a: a.activation(out=pr.ap[:, :], in_=sb_.ap[:, 0:128], func=AF.Exp, scale=scale_c), sig=True)
                sb_.readers.append(e)
                t0 = st["t0"]
                dve.wait(e, s.ee_ev, pt.free_evs())
                e2 = dve.do(lambda v: v.tensor_tensor(out=pt.ap[:, 0:128], in0=pr.ap[:, :], in1=s.EE[:, t0 * 64:(t0 + 2) * 64], op=ALU.mult), sig=True)
                pr.readers.append(e2)
                if st["inv"]:
                    dve.wait(e2)
                    for (a, b) in st["inv"]:
                        e2 = dve.do(lambda v, a=a, b=b: v.memset(pt.ap[a * 64:(a + 1) * 64, b * 64:(b + 1) * 64], 0.0), sig=True)
                st["pt_ev"] = e2

        def emit_pv(st):
            s, n, kt, pt = hs[st["h"]], st["n"], st["kt"], st["PT"]
            if st["first"]:
                y = Yb.next()
                state["Y"] = y
                pe.wait(y.free_evs())
            y = state["Y"]
            pe.wait(st["pt_ev"])
            nqs = n // 128
            for qs in range(nqs):
                yap = y.ap[qs // 2][:, (qs % 2) * 256:(qs % 2) * 256 + 129]
                ev = pe.do(lambda t, qs=qs, yap=yap: t.matmul(yap, lhsT=pt.ap[:, qs * 128:(qs + 1) * 128], rhs=s.V[:, kt, :], start=(st["first"] and qs % 2 == 0), stop=st["last"]), sig=(qs == nqs - 1))
            pt.readers.append(ev)
            st["pv_ev"] = ev
            if st["lastg"] and st["last"]:
                s.readers.append(ev)

        def epilogue(st):
            y, n, q0, h = state["Y"], st["n"], st["q0"], st["h"]
            nqs = n // 128
            rc, yn = Rc.next(), Yn.next()
            dve.wait(st["pv_ev"], rc.free_evs(), yn.free_evs())
            for qs in range(nqs):
                e = dve.do(lambda v, qs=qs: v.reciprocal(out=rc.ap[:, qs:qs + 1], in_=y.ap[qs // 2][:, (qs % 2) * 256 + 128:(qs % 2) * 256 + 129]), sig=(qs == nqs - 1))
            dve.wait(e)
            for qs in range(nqs):
                e = dve.do(lambda v, qs=qs: v.tensor_scalar(out=yn.ap[:, qs, :], in0=y.ap[qs // 2][:, (qs % 2) * 256:(qs % 2) * 256 + 128], scalar1=rc.ap[:, qs:qs + 1], scalar2=None, op0=ALU.mult), sig=(qs == nqs - 1))
            y.readers.append(e)
            rc.readers.append(e)
            if st["g"] == 0:
                state["yt"] = YT.next()
                state["yt_fe"] = state["yt"].free_evs()
                state["yt_evs"] = []
            yt, yt_fe, yt_evs = state["yt"], state["yt_fe"], state["yt_evs"]
            lastg = st["lastg"]

            def pe_part():
                tb = Tb.next()
                pe.wait(e, tb.free_evs())
                for qs in range(nqs):
                    ep = pe.do(lambda t, qs=qs: t.transpose(out=tb.ap[:, qs * 128:(qs + 1) * 128], in_=yn.ap[:, qs, :], identity=ident[:]), sig=(qs == nqs - 1))
                yn.readers.append(ep)
                dve.wait(ep, yt_fe)
                ec = dve.do(lambda v: v.tensor_copy(out=yt.ap[:, q0:q0 + n], in_=tb.ap[:, 0:n]), sig=True)
                tb.readers.append(ec)
                yt_evs.append(ec)
                if lastg:
                    k.store(sp, yt, D_["yT"].ap()[br, h][:, 0:ntok_q], yt.ap[:, 0:ntok_q], list(yt_evs))
            return pe_part

        deferred = []
        for si in range(min(la, len(steps))):
            emit_qk(steps[si])
        for si, st in enumerate(steps):
            emit_exp(st)
            if si + la < len(steps):
                emit_qk(steps[si + la])
            emit_pv(st)
            deferred = [(c - 1, f) for (c, f) in deferred]
            for c, f in [d for d in deferred if d[0] <= 0]:
                f()
            deferred = [d for d in deferred if d[0] > 0]
            if st["last"]:
                deferred.append((la + 1, epilogue(st)))
            if st["first"] and st["g"] == 0 and st["h"] + 1 < H and (st["h"] + 1) not in hs:
                hs[st["h"] + 1] = load_head(st["h"] + 1)
        for c, f in deferred:
            f()
        k.flush()


def phase_merge(k, l, ntq):
    pe, act, dve, pool, sp = k.pe, k.act, k.dve, k.pool, k.sp
    D_ = k.dram
    ngt = ntq // 2
    ntg = ngt * 128
    WB = D_["w_branch"].ap()[l]
    with ExitStack() as es:
        A3 = k.sb(es, "mg_A", [128, 48, ntg], BF16)
        ws = Ring([Slot(k.sb(es, "mg_w%d" % i, [128, 3, 16, 256], BF16), k.dsem()) for i in range(2)])
        gs = Ring([Slot(k.sb(es, "mg_g%d" % i, [128, 3, ntg], BF16), k.dsem()) for i in range(2)])
        accs = Ring([Slot(k.sb(es, "mg_acc%d" % i, [128, ntg], F32)) for i in range(2)])
        tmps = Ring([Slot(k.sb(es, "mg_tmp%d" % i, [128, 512], F32)) for i in range(3)])
        outs = Ring([Slot(k.sb(es, "mg_o%d" % i, [128, ntg], BF16), k.dsem()) for i in range(2)])
        pb = Ring([Slot(k.ps(es, "mg_p%d" % i, [128, 512], F32)) for i in range(6)])
        dA = k.dsem()
        a_readers = []
        groups = tok_groups(ntg)

        def wload(cs):
            s = ws.next()
            pool.wait(s.free_evs())
            for i in range(3):
                s.ready = pool.dma(s.dsem, s.ap[:, i, :, :], WB[i][:, cs * 256:(cs + 1) * 256].rearrange("(k p) n -> p k n", p=128))
            return s

        for tg in range(2):
            tok0 = tg * ntg
            sp.wait(a_readers)
            a_readers = []
            for i in range(3):
                evA = sp.dma(dA, A3[:, i * 16:(i + 1) * 16, :], D_["yT"].ap()[i][:, :, tok0:tok0 + ntg].rearrange("h p t -> p h t"))
            nxt = wload(0)
            for cs in range(8):
                cur = nxt
                if cs + 1 < 8:
                    nxt = wload(cs + 1)
                for cc in range(2):
                    c = cs * 2 + cc
                    g = gs.next()
                    sp.wait(g.free_evs())
                    g.ready = sp.dma(g.dsem, g.ap[:], D_["gateT"].ap()[:, c * 128:(c + 1) * 128, tok0:tok0 + ntg].rearrange("i p t -> p i t"))
                    acc, o = accs.next(), outs.next()
                    acc_fe, o_fe = acc.free_evs(), o.free_evs()
                    last = {}
                    oevs = []
                    for i in range(3):
                        for (g0, n) in groups:
                            b = pb.next()
                            pe.wait(cur.ready, evA, b.free_evs())
                            for hc in range(16):
                                ev = pe.do(lambda t, b=b, i=i, hc=hc, cc=cc, g0=g0, n=n, cur=cur: t.matmul(b.ap[:, 0:n], lhsT=cur.ap[:, i, hc, cc * 128:(cc + 1) * 128], rhs=A3[:, i * 16 + hc, g0:g0 + n],
                                                                                                          start=(hc == 0), stop=(hc == 15)), sig=(hc == 15))
                            cur.readers.append(ev)
                            a_readers.append(ev)
                            if i == 0:
                                dve.wait(ev, g.ready, acc_fe)
                                e = dve.do(lambda v, b=b, g=g, acc=acc, g0=g0, n=n: v.tensor_tensor(out=acc.ap[:, g0:g0 + n], in0=b.ap[:, 0:n], in1=g.ap[:, 0, g0:g0 + n], op=ALU.mult), sig=True)
                                b.readers.append(e)
                                last[g0] = e
                            else:
                                tmp = tmps.next()
                                dve.wait(ev, g.ready, tmp.free_evs())
                                e = dve.do(lambda v, b=b, g=g, tmp=tmp, i=i, g0=g0, n=n: v.tensor_tensor(out=tmp.ap[:, 0:n], in0=b.ap[:, 0:n], in1=g.ap[:, i, g0:g0 + n], op=ALU.mult), sig=True)
                                b.readers.append(e)
                                pool.wait(e, last[g0])
                                if i == 1:
                                    e2 = pool.do(lambda p_, acc=acc, tmp=tmp, g0=g0, n=n: p_.tensor_tensor(out=acc.ap[:, g0:g0 + n], in0=acc.ap[:, g0:g0 + n], in1=tmp.ap[:, 0:n], op=ALU.add), sig=True)
                                    last[g0] = e2
                                else:
                                    pool.wait(o_fe)
                                    e2 = pool.do(lambda p_, acc=acc, tmp=tmp, o=o, g0=g0, n=n: p_.tensor_tensor(out=o.ap[:, g0:g0 + n], in0=acc.ap[:, g0:g0 + n], in1=tmp.ap[:, 0:n], op=ALU.add), sig=True)
                                    oevs.append(e2)
                                    acc.readers.append(e2)
                                tmp.readers.append(e2)
                    g.readers.append(e)
                    k.store(sp, o, D_["accT"].ap()[c * 128:(c + 1) * 128, tok0:tok0 + ntg], o.ap[:, :], oevs)
        k.flush()


def phase_tm_out(k, l, which, ntq):
    pe, act, dve, pool, sp = k.pe, k.act, k.dve, k.pool, k.sp
    D_ = k.dram
    with ExitStack() as es:
        if which == "out":
            KC, wc, Wd, ngroups, ngt = 16, 512, D_["w_out"].ap()[l], 1, ntq
            Asrc = D_["accT"].ap()
        else:
            KC, wc, Wd, ngroups, ngt = NFC, 256, D_["w_down"].ap()[l], 2, ntq // 2
            Asrc = D_["hidT"].ap()
        ntg = ngt * 128
        A = k.sb(es, "to_A", [128, KC, ntg], BF16)
        pj = Proj(k, es, A, KC, wc, 2, 4, "to")
        stg = Ring([Slot(k.sb(es, "to_st%d" % i, [128, 512], F32), k.dsem()) for i in range(3)])
        dA = k.dsem()
        tog = [0]
        a_readers = []
        for tg in range(ngroups):
            tok0 = tg * ntg
            sp.wait(a_readers)
            evA = sp.dma(dA, A[:], Asrc[:, tok0:tok0 + ntg].rearrange("(k p) t -> p k t", p=128))

            def mk_run(c0, evA=evA, tok0=tok0):
                def run(slab):
                    for i in range(ngt):
                        b, ev = pj.tm_tile(slab, wc, i, extra_wait=evA)
                        a_readers.append(ev)
                        st = stg.next()
                        eng = act if tog[0] % 2 == 0 else dve
                        tog[0] += 1
                        eng.wait(ev, st.free_evs())
                        e2 = evac_copy(k, eng, st.ap[:, 0:wc], b.ap[:, 0:wc])
                        b.readers.append(e2)
                        k.store(sp, st, D_["fout"].ap()[tok0 + i * 128:tok0 + (i + 1) * 128, c0:c0 + wc], st.ap[:, 0:wc], e2)
                return run

            for c0 in range(0, D, wc):
                pj.add("tm", (lambda s, c0=c0: (s.ap[:, :, 0:wc], Wd[:, c0:c0 + wc].rearrange("(k p) n -> p k n", p=128))), mk_run(c0))
            pj.run()
        k.flush()


def phase_resid(k, l, sub, src, ntiles, final):
    pe, act, dve, pool, sp = k.pe, k.act, k.dve, k.pool, k.sp
    D_ = k.dram
    mv_t = D_["mvec"]
    with ExitStack() as es:
        d0 = k.dsem()
        gate = {}
        for j, nm in ((0, "lat"), (1, "ctx")):
            gate[j] = k.sb(es, "rs_gate" + nm, [128, 2048], F32)
            sp.dma(d0, gate[j][:], bc_ap(mv_t, (l * 2 + j) * 12288 + (2 + 3 * sub) * 2048, 2048))
        lng = k.sb(es, "rs_lng", [128, 2048], F32)
        lnb = k.sb(es, "rs_lnb", [128, 2048], F32)
        sp.dma(d0, lng[:], bc_ap(D_["ln_f_g" if sub else "ln_a_g"], l * 2048, 2048))
        evc = sp.dma(d0, lnb[:], bc_ap(D_["ln_f_b" if sub else "ln_a_b"], l * 2048, 2048))
        xr = Ring([Slot(k.sb(es, "rs_x%d" % i, [128, 2048], F32), k.dsem()) for i in range(2)])
        fr = Ring([Slot(k.sb(es, "rs_f%d" % i, [128, 2048], F32), k.dsem()) for i in range(2)])
        orr = Ring([Slot(k.sb(es, "rs_o%d" % i, [128, 2048], F32), k.dsem()) for i in range(2)])
        sm = Ring([Slot(k.sb(es, "rs_sm%d" % i, [128, 32], F32)) for i in range(2)])
        fout = D_["fout"].ap()

        def load(i):
            xs, fs = xr.next(), fr.next()
            sp.wait(xs.free_evs(), fs.free_evs())
            xs.ready = sp.dma(xs.dsem, xs.ap[:], src(i))
            fs.ready = sp.dma(fs.dsem, fs.ap[:], fout[i * 128:(i + 1) * 128, :])
            return xs, fs

        nxt = load(0)
        for i in range(ntiles):
            xs, fs = nxt
            if i + 1 < ntiles:
                nxt = load(i + 1)
            gt = gate[0 if i < NLT else 1]
            dve.wait(xs.ready, fs.ready, evc)
            e = dve.do(lambda v, fs=fs, gt=gt: v.tensor_tensor(out=fs.ap[:], in0=fs.ap[:], in1=gt[:], op=ALU.mult), sig=True)
            dve.wait(e)
            e = dve.do(lambda v, xs=xs, fs=fs: v.scalar_tensor_tensor(out=xs.ap[:], in0=xs.ap[:], scalar=ALPHA, in1=fs.ap[:], op0=ALU.mult, op1=ALU.add), sig=True)
            smt = sm.next()
            st = smt.ap
            stats = st[:, 0:24].rearrange("p (c s) -> p c s", s=6)
            mv, rstd, nmr = st[:, 24:26], st[:, 26:27], st[:, 27:28]
            dve.wait(e, smt.free_evs())
            e = ln_stats(k, xs.ap, mv, rstd, nmr, POST_EPS, stats)
            act.wait(e)
            e = act.do(lambda a, xs=xs, fs=fs, rstd=rstd, nmr=nmr: a.activation(out=fs.ap[:], in_=xs.ap[:], func=AF.Identity, bias=nmr, scale=rstd), sig=True)
            xs.readers.append(e)
            smt.readers.append(e)
            dve.wait(e)
            e = dve.do(lambda v, fs=fs: v.tensor_tensor(out=fs.ap[:], in0=fs.ap[:], in1=lng[:], op=ALU.mult), sig=True)
            os_ = orr.next()
            pool.wait(e, os_.free_evs())
            e = pool.do(lambda g, fs=fs, os_=os_: g.tensor_tensor(out=os_.ap[:], in0=fs.ap[:], in1=lnb[:], op=ALU.add), sig=True)
            fs.readers.append(e)
            if final:
                if i < NLT:
                    k.store(sp, os_, k.out.ap()[i * 128:(i + 1) * 128, :], os_.ap[:], e)
            else:
                k.store(sp, os_, D_["xres"].ap()[i * 128:(i + 1) * 128, :], os_.ap[:], e)
        k.flush()


def phase_ffn_up(k, l, hT, ntq):
    pe, act, dve, pool, sp = k.pe, k.act, k.dve, k.pool, k.sp
    D_ = k.dram
    Wu = D_["w_up"].ap()[l]
    ntok = ntq * 128
    has_ctx = ntq == NT
    GW = T + 4
    with ExitStack() as es:
        pj = Proj(k, es, hT, 16, 512, 2, 4, "fu")
        Gb = Ring([Slot(k.sb(es, "fu_g%d" % i, [128, GW], F32)) for i in range(2)])
        Vb = Ring([Slot(k.sb(es, "fu_v%d" % i, [128, T], F32)) for i in range(2)])
        Cb = Ring([Slot(k.sb(es, "fu_c%d" % i, [128, T], F32)) for i in range(2)])
        Ho = Ring([Slot(k.sb(es, "fu_h%d" % i, [128, T], BF16), k.dsem()) for i in range(2)])
        cp = k.sb(es, "fu_cp", [128, NFC, 4], F32)
        d0 = k.dsem()
        evcp = sp.dma(d0, cp[:], D_["convp"].ap()[l])
        ev_ms = None
        for s in Gb.slots:
            ev_ms = pool.do(lambda g, s=s: g.memset(s.ap[:], 0.0), sig=True)
        groups = tok_groups(ntok)
        segs = [(0, S, 0)] + ([(S, CL, S + 2)] if has_ctx else [])

        def gcol(g0):
            return 1 + g0 if g0 < S else S + 3 + (g0 - S)

        def mk_run(sp_):
            def run(slab):
                for cc in range(2):
                    ch = sp_ * 2 + cc
                    gb, vb, cb, ho = Gb.next(), Vb.next(), Cb.next(), Ho.next()
                    gfe, vfe = gb.free_evs(), vb.free_evs()
                    gevs, vevs = [], []
                    for (g0, n) in groups:
                        b, ev = pj.fm_group(slab, cc, g0, n)
                        act.wait(ev, gfe, ev_ms)
                        e = evac_copy(k, act, gb.ap[:, gcol(g0):gcol(g0) + n], b.ap[:, 0:n])
                        b.readers.append(e)
                        gevs.append(e)
                    for (g0, n) in groups:
                        b, ev = pj.fm_group(slab, 2 + cc, g0, n)
                        act.wait(ev, vfe)
                        e = evac_copy(k, act, vb.ap[:, g0:g0 + n], b.ap[:, 0:n])
                        b.readers.append(e)
                        vevs.append(e)
                    w = [cp[:, ch, j:j + 1] for j in range(4)]
                    dve.wait(gevs, evcp, cb.free_evs())
                    for (t0, n, c0) in segs:
                        e = dve.do(lambda v, gb=gb, cb=cb, t0=t0, n=n, c0=c0, w=w: v.tensor_scalar(out=cb.ap[:, t0:t0 + n], in0=gb.ap[:, c0:c0 + n], scalar1=w[0], scalar2=w[3], op0=ALU.mult, op1=ALU.add), sig=True)
                    for j in (1, 2):
                        dve.wait(e)
                        for (t0, n, c0) in segs:
                            e = dve.do(lambda v, gb=gb, cb=cb, t0=t0, n=n, c0=c0, w=w, j=j: v.scalar_tensor_tensor(out=cb.ap[:, t0:t0 + n], in0=gb.ap[:, c0 + j:c0 + j + n], scalar=w[j], in1=cb.ap[:, t0:t0 + n],
                                                                                                                     op0=ALU.mult, op1=ALU.add), sig=True)
                    gb.readers.append(e)
                    act.wait(e)
                    e = act.do(lambda a, cb=cb: a.activation(out=cb.ap[:, 0:ntok], in_=cb.ap[:, 0:ntok], func=AF.Silu), sig=True)
                    pool.wait(e, vevs, ho.free_evs())
                    e = pool.do(lambda g, cb=cb, vb=vb, ho=ho: g.tensor_tensor(out=ho.ap[:, 0:ntok], in0=cb.ap[:, 0:ntok], in1=vb.ap[:, 0:ntok], op=ALU.mult), sig=True)
                    cb.readers.append(e)
                    vb.readers.append(e)
                    k.store(sp, ho, D_["hidT"].ap()[ch * 128:(ch + 1) * 128, 0:ntok], ho.ap[:, 0:ntok], e)
            return run

        for sp_ in range(NFC // 2):
            pj.add("fm", (lambda s, sp_=sp_: [(s.ap[:, :, 0:256], Wu[:, sp_ * 256:(sp_ + 1) * 256].rearrange("(k p) n -> p k n", p=128)),
                                              (s.ap[:, :, 256:512], Wu[:, DFF + sp_ * 256:DFF + (sp_ + 1) * 256].rearrange("(k p) n -> p k n", p=128))]), mk_run(sp_))
        pj.run()
        k.flush()
```

```python
import numpy as np
from contextlib import ExitStack
import concourse.bass as bass
import concourse.mybir as mybir
from concourse.bass_utils import run_bass_kernel_spmd

F32 = mybir.dt.float32
BF16 = mybir.dt.bfloat16
AF = mybir.ActivationFunctionType
ALU = mybir.AluOpType

D = 2048
S = 2048
CL = 256
T = S + CL
NT = T // 128
NLT = S // 128
L = 2
H = 16
DFF = 5632
NFC = DFF // 128
N_IN = 19008
O_MQ, O_CKV, O_KR, O_NA, O_GQ, O_GKV, O_GATE = 0, 3072, 3584, 3648, 9792, 11840, 12864
ALPHA = (2 * L) ** 0.25
MLA_SCALE = 192 ** -0.5
NA_SCALE = 128 ** -0.5
GQA_SCALE = 128 ** -0.5
ADA_EPS, POST_EPS, RMS_EPS = 1e-6, 1e-5, 1e-6
NEG = -1.0e4


class DSem:
    def __init__(self, h, key):
        self.h, self.key, self.cnt = h, key, 0


class Eng:
    def __init__(self, name, sem):
        self.name, self.sem, self.cnt, self.ops, self.waited = name, sem, 0, [], {}

    def wait(self, *evs):
        for ev in evs:
            if ev is None:
                continue
            if isinstance(ev, list):
                self.wait(*ev)
                continue
            key, sem, val = ev
            if self.waited.get(key, 0) >= val:
                continue
            self.waited[key] = val
            self.ops.append(lambda e, sem=sem, val=val: e.wait_ge(sem, val))

    def do(self, fn, sig=False):
        if sig:
            self.cnt += 1
            n, sem = self.cnt, self.sem
            self.ops.append(lambda e: fn(e).then_inc(sem, 1))
            return (self.name, sem, n)
        self.ops.append(fn)
        return None

    def dma(self, dsem, out, in_):
        dsem.cnt += 16
        n, h = dsem.cnt, dsem.h
        self.ops.append(lambda e: e.dma_start(out=out, in_=in_).then_inc(h, 16))
        return (dsem.key, h, n)


class Slot:
    def __init__(self, ap, dsem=None):
        self.ap, self.dsem = ap, dsem
        self.ready = None
        self.readers = []

    def free_evs(self):
        r = self.readers
        self.readers = []
        return r


class Ring:
    def __init__(self, slots):
        self.slots, self.i = slots, 0

    def next(self):
        s = self.slots[self.i % len(self.slots)]
        self.i += 1
        return s


class K:
    def __init__(self, dbg=(), stop_after=None):
        self.dbg, self.stop_after = set(dbg), stop_after
        self.branches = (0, 1, 2)
        self.nc = nc = bass.Bass("TRN2", target_bir_lowering=False)
        self.es = ExitStack()
        self.ds_pool, self.ds_i = [], 0
        self.uid = 0
        self.engs = {}
        for n in ("pe", "act", "dve", "pool", "sp"):
            self.engs[n] = Eng(n, self.es.enter_context(nc.semaphore("sem_" + n)))
        self.pe, self.act, self.dve, self.pool, self.sp = (self.engs[n] for n in ("pe", "act", "dve", "pool", "sp"))
        self.dram = {}
        self.pending = []

    def inp(self, name, shape, dt=F32):
        t = self.nc.dram_tensor(name, list(shape), dt, kind="ExternalInput")
        self.dram[name] = t
        return t

    def scr(self, name, shape, dt=BF16):
        if name in self.dbg:
            t = self.nc.dram_tensor(name, list(shape), dt, kind="ExternalOutput")
        else:
            t = self.nc.dram_tensor(name, list(shape), dt)
        self.dram[name] = t
        return t

    def dsem(self):
        if self.ds_i >= len(self.ds_pool):
            n = len(self.ds_pool)
            self.ds_pool.append(DSem(self.es.enter_context(self.nc.semaphore("ds%d" % n)), "ds%d" % n))
        d = self.ds_pool[self.ds_i]
        self.ds_i += 1
        return d

    def sb(self, es, name, shape, dt):
        self.uid += 1
        return es.enter_context(self.nc.sbuf_tensor("%s_%d" % (name, self.uid), list(shape), dt))

    def ps(self, es, name, shape, dt):
        self.uid += 1
        return es.enter_context(self.nc.psum_tensor("%s_%d" % (name, self.uid), list(shape), dt))

    def flush(self, waiter=None):
        w = waiter or self.sp
        w.wait(self.pending)
        self.pending = []
        with self.nc.Block() as block:
            ops = {n: list(e.ops) for n, e in self.engs.items()}

            @block.tensor
            def _(t):
                for f in ops["pe"]:
                    f(t)

            @block.scalar
            def _(a):
                for f in ops["act"]:
                    f(a)

            @block.vector
            def _(v):
                for f in ops["dve"]:
                    f(v)

            @block.gpsimd
            def _(g):
                for f in ops["pool"]:
                    f(g)

            @block.sync
            def _(s):
                for f in ops["sp"]:
                    f(s)
        for e in self.engs.values():
            e.ops = []
        self.ds_i = 0

    def store(self, eng, slot, out, in_, after):
        eng.wait(after)
        ev = eng.dma(slot.dsem, out, in_)
        slot.readers.append(ev)
        self.pending.append(ev)
        return ev


def bc_ap(t, off, n, parts=128):
    return bass.AP(t, off, [[0, parts], [1, n]])


def phase_ada(k, l):
    pe, act, dve, pool, sp = k.pe, k.act, k.dve, k.pool, k.sp
    with ExitStack() as es:
        csb = k.sb(es, "ada_c", [128, 32], F32)
        scT = k.sb(es, "ada_sc", [128, 16, 2], BF16)
        bsb = k.sb(es, "ada_b", [2, 12288], F32)
        msb = k.sb(es, "ada_m", [2, 12288], F32)
        ws = Ring([Slot(k.sb(es, "ada_w%d" % i, [128, 16, 512], BF16), k.dsem()) for i in range(2)])
        pb = Ring([Slot(k.ps(es, "ada_p%d" % i, [128, 512], F32)) for i in range(2)])
        d0 = k.dsem()
        sp.dma(d0, csb[:], k.dram["cT"].ap().rearrange("p k j -> p (k j)"))
        e2 = sp.dma(d0, bsb[:], bc_ap(k.dram["b_ada"], l * 12288, 12288, parts=2))
        act.wait(e2)
        ev_sc = act.do(lambda a: a.activation(out=scT[:].rearrange("p k j -> p (k j)"), in_=csb[:], func=AF.Silu), sig=True)
        wada = k.dram["w_ada"].ap()

        def load(n):
            s = ws.next()
            pool.wait(s.free_evs())
            s.ready = pool.dma(s.dsem, s.ap[:], wada[l, :, n * 512:(n + 1) * 512].rearrange("(k p) n -> p k n", p=128))
            return s

        def mm(b, cur, kk):
            return lambda t: t.matmul(b.ap[0:2, :], lhsT=scT[:, kk, :], rhs=cur.ap[:, kk, :], start=(kk == 0), stop=(kk == 15))

        def addb(b, n):
            return lambda v: v.tensor_tensor(out=msb[:, n * 512:(n + 1) * 512], in0=b.ap[0:2, :], in1=bsb[:, n * 512:(n + 1) * 512], op=ALU.add)

        nxt = load(0)
        ev2 = None
        for n in range(24):
            cur = nxt
            if n + 1 < 24:
                nxt = load(n + 1)
            b = pb.next()
            pe.wait(cur.ready, ev_sc, b.free_evs())
            for kk in range(16):
                ev = pe.do(mm(b, cur, kk), sig=(kk == 15))
            cur.readers.append(ev)
            dve.wait(ev, e2)
            ev2 = dve.do(addb(b, n), sig=True)
            b.readers.append(ev2)
        st = Slot(None, k.dsem())
        k.store(sp, st, k.dram["mvec"].ap()[l], msb[:], ev2)
        k.flush()


def ln_stats(k, xin_ap, mv, rstd, nmr, eps, stats):
    dve, act = k.dve, k.act
    for c in range(4):
        ev = dve.do(lambda v, c=c: v.bn_stats(out=stats[:, c, :], in_=xin_ap[:, c * 512:(c + 1) * 512]), sig=(c == 3))
    dve.wait(ev)
    ev = dve.do(lambda v: v.bn_aggr(out=mv, in_=stats), sig=True)
    act.wait(ev)
    ev = act.do(lambda a: a.activation(out=rstd, in_=mv[:, 1:2], func=AF.Sqrt, bias=eps, scale=1.0), sig=True)
    dve.wait(ev)
    ev = dve.do(lambda v: v.reciprocal(out=rstd, in_=rstd), sig=True)
    dve.wait(ev)
    return dve.do(lambda v: v.tensor_scalar(out=nmr, in0=mv[:, 0:1], scalar1=-1.0, scalar2=rstd, op0=ALU.mult, op1=ALU.mult), sig=True)


def phase_mod(k, l, sub, src, hT, ident, ntiles):
    pe, act, dve, pool, sp = k.pe, k.act, k.dve, k.pool, k.sp
    mv_t = k.dram["mvec"]
    with ExitStack() as es:
        bcs = {}
        d0 = k.dsem()
        evb = None
        for j, nm in ((0, "lat"), (1, "ctx")):
            sh = k.sb(es, "mod_sh" + nm, [128, 2048], F32)
            s1 = k.sb(es, "mod_s1" + nm, [128, 2048], F32)
            base = (l * 2 + j) * 12288 + sub * 3 * 2048
            sp.dma(d0, sh[:], bc_ap(mv_t, base, 2048))
            evb = sp.dma(d0, s1[:], bc_ap(mv_t, base + 2048, 2048))
            bcs[j] = (sh, s1)
        dve.wait(evb)
        for j in (0, 1):
            s1 = bcs[j][1]
            evb2 = dve.do(lambda v, s1=s1: v.tensor_scalar(out=s1[:], in0=s1[:], scalar1=1.0, scalar2=None, op0=ALU.add), sig=True)
        xr = Ring([Slot(k.sb(es, "mod_x%d" % i, [128, 2048], F32), k.dsem()) for i in range(2)])
        xn = Ring([Slot(k.sb(es, "mod_xn%d" % i, [128, 2048], F32)) for i in range(2)])
        hb = Ring([Slot(k.sb(es, "mod_hb%d" % i, [128, 2048], BF16)) for i in range(3)])
        sm = Ring([Slot(k.sb(es, "mod_sm%d" % i, [128, 32], F32)) for i in range(2)])
        pt = Ring([Slot(k.ps(es, "mod_pt%d" % i, [128, 512], BF16)) for i in range(4)])

        def load(i):
            s = xr.next()
            sp.wait(s.free_evs())
            s.ready = sp.dma(s.dsem, s.ap[:], src(i))
            return s

        def emit_tr(hs, ev, i):
            for jj in range(4):
                p = pt.next()
                pe.wait(ev, p.free_evs())
                for a in range(4):
                    kk = 4 * jj + a
                    evp = pe.do(lambda t, p=p, hs=hs, kk=kk, a=a: t.transpose(out=p.ap[:, a * 128:(a + 1) * 128], in_=hs.ap[:, kk * 128:(kk + 1) * 128], identity=ident[:]), sig=(a == 3))
                eng = act if jj % 2 == 0 else dve
                eng.wait(evp)
                evc = evac_copy(k, eng, hT[:, 4 * jj:4 * jj + 4, i * 128:(i + 1) * 128], p.ap[:].rearrange("p (a b) -> p a b", a=4))
                p.readers.append(evc)
            hs.readers.append(evp)

        pend = []
        nxt = load(0)
        for i in range(ntiles):
            cur = nxt
            if i + 1 < ntiles:
                nxt = load(i + 1)
            j = 0 if i < NLT else 1
            sh, s1 = bcs[j]
            smt = sm.next()
            st = smt.ap
            stats = st[:, 0:24].rearrange("p (c s) -> p c s", s=6)
            mv, rstd, nmr = st[:, 24:26], st[:, 26:27], st[:, 27:28]
            dve.wait(cur.ready, smt.free_evs())
            ev = ln_stats(k, cur.ap, mv, rstd, nmr, ADA_EPS, stats)
            xs = xn.next()
            act.wait(ev, xs.free_evs())
            ev = act.do(lambda a, xs=xs, cur=cur, rstd=rstd, nmr=nmr: a.activation(out=xs.ap[:], in_=cur.ap[:], func=AF.Identity, bias=nmr, scale=rstd), sig=True)
            cur.readers.append(ev)
            smt.readers.append(ev)
            dve.wait(ev, evb2)
            ev = dve.do(lambda v, xs=xs, s1=s1: v.tensor_tensor(out=xs.ap[:], in0=xs.ap[:], in1=s1[:], op=ALU.mult), sig=True)
            hs = hb.next()
            pool.wait(ev, hs.free_evs())
            ev = pool.do(lambda g, xs=xs, hs=hs, sh=sh: g.tensor_tensor(out=hs.ap[:], in0=xs.ap[:], in1=sh[:], op=ALU.add), sig=True)
            xs.readers.append(ev)
            pend.append((hs, ev, i))
            if len(pend) > 1:
                emit_tr(*pend.pop(0))
        while pend:
            emit_tr(*pend.pop(0))
        k.flush()


def declare(k):
    k.inp("x", [S, D])
    k.inp("ctx", [CL, D])
    k.inp("cT", [128, 16, 2])
    k.inp("w_ada", [L, D, 6 * D])
    k.inp("b_ada", [L, 6 * D])
    k.inp("w_in", [L, D, N_IN])
    k.inp("mla_kv_norm", [L, 512])
    k.inp("w_mla_ukv", [L, 512, 4096])
    k.inp("gqa_q_norm", [L, 128])
    k.inp("gqa_k_norm", [L, 128])
    k.inp("na_bias", [L, H, 128, 7680])
    k.inp("w_branch", [L, 3, D, D])
    k.inp("w_out", [L, D, D])
    k.inp("ln_a_g", [L, D])
    k.inp("ln_a_b", [L, D])
    k.inp("w_up", [L, D, 2 * DFF])
    k.inp("convp", [L, 128, NFC, 4])
    k.inp("w_down", [L, DFF, D])
    k.inp("ln_f_g", [L, D])
    k.inp("ln_f_b", [L, D])
    k.inp("ropeM", [2, 128, NLT, 64])
    k.inp("ropeG", [2, 128, NLT, 128])
    k.out = k.nc.dram_tensor("out", [S, D], F32, kind="ExternalOutput")
    k.scr("mvec", [L, 2, 6 * D], F32)
    k.scr("xres", [T, D], F32)
    k.scr("fout", [T, D], F32)
    k.scr("hT_dbg", [D, T], BF16)
    k.scr("qa_nT", [H, 128, T])
    k.scr("qa_rT", [H // 2, 128, T])
    k.scr("ka_nT", [H, 128, T])
    k.scr("ka_rT", [128, T])
    k.scr("cgT", [512, T])
    k.scr("va", [T, D])
    k.scr("qbT", [H, 128, T])
    k.scr("kbT", [H, 128, T])
    k.scr("vb", [T, D])
    k.scr("qcT", [H, 128, T])
    k.scr("kcT", [4, 128, T])
    k.scr("vc", [T, 512])
    k.scr("gateT", [3, D, T])
    k.scr("yT", [3, H, 128, T])
    k.scr("accT", [D, T])
    k.scr("hidT", [DFF, T])
    k.scr("rstd_dbg", [128, NT], F32)


def make_ident(k, es):
    ident = k.sb(es, "ident", [128, 128], BF16)
    k.pool.do(lambda g: g.memset(ident[:], 0.0))
    k.pool.do(lambda g: g.affine_select(out=ident[:], in_=ident[:], pattern=[[-1, 128]], compare_op=ALU.not_equal, fill=1.0, base=0, channel_multiplier=1))
    return ident


def build(dbg=(), stop_after=None):
    k = K(dbg, stop_after)
    declare(k)
    sp = k.sp
    with ExitStack() as es:
        ident = make_ident(k, es)
        k.flush()
        for l in range(L):
            phase_ada(k, l)
        if stop_after == "ada":
            return k
        xin, cin, xres = k.dram["x"].ap(), k.dram["ctx"].ap(), k.dram["xres"].ap()
        for l in range(L):
            ntq = NT if l < L - 1 else NLT
            if l == 0:
                src = lambda i: (xin[i * 128:(i + 1) * 128, :] if i < NLT else cin[(i - NLT) * 128:(i - NLT + 1) * 128, :])
            else:
                src = lambda i: xres[i * 128:(i + 1) * 128, :]
            with ExitStack() as es2:
                hT = k.sb(es2, "hT", [128, 16, T], BF16)
                phase_mod(k, l, 0, src, hT, ident, NT)
                if "hT_dbg" in k.dbg and l == 0:
                    st = Slot(None, k.dsem())
                    k.store(sp, st, k.dram["hT_dbg"].ap().rearrange("(k p) t -> p k t", p=128), hT[:], None)
                    k.flush()
                if stop_after == "mod":
                    return k
                rms_all = k.sb(es2, "rms_all", [128, NT], F32)
                rstd_all = k.sb(es2, "rstd_all", [128, NT], F32)
                phase_win(k, l, hT, ident, ntq, rms_all, rstd_all)
                if stop_after == "win":
                    return k
                phase_ukv(k, l, rstd_all)
                if stop_after == "ukv":
                    return k
                for br in k.branches:
                    phase_attn(k, l, br, ident, ntq, rstd_all)
                if stop_after == "attn":
                    return k
            phase_merge(k, l, ntq)
            if stop_after == "merge":
                return k
            phase_tm_out(k, l, "out", ntq)
            phase_resid(k, l, 0, src, ntq, False)
            if stop_after == "mixer":
                return k
            srcx = lambda i: xres[i * 128:(i + 1) * 128, :]
            with ExitStack() as es2:
                hT = k.sb(es2, "hT", [128, 16, T], BF16)
                phase_mod(k, l, 1, srcx, hT, ident, ntq)
                phase_ffn_up(k, l, hT, ntq)
            if stop_after == "ffnup":
                return k
            phase_tm_out(k, l, "down", ntq)
            phase_resid(k, l, 1, srcx, ntq, l == L - 1)
            if stop_after == "layer%d" % l:
                return k
    return k


def host_inputs(inputs):
    f32 = np.float32
    common = {}
    for n in ("w_ada", "b_ada", "w_in", "mla_kv_norm", "w_mla_ukv", "gqa_q_norm", "gqa_k_norm", "w_branch",
              "w_out", "ln_a_g", "ln_a_b", "w_up", "w_down", "ln_f_g", "ln_f_b"):
        common[n] = np.ascontiguousarray(inputs[n], dtype=f32)
    cw = np.asarray(inputs["conv_w"], f32)
    cb = np.asarray(inputs["conv_b"], f32)
    cp = np.concatenate([cw, cb[:, None, :]], axis=1)
    common["convp"] = np.ascontiguousarray(cp.reshape(L, 4, NFC, 128).transpose(0, 3, 2, 1))
    rpb = np.asarray(inputs["na_rpb"], f32)
    a_ = np.arange(2)[:, None, None, None]
    kc = np.arange(64)[None, :, None, None]
    t_ = np.arange(24)[None, None, :, None]
    qc = np.arange(64)[None, None, None, :]
    dr = a_ - t_ + 11
    cs = np.clip(qc - 8, 0, 48)
    colv = (kc >= cs) & (kc < cs + 16)
    shp = (2, 64, 24, 64)
    dri = np.broadcast_to(np.clip(dr + 7, 0, 14), shp)
    dci = np.broadcast_to(np.clip(kc - qc + 15, 0, 30), shp)
    nb = rpb[:, :, dri, dci]
    valid = np.broadcast_to(((dr >= -4) & (dr <= 3)) & colv, shp)
    tab_int = np.where(valid[None, None], nb, f32(NEG)).astype(f32).reshape(L, H, 128, 1536)
    tabs = [tab_int]
    bq = np.arange(8)[None, None, :, None]
    for J, i0 in ((0, 0), (3, 10)):
        per_tile = []
        for ii in range(6):
            r = 8 * J + bq
            kr = 2 * (i0 + ii) + a_
            rs_ = np.clip(r - 4, 0, 24)
            vrow = (kr >= rs_) & (kr < rs_ + 8)
            drb = kr - r
            shp2 = (2, 64, 8, 64)
            v2 = np.broadcast_to(vrow & colv, shp2)
            g = rpb[:, :, np.broadcast_to(np.clip(drb + 7, 0, 14), shp2), np.broadcast_to(np.clip(kc - qc + 15, 0, 30), shp2)]
            per_tile.append(np.where(v2[None, None], g, f32(NEG)).astype(f32).reshape(L, H, 128, 512))
        tabs.append(np.concatenate(per_tile, axis=-1))
    common["na_bias"] = np.ascontiguousarray(np.concatenate(tabs, axis=-1))
    tok = np.arange(S)
    rows, cols = (tok // 64).astype(f32), (tok % 64).astype(f32)

    def tables(quarter):
        fr = (f32(10000.0) ** (-np.arange(quarter, dtype=f32) / f32(quarter))).astype(f32)
        ar, ac = rows[:, None] * fr[None, :], cols[:, None] * fr[None, :]
        C = np.concatenate([np.cos(ar), np.cos(ar), np.cos(ac), np.cos(ac)], 1)
        Sg = np.concatenate([-np.sin(ar), np.sin(ar), -np.sin(ac), np.sin(ac)], 1)
        tb = np.stack([C, Sg], 0).astype(f32)
        return np.ascontiguousarray(tb.reshape(2, NLT, 128, 4 * quarter).transpose(0, 2, 1, 3))
    common["ropeM"] = tables(16)
    common["ropeG"] = tables(32)
    x = np.asarray(inputs["x"], f32)
    ctx = np.asarray(inputs["ctx"], f32)
    c = np.asarray(inputs["c"], f32)
    cc = np.asarray(inputs["c_ctx"], f32)
    maps = []
    for b in range(8):
        m = dict(common)
        m["x"] = np.ascontiguousarray(x[b])
        m["ctx"] = np.ascontiguousarray(ctx[b])
        c2 = np.stack([c[b], cc], 0)
        m["cT"] = np.ascontiguousarray(c2.reshape(2, 16, 128).transpose(2, 1, 0))
        maps.append(m)
    return maps


def kernel(**inputs):
    k = build()
    maps = host_inputs(inputs)
    res = run_bass_kernel_spmd(k.nc, maps, core_ids=list(range(8)))
    return np.stack([np.asarray(r["out"], np.float32) for r in res.results], 0)


def tok_groups(ntok):
    g, t0 = [], 0
    while t0 < ntok:
        n = min(512, ntok - t0)
        g.append((t0, n))
        t0 += n
    return g


class Proj:
    def __init__(self, k, es, A, KC, wcols, nslots, nbanks, pfx):
        self.k, self.A, self.KC = k, A, KC
        self.ws = Ring([Slot(k.sb(es, pfx + "_w%d" % i, [128, KC, wcols], BF16), k.dsem()) for i in range(nslots)])
        self.pb = Ring([Slot(k.ps(es, pfx + "_p%d" % i, [128, 512], F32)) for i in range(nbanks)])
        self.items = []

    def add(self, mode, src_fn, run_fn):
        self.items.append((mode, src_fn, run_fn))

    def load(self, idx):
        _, src_fn, _ = self.items[idx]
        s = self.ws.next()
        self.k.pool.wait(s.free_evs())
        pairs = src_fn(s)
        if isinstance(pairs, tuple):
            pairs = [pairs]
        for dst, src in pairs:
            s.ready = self.k.pool.dma(s.dsem, dst, src)
        return s

    def run(self):
        n = len(self.items)
        depth = len(self.ws.slots) - 1
        loaded = []
        for j in range(min(depth, n)):
            loaded.append(self.load(j))
        for idx in range(n):
            if idx + depth < n:
                loaded.append(self.load(idx + depth))
            self.items[idx][2](loaded[idx])
        self.items = []

    def tm_tile(self, slab, width, i, extra_wait=None):
        pe = self.k.pe
        b = self.pb.next()
        pe.wait(slab.ready, b.free_evs(), extra_wait)
        A, KC = self.A, self.KC
        for kk in range(KC):
            ev = pe.do(lambda t, b=b, kk=kk: t.matmul(b.ap[:, 0:width], lhsT=A[:, kk, i * 128:(i + 1) * 128], rhs=slab.ap[:, kk, 0:width],
                                                     start=(kk == 0), stop=(kk == KC - 1)), sig=(kk == KC - 1))
        slab.readers.append(ev)
        return b, ev

    def fm_group(self, slab, c, g0, n, kcs=None, extra_wait=None):
        pe = self.k.pe
        b = self.pb.next()
        pe.wait(slab.ready, b.free_evs(), extra_wait)
        A = self.A
        kcs = list(range(self.KC)) if kcs is None else kcs
        for j, kk in enumerate(kcs):
            ev = pe.do(lambda t, b=b, kk=kk, j=j: t.matmul(b.ap[:, 0:n], lhsT=slab.ap[:, j, c * 128:(c + 1) * 128], rhs=A[:, kk, g0:g0 + n],
                                                           start=(j == 0), stop=(j == len(kcs) - 1)), sig=(j == len(kcs) - 1))
        slab.readers.append(ev)
        return b, ev


def evac_copy(k, eng, out, in_, func=None, scale=None):
    if eng is k.act:
        f = func if func is not None else (AF.Copy if scale is None else AF.Identity)
        if scale is None:
            return eng.do(lambda a: a.activation(out=out, in_=in_, func=f), sig=True)
        return eng.do(lambda a: a.activation(out=out, in_=in_, func=f, scale=scale), sig=True)
    assert func is None
    if scale is None:
        return eng.do(lambda v: v.tensor_copy(out=out, in_=in_), sig=True)
    return eng.do(lambda v: v.tensor_scalar(out=out, in0=in_, scalar1=scale, scalar2=None, op0=ALU.mult), sig=True)


class FMOut:
    def __init__(self, k, es, nslots, pfx):
        self.k = k
        self.ring = Ring([Slot(k.sb(es, pfx + "_fs%d" % i, [128, T], BF16), k.dsem()) for i in range(nslots)])
        self.cur, self.evs, self.tog = None, [], 0

    def evac(self, b, ev, g0, n, first, last, dst, ntok, func=None):
        k = self.k
        if first:
            self.cur = self.ring.next()
            self.evs = []
            self.fe = self.cur.free_evs()
        eng = k.act if (func is not None or self.tog % 2 == 0) else k.dve
        self.tog += 1
        eng.wait(ev, self.fe)
        e2 = evac_copy(k, eng, self.cur.ap[:, g0:g0 + n], b.ap[:, 0:n], func=func)
        b.readers.append(e2)
        self.evs.append(e2)
        if last:
            k.store(k.sp, self.cur, dst[:, 0:ntok], self.cur.ap[:, 0:ntok], self.evs)


class TOut:
    def __init__(self, k, es, ident, nslots, pfx):
        self.k, self.ident = k, ident
        self.ring = Ring([Slot(k.sb(es, pfx + "_ts%d" % i, [128, 4, T], BF16), k.dsem()) for i in range(nslots)])
        self.pt = Ring([Slot(k.ps(es, pfx + "_tp%d" % i, [128, 512], BF16)) for i in range(2)])
        self.tog = 0
        self.q = []
        self.DELAY = 2

    def begin(self):
        self.cur = self.ring.next()
        self.fe = self.cur.free_evs()
        self.evs = []

    def tile(self, o_slot, o_ev, nblk, i):
        self.q.append((o_slot, o_ev, nblk, i))
        while len(self.q) > self.DELAY:
            self._emit(*self.q.pop(0))

    def _emit(self, o_slot, o_ev, nblk, i):
        k, pe, ident = self.k, self.k.pe, self.ident
        p = self.pt.next()
        pe.wait(o_ev, p.free_evs())
        for j in range(nblk):
            evp = pe.do(lambda t, j=j, p=p: t.transpose(out=p.ap[:, j * 128:(j + 1) * 128], in_=o_slot.ap[:, j * 128:(j + 1) * 128], identity=ident[:]), sig=(j == nblk - 1))
        o_slot.readers.append(evp)
        eng = k.act if self.tog % 2 == 0 else k.dve
        self.tog += 1
        eng.wait(evp, self.fe)
        e2 = evac_copy(k, eng, self.cur.ap[:, 0:nblk, i * 128:(i + 1) * 128], p.ap[:, 0:nblk * 128].rearrange("p (a b) -> p a b", b=128))
        p.readers.append(e2)
        self.evs.append(e2)

    def finish(self, dsts, ntok):
        k = self.k
        while self.q:
            self._emit(*self.q.pop(0))
        for j, d in enumerate(dsts):
            k.store(k.sp, self.cur, d[:, 0:ntok], self.cur.ap[:, j, 0:ntok], self.evs)


def tab_bc(tab, i, nh, Dh, sub=None):
    pstep = NLT * Dh
    if sub is None:
        return bass.AP(tab, i * Dh, [[pstep, 128], [0, nh], [1, Dh]])
    b, q = sub
    return bass.AP(tab, i * Dh + b * q, [[pstep, 128], [0, nh], [2 * q, 2], [1, q]])


def rope_emit(k, x_ap, x_ev, nh, Dh, tabC, tabS, i, t1, t2, o_ap, o_free, sc=None):
    dve, pool = k.dve, k.pool
    q = Dh // 4
    W = nh * Dh
    xv = x_ap.rearrange("p (h a b q) -> p h a b q", h=nh, a=2, b=2)
    t1v = t1.ap[:, 0:W].rearrange("p (h d) -> p h d", h=nh)
    t2v = t2.ap[:, 0:W].rearrange("p (h a b q) -> p h a b q", h=nh, a=2, b=2)
    dve.wait(x_ev, t1.free_evs(), t2.free_evs())
    if sc is None:
        dve.do(lambda v: v.tensor_tensor(out=t1v, in0=x_ap.rearrange("p (h d) -> p h d", h=nh), in1=tab_bc(tabC, i, nh, Dh), op=ALU.mult))
        dve.do(lambda v: v.tensor_tensor(out=t2v[:, :, :, 0, :], in0=xv[:, :, :, 1, :], in1=tab_bc(tabS, i, nh, Dh, (0, q)), op=ALU.mult))
        ev = dve.do(lambda v: v.tensor_tensor(out=t2v[:, :, :, 1, :], in0=xv[:, :, :, 0, :], in1=tab_bc(tabS, i, nh, Dh, (1, q)), op=ALU.mult), sig=True)
    else:
        assert nh == 1
        Cv = bass.AP(tabC, i * Dh, [[NLT * Dh, 128], [1, Dh]])
        dve.do(lambda v: v.scalar_tensor_tensor(out=t1.ap[:, 0:W], in0=x_ap, scalar=sc, in1=Cv, op0=ALU.mult, op1=ALU.mult))
        for b in (0, 1):
            Sv = bass.AP(tabS, i * Dh + b * q, [[NLT * Dh, 128], [2 * q, 2], [1, q]])
            ev = dve.do(lambda v, b=b, Sv=Sv: v.scalar_tensor_tensor(out=t2v[:, 0, :, b, :], in0=xv[:, 0, :, 1 - b, :], scalar=sc, in1=Sv, op0=ALU.mult, op1=ALU.mult), sig=(b == 1))
    pool.wait(ev, o_free)
    ev2 = pool.do(lambda g: g.tensor_tensor(out=o_ap, in0=t1.ap[:, 0:W], in1=t2.ap[:, 0:W], op=ALU.add), sig=True)
    t1.readers.append(ev2)
    t2.readers.append(ev2)
    return ev2, [ev]


def phase_win(k, l, hT, ident, ntq, rms_all, rstd_all):
    pe, act, dve, pool, sp = k.pe, k.act, k.dve, k.pool, k.sp
    W = k.dram["w_in"].ap()[l]
    ntok_q = ntq * 128
    with ExitStack() as es:
        pj = Proj(k, es, hT, 16, 512, 2, 4, "win")
        fmo = FMOut(k, es, 2, "win")
        tout = TOut(k, es, ident, 2, "win")
        vst = Ring([Slot(k.sb(es, "win_vst%d" % i, [128, 512], BF16), k.dsem()) for i in range(2)])
        t1r = Ring([Slot(k.sb(es, "win_t1%d" % i, [128, 512], F32)) for i in range(2)])
        t2r = Ring([Slot(k.sb(es, "win_t2%d" % i, [128, 512], F32)) for i in range(2)])
        xgr = Ring([Slot(k.sb(es, "win_xg%d" % i, [128, 512], F32)) for i in range(3)])
        orr = Ring([Slot(k.sb(es, "win_o%d" % i, [128, 512], BF16)) for i in range(4)])
        smr = Ring([Slot(k.sb(es, "win_sm%d" % i, [128, 16], F32)) for i in range(4)])
        junk = k.sb(es, "win_junk", [128, 512], BF16)
        rM = [k.sb(es, "win_rM%d" % j, [128, NLT, 64], F32) for j in range(2)]
        rG = [k.sb(es, "win_rG%d" % j, [128, NLT, 128], F32) for j in range(2)]
        g512 = k.sb(es, "win_g512", [128, 512], F32)
        gqk = [k.sb(es, "win_gqk%d" % j, [128, 128], F32) for j in range(2)]
        d0 = k.dsem()
        for j in range(2):
            sp.dma(d0, rM[j][:], k.dram["ropeM"].ap()[j])
            sp.dma(d0, rG[j][:], k.dram["ropeG"].ap()[j])
        sp.dma(d0, g512[:], bc_ap(k.dram["mla_kv_norm"], l * 512, 512))
        sp.dma(d0, gqk[0][:], bc_ap(k.dram["gqa_q_norm"], l * 128, 128))
        ev_tab = sp.dma(d0, gqk[1][:], bc_ap(k.dram["gqa_k_norm"], l * 128, 128))
        for e in (act, dve, pool):
            e.wait(ev_tab)

        def src2d(c0, w):
            return lambda s: (s.ap[:, :, 0:w], W[:, c0:c0 + w].rearrange("(k p) n -> p k n", p=128))

        def src_mq(h0, nh, d0_, dw):
            v = W[:, 0:3072].rearrange("(k p) (h d) -> p k h d", p=128, d=192)
            return lambda s: [(s.ap[:, :, hh * dw:(hh + 1) * dw], v[:, :, h0 + hh, d0_:d0_ + dw]) for hh in range(nh)]

        def fm_run(nchunks, dsts, ntok, func=None):
            groups = tok_groups(ntok)

            def run(slab):
                for c in range(nchunks):
                    for gi, (g0, n) in enumerate(groups):
                        b, ev = pj.fm_group(slab, c, g0, n)
                        fmo.evac(b, ev, g0, n, gi == 0, gi == len(groups) - 1, dsts[c], ntok, func=func)
            return run

        def v_run(dst, width, ntiles):
            def run(slab):
                for i in range(ntiles):
                    b, ev = pj.tm_tile(slab, width, i)
                    st = vst.next()
                    act.wait(ev, st.free_evs())
                    e2 = evac_copy(k, act, st.ap[:, 0:width], b.ap[:, 0:width])
                    b.readers.append(e2)
                    k.store(sp, st, dst[i * 128:(i + 1) * 128, :], st.ap[:, 0:width], e2)
            return run

        def mqrope_run(h0):
            def run(slab):
                tout.begin()
                for i in range(ntq):
                    b, ev = pj.tm_tile(slab, 512, i)
                    o = orr.next()
                    if i < NLT:
                        e2, xr = rope_emit(k, b.ap[:, 0:512], ev, 8, 64, rM[0], rM[1], i, t1r.next(), t2r.next(), o.ap[:, 0:512], o.free_evs())
                        b.readers.extend(xr)
                    else:
                        act.wait(ev, o.free_evs())
                        e2 = evac_copy(k, act, o.ap[:, 0:512], b.ap[:, 0:512])
                        b.readers.append(e2)
                    tout.tile(o, e2, 4, i)
                tout.finish([k.dram["qa_rT"].ap()[h0 // 2 + j] for j in range(4)], ntok_q)
            return run

        def ckv_run(slab):
            tout.begin()
            for i in range(NT):
                b, ev = pj.tm_tile(slab, 512, i)
                sm = smr.next()
                act.wait(ev, sm.free_evs())
                e1 = act.do(lambda a, b=b, sm=sm: a.activation(out=junk[:], in_=b.ap[:, 0:512], func=AF.Square, accum_out=sm.ap[:, 0:1]), sig=True)
                act.wait(e1)
                e1 = act.do(lambda a, sm=sm, i=i: a.activation(out=rms_all[:, i:i + 1], in_=sm.ap[:, 0:1], func=AF.Sqrt, bias=RMS_EPS, scale=1.0 / 512), sig=True)
                sm.readers.append(e1)
                dve.wait(e1)
                e3 = dve.do(lambda v, i=i: v.reciprocal(out=rstd_all[:, i:i + 1], in_=rms_all[:, i:i + 1]), sig=True)
                o = orr.next()
                dve.wait(o.free_evs())
                e2 = dve.do(lambda v, b=b, o=o: v.tensor_tensor(out=o.ap[:, 0:512], in0=b.ap[:, 0:512], in1=g512[:], op=ALU.mult), sig=True)
                b.readers.extend([e1, e2])
                tout.tile(o, e2, 4, i)
            k.ev_rstd = e3
            tout.finish([k.dram["cgT"].ap()[j * 128:(j + 1) * 128, :] for j in range(4)], T)

        def kr_run(slab):
            tout.begin()
            for i in range(NT):
                b, ev = pj.tm_tile(slab, 64, i)
                o = orr.next()
                sc = rms_all[:, i:i + 1]
                dve.wait(k.ev_rstd)
                if i < NLT:
                    e2, xr = rope_emit(k, b.ap[:, 0:64], ev, 1, 64, rM[0], rM[1], i, t1r.next(), t2r.next(), o.ap[:, 0:64], o.free_evs(), sc=sc)
                    b.readers.extend(xr)
                    pool.wait(e2)
                    e2 = pool.do(lambda g, o=o: g.tensor_copy(out=o.ap[:, 64:128], in_=o.ap[:, 0:64]), sig=True)
                else:
                    dve.wait(ev, o.free_evs())
                    dve.do(lambda v, b=b, o=o, sc=sc: v.tensor_scalar(out=o.ap[:, 0:64], in0=b.ap[:, 0:64], scalar1=sc, scalar2=None, op0=ALU.mult))
                    e2 = dve.do(lambda v, b=b, o=o, sc=sc: v.tensor_scalar(out=o.ap[:, 64:128], in0=b.ap[:, 0:64], scalar1=sc, scalar2=None, op0=ALU.mult), sig=True)
                    b.readers.append(e2)
                tout.tile(o, e2, 1, i)
            tout.finish([k.dram["ka_rT"].ap()], T)

        def gqa_run(gvec, dsts, ntiles):
            def run(slab):
                tout.begin()
                for i in range(ntiles):
                    b, ev = pj.tm_tile(slab, 512, i)
                    sm = smr.next()
                    act.wait(ev, sm.free_evs())
                    for hh in range(4):
                        e1 = act.do(lambda a, b=b, sm=sm, hh=hh: a.activation(out=junk[:, 0:128], in_=b.ap[:, hh * 128:(hh + 1) * 128], func=AF.Square, accum_out=sm.ap[:, hh:hh + 1]), sig=(hh == 3))
                    act.wait(e1)
                    e1 = act.do(lambda a, sm=sm: a.activation(out=sm.ap[:, 4:8], in_=sm.ap[:, 0:4], func=AF.Sqrt, bias=RMS_EPS, scale=1.0 / 128), sig=True)
                    dve.wait(e1)
                    e3 = dve.do(lambda v, sm=sm: v.reciprocal(out=sm.ap[:, 8:12], in_=sm.ap[:, 4:8]), sig=True)
                    xg = xgr.next()
                    dve.wait(e3, xg.free_evs())
                    for hh in range(4):
                        e4 = dve.do(lambda v, b=b, sm=sm, hh=hh, xg=xg: v.scalar_tensor_tensor(out=xg.ap[:, hh * 128:(hh + 1) * 128], in0=b.ap[:, hh * 128:(hh + 1) * 128],
                                                                                              scalar=sm.ap[:, 8 + hh:9 + hh], in1=gvec[:], op0=ALU.mult, op1=ALU.mult), sig=(hh == 3))
                    b.readers.extend([e1, e4])
                    sm.readers.append(e4)
                    o = orr.next()
                    if i < NLT:
                        e2, xr = rope_emit(k, xg.ap[:, 0:512], e4, 4, 128, rG[0], rG[1], i, t1r.next(), t2r.next(), o.ap[:, 0:512], o.free_evs())
                        xg.readers.extend(xr)
                    else:
                        pool.wait(e4, o.free_evs())
                        e2 = pool.do(lambda g, xg=xg, o=o: g.tensor_copy(out=o.ap[:, 0:512], in_=xg.ap[:, 0:512]), sig=True)
                        xg.readers.append(e2)
                    tout.tile(o, e2, 4, i)
                tout.finish(dsts, ntiles * 128)
            return run

        D_ = k.dram
        pj.add("tm", src2d(O_CKV, 512), ckv_run)
        pj.add("tm", src2d(O_KR, 64), kr_run)
        for h0 in (0, 8):
            pj.add("tm", src_mq(h0, 8, 128, 64), mqrope_run(h0))
        for h0 in range(0, H, 4):
            pj.add("fm", src_mq(h0, 4, 0, 128), fm_run(4, [D_["qa_nT"].ap()[h0 + j] for j in range(4)], ntok_q))
        for h0 in range(0, H, 4):
            pj.add("fm", src2d(O_NA + h0 * 128, 512), fm_run(4, [D_["qbT"].ap()[h0 + j] for j in range(4)], ntok_q))
        for h0 in range(0, H, 4):
            pj.add("fm", src2d(O_NA + 2048 + h0 * 128, 512), fm_run(4, [D_["kbT"].ap()[h0 + j] for j in range(4)], T))
        for c0 in range(0, 2048, 512):
            pj.add("tm", src2d(O_NA + 4096 + c0, 512), v_run(D_["vb"].ap()[:, c0:c0 + 512], 512, NT))
        for h0 in range(0, H, 4):
            pj.add("tm", src2d(O_GQ + h0 * 128, 512), gqa_run(gqk[0], [D_["qcT"].ap()[h0 + j] for j in range(4)], ntq))
        pj.add("tm", src2d(O_GKV, 512), gqa_run(gqk[1], [D_["kcT"].ap()[j] for j in range(4)], NT))
        pj.add("tm", src2d(O_GKV + 512, 512), v_run(D_["vc"].ap(), 512, NT))
        for br in range(3):
            for c0 in range(0, 2048, 512):
                pj.add("fm", src2d(O_GATE + br * 2048 + c0, 512),
                       fm_run(4, [D_["gateT"].ap()[br, c0 + j * 128:c0 + (j + 1) * 128, :] for j in range(4)], ntok_q, func=AF.Sigmoid))
        pj.run()
        if "rstd_dbg" in k.dbg:
            st = Slot(None, k.dsem())
            k.store(sp, st, k.dram["rstd_dbg"].ap(), rstd_all[:], k.ev_rstd)
        k.flush()


def phase_ukv(k, l, rstd_all):
    pe, act, dve, pool, sp = k.pe, k.act, k.dve, k.pool, k.sp
    with ExitStack() as es:
        A2 = k.sb(es, "ukv_A", [128, 4, T], BF16)
        Wu = k.sb(es, "ukv_W", [128, 4, 4096], BF16)
        pb = Ring([Slot(k.ps(es, "ukv_p%d" % i, [128, 512], F32)) for i in range(4)])
        fmo = FMOut(k, es, 2, "ukv")
        vst = Ring([Slot(k.sb(es, "ukv_vst%d" % i, [128, 512], BF16), k.dsem()) for i in range(2)])
        d0, d1 = k.dsem(), k.dsem()
        evA = sp.dma(d0, A2[:], k.dram["cgT"].ap().rearrange("(k p) t -> p k t", p=128))
        evW = pool.dma(d1, Wu[:], k.dram["w_mla_ukv"].ap()[l].rearrange("(k p) n -> p k n", p=128))
        groups = tok_groups(T)
        for h in range(H):
            for gi, (g0, n) in enumerate(groups):
                b = pb.next()
                pe.wait(evA, evW, b.free_evs())
                for kk in range(4):
                    ev = pe.do(lambda t, b=b, kk=kk, h=h, g0=g0, n=n: t.matmul(b.ap[:, 0:n], lhsT=Wu[:, kk, h * 256:h * 256 + 128], rhs=A2[:, kk, g0:g0 + n],
                                                                              start=(kk == 0), stop=(kk == 3)), sig=(kk == 3))
                fmo.evac(b, ev, g0, n, gi == 0, gi == len(groups) - 1, k.dram["ka_nT"].ap()[h], T)
        for i in range(NT):
            for hg in range(4):
                b = pb.next()
                pe.wait(b.free_evs())
                for hh in range(4):
                    c0 = (hg * 4 + hh) * 256 + 128
                    for kk in range(4):
                        ev = pe.do(lambda t, b=b, kk=kk, hh=hh, c0=c0, i=i: t.matmul(b.ap[:, hh * 128:(hh + 1) * 128], lhsT=A2[:, kk, i * 128:(i + 1) * 128], rhs=Wu[:, kk, c0:c0 + 128],
                                                                                     start=(kk == 0), stop=(kk == 3)), sig=(kk == 3 and hh == 3))
                st = vst.next()
                act.wait(ev, st.free_evs())
                e2 = evac_copy(k, act, st.ap[:, :], b.ap[:, :], scale=rstd_all[:, i:i + 1])
                b.readers.append(e2)
                k.store(sp, st, k.dram["va"].ap()[i * 128:(i + 1) * 128, hg * 512:(hg + 1) * 512], st.ap[:, :], e2)
        k.flush()


def na_groups():
    gs = []
    for J in range(4):
        rs = [min(max(r - 4, 0), 24) for r in range(8 * J, 8 * J + 8)]
        lo, hi = min(rs), max(rs) + 8
        kts = []
        for ii, i in enumerate(range(lo // 2, (hi + 1) // 2)):
            if J in (0, 3):
                c0 = 1536 + (0 if J == 0 else 3072) + ii * 512
            else:
                c0 = (7 - 2 * (i - 4 * J) + 4) * 64
            kts.append((i, c0, None))
        kts += [(16, None, None), (17, None, None)]
        gs.append((J * 512, 512, kts))
    return gs


def phase_attn(k, l, br, ident, ntq, rstd_all):
    pe, act, dve, pool, sp = k.pe, k.act, k.dve, k.pool, k.sp
    D_ = k.dram
    ntok_q = ntq * 128
    mla, na = br == 0, br == 1
    with ExitStack() as es:
        slots = []
        for i in range(2):
            s = Slot(None, k.dsem())
            s.QN = k.sb(es, "at_qn%d" % i, [128, T], BF16)
            s.KN = k.sb(es, "at_kn%d" % i, [128, T], BF16)
            s.V = k.sb(es, "at_v%d" % i, [128, NT, 129], BF16)
            if mla:
                s.QR = k.sb(es, "at_qr%d" % i, [128, T], BF16)
            if na:
                s.EE = k.sb(es, "at_ee%d" % i, [128, 7680], BF16)
            slots.append(s)
        EEraw = Slot(k.sb(es, "at_eeraw", [128, 7680], F32), k.dsem()) if na else None
        ev_init = None
        for s in slots:
            ev_init = pool.do(lambda g, s=s: g.memset(s.V[:, :, 128:129], 1.0), sig=True)
        sp.wait(ev_init)
        hring = Ring(slots)
        ev_kr = None
        if mla:
            KRb = k.sb(es, "at_krb", [128, T], BF16)
            sc_all = k.sb(es, "at_sc", [128, NT], F32)
            dk = k.dsem()
            ev_kr = sp.dma(dk, KRb[:], D_["ka_rT"].ap())
            ev_sc = dve.do(lambda v: v.tensor_scalar(out=sc_all[:], in0=rstd_all[:], scalar1=MLA_SCALE, scalar2=None, op0=ALU.mult), sig=True)
            act.wait(ev_sc)
        la = 3 if na else 2
        Sb = Ring([Slot(k.ps(es, "at_s%d" % i, [128, 512], F32)) for i in range(la + 1)])
        Yb = Ring([Slot([k.ps(es, "at_y%d_%d" % (i, j), [128, 512], F32) for j in range(2)]) for i in range(1 if na else 2)])
        Tb = Ring([Slot(k.ps(es, "at_t%d" % i, [128, 512], BF16)) for i in range(1)])
        PT = Ring([Slot(k.sb(es, "at_pt%d" % i, [128, 512], BF16)) for i in range(la + 2)])
        PR = Ring([Slot(k.sb(es, "at_pr%d" % i, [128, 512], BF16)) for i in range(la + 1)]) if na else None
        Yn = Ring([Slot(k.sb(es, "at_yn%d" % i, [128, 4, 128], BF16)) for i in range(2)])
        Rc = Ring([Slot(k.sb(es, "at_rc%d" % i, [128, 4], F32)) for i in range(2)])
        YT = Ring([Slot(k.sb(es, "at_yt%d" % i, [128, T], BF16), k.dsem()) for i in range(2)])

        def load_head(h):
            s = hring.next()
            fe = s.free_evs()
            sp.wait(fe)
            if br == 0:
                base = (h % 2) * 64
                sp.dma(s.dsem, s.QN[:, 0:ntok_q], D_["qa_nT"].ap()[h][:, 0:ntok_q])
                sp.dma(s.dsem, s.QR[base:base + 64, 0:ntok_q], D_["qa_rT"].ap()[h // 2][base:base + 64, 0:ntok_q])
                sp.dma(s.dsem, s.KN[:], D_["ka_nT"].ap()[h])
                vsrc = D_["va"].ap()[:, h * 128:(h + 1) * 128]
            elif br == 1:
                sp.dma(s.dsem, s.QN[:, 0:ntok_q], D_["qbT"].ap()[h][:, 0:ntok_q])
                sp.dma(s.dsem, s.KN[:], D_["kbT"].ap()[h])
                sp.wait(EEraw.free_evs())
                ev_raw = sp.dma(EEraw.dsem, EEraw.ap[:], D_["na_bias"].ap()[l, h])
                vsrc = D_["vb"].ap()[:, h * 128:(h + 1) * 128]
            else:
                sp.dma(s.dsem, s.QN[:, 0:ntok_q], D_["qcT"].ap()[h][:, 0:ntok_q])
                sp.dma(s.dsem, s.KN[:], D_["kcT"].ap()[h // 4])
                vsrc = D_["vc"].ap()[:, (h // 4) * 128:(h // 4 + 1) * 128]
            s.ready = sp.dma(s.dsem, s.V[:, :, 0:128], vsrc.rearrange("(i p) e -> p i e", p=128))
            s.ee_ev = None
            if na:
                s.ee_ev = None
                for c in range(15):
                    ee_chunks.append((s, c, ev_raw, fe))
            return s

        def head_groups():
            gs = []
            if na:
                gs.extend(na_groups())
            else:
                for g in range(4):
                    gs.append((g * 512, 512, [(i, None, None) for i in range(NT)]))
            if ntq == NT:
                gs.append((S, 256, [(16, None, None), (17, None, None)]))
            return gs

        steps = []
        for h in range(H):
            for gidx, (q0, n, kts) in enumerate(head_groups()):
                for ki, (kt, t0, inv) in enumerate(kts):
                    steps.append(dict(h=h, g=gidx, q0=q0, n=n, kt=kt, t0=t0, inv=inv, first=(ki == 0), last=(ki == len(kts) - 1),
                                      lastg=(gidx == len(head_groups()) - 1)))
        hs = {}
        state = dict(Y=None, yt=None)
        scale_c = NA_SCALE if na else GQA_SCALE

        def emit_qk(st):
            h = st["h"]
            if h not in hs:
                hs[h] = load_head(h)
            s = hs[h]
            sb_ = Sb.next()
            st["S"] = sb_
            q0, n, kt = st["q0"], st["n"], st["kt"]
            pe.wait(s.ready, ev_kr, sb_.free_evs())
            if mla:
                base = (h % 2) * 64
                pe.do(lambda t: t.matmul(sb_.ap[:, 0:n], lhsT=s.KN[:, kt * 128:(kt + 1) * 128], rhs=s.QN[:, q0:q0 + n], start=True, stop=False))
                st["qk_ev"] = pe.do(lambda t: t.matmul(sb_.ap[:, 0:n], lhsT=KRb[base:base + 64, kt * 128:(kt + 1) * 128], rhs=s.QR[base:base + 64, q0:q0 + n], start=False, stop=True), sig=True)
            else:
                st["qk_ev"] = pe.do(lambda t: t.matmul(sb_.ap[:, 0:n], lhsT=s.KN[:, kt * 128:(kt + 1) * 128], rhs=s.QN[:, q0:q0 + n], start=True, stop=True), sig=True)

        ee_chunks = []

        def emit_ee_chunk():
            s, c, ev_raw, fe = ee_chunks.pop(0)
            act.wait(ev_raw, fe)
            e = act.do(lambda a: a.activation(out=s.EE[:, c * 512:(c + 1) * 512], in_=EEraw.ap[:, c * 512:(c + 1) * 512], func=AF.Exp), sig=True)
            if c == 14:
                s.ee_ev = e
                EEraw.readers.append(e)

        def emit_exp(st):
            s, sb_, n, kt = hs[st["h"]], st["S"], st["n"], st["kt"]
            while ee_chunks and ee_chunks[0][0] is s:
                emit_ee_chunk()
            pt = PT.next()
            st["PT"] = pt
            if st["t0"] is None:
                act.wait(st["qk_ev"], pt.free_evs())
                sc = sc_all[:, kt:kt + 1] if mla else scale_c
                e = act.do(lambda a: a.activation(out=pt.ap[:, 0:n], in_=sb_.ap[:, 0:n], func=AF.Exp, scale=sc), sig=True)
                sb_.readers.append(e)
                st["pt_ev"] = e
            else:
                pr = PR.next()
                act.wait(st["qk_ev"], pr.free_evs())
                e = act.do(lambda a: a.activation(out=pr.ap[:, 0:n], in_=sb_.ap[:, 0:n], func=AF.Exp, scale=scale_c), sig=True)
                sb_.readers.append(e)
                c0 = st["t0"]
                dve.wait(e, s.ee_ev, pt.free_evs())
                e2 = dve.do(lambda v: v.tensor_tensor(out=pt.ap[:, 0:n], in0=pr.ap[:, 0:n], in1=s.EE[:, c0:c0 + n], op=ALU.mult), sig=True)
                pr.readers.append(e2)
                st["pt_ev"] = e2

        def emit_pv(st):
            s, n, kt, pt = hs[st["h"]], st["n"], st["kt"], st["PT"]
            if st["first"]:
                y = Yb.next()
                state["Y"] = y
                pe.wait(y.free_evs())
            y = state["Y"]
            pe.wait(st["pt_ev"])
            nqs = n // 128
            for qs in range(nqs):
                yap = y.ap[qs // 2][:, (qs % 2) * 256:(qs % 2) * 256 + 129]
                ev = pe.do(lambda t, qs=qs, yap=yap: t.matmul(yap, lhsT=pt.ap[:, qs * 128:(qs + 1) * 128], rhs=s.V[:, kt, :], start=(st["first"] and qs % 2 == 0), stop=st["last"]), sig=(qs == nqs - 1))
            pt.readers.append(ev)
            st["pv_ev"] = ev
            if st["lastg"] and st["last"]:
                s.readers.append(ev)

        def epilogue(st):
            y, n, q0, h = state["Y"], st["n"], st["q0"], st["h"]
            nqs = n // 128
            rc, yn = Rc.next(), Yn.next()
            dve.wait(st["pv_ev"], rc.free_evs(), yn.free_evs())
            for qs in range(nqs):
                e = dve.do(lambda v, qs=qs: v.reciprocal(out=rc.ap[:, qs:qs + 1], in_=y.ap[qs // 2][:, (qs % 2) * 256 + 128:(qs % 2) * 256 + 129]), sig=(qs == nqs - 1))
            dve.wait(e)
            for qs in range(nqs):
                e = dve.do(lambda v, qs=qs: v.tensor_scalar(out=yn.ap[:, qs, :], in0=y.ap[qs // 2][:, (qs % 2) * 256:(qs % 2) * 256 + 128], scalar1=rc.ap[:, qs:qs + 1], scalar2=None, op0=ALU.mult), sig=(qs == nqs - 1))
            y.readers.append(e)
            rc.readers.append(e)
            if st["g"] == 0:
                state["yt"] = YT.next()
                state["yt_fe"] = state["yt"].free_evs()
                state["yt_evs"] = []
            yt, yt_fe, yt_evs = state["yt"], state["yt_fe"], state["yt_evs"]
            lastg = st["lastg"]

            def pe_part():
                tb = Tb.next()
                pe.wait(e, tb.free_evs())
                for qs in range(nqs):
                    ep = pe.do(lambda t, qs=qs: t.transpose(out=tb.ap[:, qs * 128:(qs + 1) * 128], in_=yn.ap[:, qs, :], identity=ident[:]), sig=(qs == nqs - 1))
                yn.readers.append(ep)
                dve.wait(ep, yt_fe)
                ec = dve.do(lambda v: v.tensor_copy(out=yt.ap[:, q0:q0 + n], in_=tb.ap[:, 0:n]), sig=True)
                tb.readers.append(ec)
                yt_evs.append(ec)
                if lastg:
                    k.store(sp, yt, D_["yT"].ap()[br, h][:, 0:ntok_q], yt.ap[:, 0:ntok_q], list(yt_evs))
            return pe_part

        deferred = []
        for si in range(min(la, len(steps))):
            emit_qk(steps[si])
        for si, st in enumerate(steps):
            emit_exp(st)
            if si + la < len(steps):
                emit_qk(steps[si + la])
            emit_pv(st)
            if ee_chunks:
                emit_ee_chunk()
            deferred = [(c - 1, f) for (c, f) in deferred]
            for c, f in [d for d in deferred if d[0] <= 0]:
                f()
            deferred = [d for d in deferred if d[0] > 0]
            if st["last"]:
                deferred.append((la + 1, epilogue(st)))
            if st["first"] and st["g"] == 0 and st["h"] + 1 < H and (st["h"] + 1) not in hs:
                hs[st["h"] + 1] = load_head(st["h"] + 1)
        for c, f in deferred:
            f()
        k.flush()


def phase_merge(k, l, ntq):
    pe, act, dve, pool, sp = k.pe, k.act, k.dve, k.pool, k.sp
    D_ = k.dram
    ngt = ntq // 2
    ntg = ngt * 128
    WB = D_["w_branch"].ap()[l]
    with ExitStack() as es:
        A3 = k.sb(es, "mg_A", [128, 48, ntg], BF16)
        ws = Ring([Slot(k.sb(es, "mg_w%d" % i, [128, 3, 16, 256], BF16), k.dsem()) for i in range(2)])
        gs = Ring([Slot(k.sb(es, "mg_g%d" % i, [128, 3, ntg], BF16), k.dsem()) for i in range(2)])
        accs = Ring([Slot(k.sb(es, "mg_acc%d" % i, [128, ntg], F32)) for i in range(2)])
        tmps = Ring([Slot(k.sb(es, "mg_tmp%d" % i, [128, 512], F32)) for i in range(3)])
        outs = Ring([Slot(k.sb(es, "mg_o%d" % i, [128, ntg], BF16), k.dsem()) for i in range(2)])
        pb = Ring([Slot(k.ps(es, "mg_p%d" % i, [128, 512], F32)) for i in range(6)])
        dA = k.dsem()
        a_readers = []
        groups = tok_groups(ntg)

        def wload(cs):
            s = ws.next()
            pool.wait(s.free_evs())
            for i in range(3):
                s.ready = pool.dma(s.dsem, s.ap[:, i, :, :], WB[i][:, cs * 256:(cs + 1) * 256].rearrange("(k p) n -> p k n", p=128))
            return s

        for tg in range(2):
            tok0 = tg * ntg
            sp.wait(a_readers)
            a_readers = []
            for i in range(3):
                evA = sp.dma(dA, A3[:, i * 16:(i + 1) * 16, :], D_["yT"].ap()[i][:, :, tok0:tok0 + ntg].rearrange("h p t -> p h t"))
            nxt = wload(0)
            for cs in range(8):
                cur = nxt
                if cs + 1 < 8:
                    nxt = wload(cs + 1)
                for cc in range(2):
                    c = cs * 2 + cc
                    g = gs.next()
                    sp.wait(g.free_evs())
                    g.ready = sp.dma(g.dsem, g.ap[:], D_["gateT"].ap()[:, c * 128:(c + 1) * 128, tok0:tok0 + ntg].rearrange("i p t -> p i t"))
                    acc, o = accs.next(), outs.next()
                    acc_fe, o_fe = acc.free_evs(), o.free_evs()
                    last = {}
                    oevs = []
                    for i in range(3):
                        for (g0, n) in groups:
                            b = pb.next()
                            pe.wait(cur.ready, evA, b.free_evs())
                            for hc in range(16):
                                ev = pe.do(lambda t, b=b, i=i, hc=hc, cc=cc, g0=g0, n=n, cur=cur: t.matmul(b.ap[:, 0:n], lhsT=cur.ap[:, i, hc, cc * 128:(cc + 1) * 128], rhs=A3[:, i * 16 + hc, g0:g0 + n],
                                                                                                          start=(hc == 0), stop=(hc == 15)), sig=(hc == 15))
                            cur.readers.append(ev)
                            a_readers.append(ev)
                            if i == 0:
                                dve.wait(ev, g.ready, acc_fe)
                                e = dve.do(lambda v, b=b, g=g, acc=acc, g0=g0, n=n: v.tensor_tensor(out=acc.ap[:, g0:g0 + n], in0=b.ap[:, 0:n], in1=g.ap[:, 0, g0:g0 + n], op=ALU.mult), sig=True)
                                b.readers.append(e)
                                last[g0] = e
                            else:
                                tmp = tmps.next()
                                dve.wait(ev, g.ready, tmp.free_evs())
                                e = dve.do(lambda v, b=b, g=g, tmp=tmp, i=i, g0=g0, n=n: v.tensor_tensor(out=tmp.ap[:, 0:n], in0=b.ap[:, 0:n], in1=g.ap[:, i, g0:g0 + n], op=ALU.mult), sig=True)
                                b.readers.append(e)
                                pool.wait(e, last[g0])
                                if i == 1:
                                    e2 = pool.do(lambda p_, acc=acc, tmp=tmp, g0=g0, n=n: p_.tensor_tensor(out=acc.ap[:, g0:g0 + n], in0=acc.ap[:, g0:g0 + n], in1=tmp.ap[:, 0:n], op=ALU.add), sig=True)
                                    last[g0] = e2
                                else:
                                    pool.wait(o_fe)
                                    e2 = pool.do(lambda p_, acc=acc, tmp=tmp, o=o, g0=g0, n=n: p_.tensor_tensor(out=o.ap[:, g0:g0 + n], in0=acc.ap[:, g0:g0 + n], in1=tmp.ap[:, 0:n], op=ALU.add), sig=True)
                                    oevs.append(e2)
                                    acc.readers.append(e2)
                                tmp.readers.append(e2)
                    g.readers.append(e)
                    k.store(sp, o, D_["accT"].ap()[c * 128:(c + 1) * 128, tok0:tok0 + ntg], o.ap[:, :], oevs)
        k.flush()


def phase_tm_out(k, l, which, ntq):
    pe, act, dve, pool, sp = k.pe, k.act, k.dve, k.pool, k.sp
    D_ = k.dram
    with ExitStack() as es:
        if which == "out":
            KC, wc, Wd, ngroups, ngt = 16, 512, D_["w_out"].ap()[l], 1, ntq
            Asrc = D_["accT"].ap()
        else:
            KC, wc, Wd, ngroups, ngt = NFC, 256, D_["w_down"].ap()[l], 2, ntq // 2
            Asrc = D_["hidT"].ap()
        ntg = ngt * 128
        A = k.sb(es, "to_A", [128, KC, ntg], BF16)
        pj = Proj(k, es, A, KC, wc, 2, 4, "to")
        stg = Ring([Slot(k.sb(es, "to_st%d" % i, [128, 512], F32), k.dsem()) for i in range(3)])
        dA = k.dsem()
        tog = [0]
        a_readers = []
        for tg in range(ngroups):
            tok0 = tg * ntg
            sp.wait(a_readers)
            evA = sp.dma(dA, A[:], Asrc[:, tok0:tok0 + ntg].rearrange("(k p) t -> p k t", p=128))

            def mk_run(c0, evA=evA, tok0=tok0):
                def run(slab):
                    for i in range(ngt):
                        b, ev = pj.tm_tile(slab, wc, i, extra_wait=evA)
                        a_readers.append(ev)
                        st = stg.next()
                        eng = act if tog[0] % 2 == 0 else dve
                        tog[0] += 1
                        eng.wait(ev, st.free_evs())
                        e2 = evac_copy(k, eng, st.ap[:, 0:wc], b.ap[:, 0:wc])
                        b.readers.append(e2)
                        k.store(sp, st, D_["fout"].ap()[tok0 + i * 128:tok0 + (i + 1) * 128, c0:c0 + wc], st.ap[:, 0:wc], e2)
                return run

            for c0 in range(0, D, wc):
                pj.add("tm", (lambda s, c0=c0: (s.ap[:, :, 0:wc], Wd[:, c0:c0 + wc].rearrange("(k p) n -> p k n", p=128))), mk_run(c0))
            pj.run()
        k.flush()


def phase_resid(k, l, sub, src, ntiles, final):
    pe, act, dve, pool, sp = k.pe, k.act, k.dve, k.pool, k.sp
    D_ = k.dram
    mv_t = D_["mvec"]
    with ExitStack() as es:
        d0 = k.dsem()
        gate = {}
        for j, nm in ((0, "lat"), (1, "ctx")):
            gate[j] = k.sb(es, "rs_gate" + nm, [128, 2048], F32)
            sp.dma(d0, gate[j][:], bc_ap(mv_t, (l * 2 + j) * 12288 + (2 + 3 * sub) * 2048, 2048))
        lng = k.sb(es, "rs_lng", [128, 2048], F32)
        lnb = k.sb(es, "rs_lnb", [128, 2048], F32)
        sp.dma(d0, lng[:], bc_ap(D_["ln_f_g" if sub else "ln_a_g"], l * 2048, 2048))
        evc = sp.dma(d0, lnb[:], bc_ap(D_["ln_f_b" if sub else "ln_a_b"], l * 2048, 2048))
        xr = Ring([Slot(k.sb(es, "rs_x%d" % i, [128, 2048], F32), k.dsem()) for i in range(2)])
        fr = Ring([Slot(k.sb(es, "rs_f%d" % i, [128, 2048], F32), k.dsem()) for i in range(2)])
        orr = Ring([Slot(k.sb(es, "rs_o%d" % i, [128, 2048], F32), k.dsem()) for i in range(2)])
        sm = Ring([Slot(k.sb(es, "rs_sm%d" % i, [128, 32], F32)) for i in range(2)])
        fout = D_["fout"].ap()

        def load(i):
            xs, fs = xr.next(), fr.next()
            sp.wait(xs.free_evs(), fs.free_evs())
            xs.ready = sp.dma(xs.dsem, xs.ap[:], src(i))
            fs.ready = sp.dma(fs.dsem, fs.ap[:], fout[i * 128:(i + 1) * 128, :])
            return xs, fs

        nxt = load(0)
        for i in range(ntiles):
            xs, fs = nxt
            if i + 1 < ntiles:
                nxt = load(i + 1)
            gt = gate[0 if i < NLT else 1]
            dve.wait(xs.ready, fs.ready, evc)
            e = dve.do(lambda v, fs=fs, gt=gt: v.tensor_tensor(out=fs.ap[:], in0=fs.ap[:], in1=gt[:], op=ALU.mult), sig=True)
            dve.wait(e)
            e = dve.do(lambda v, xs=xs, fs=fs: v.scalar_tensor_tensor(out=xs.ap[:], in0=xs.ap[:], scalar=ALPHA, in1=fs.ap[:], op0=ALU.mult, op1=ALU.add), sig=True)
            smt = sm.next()
            st = smt.ap
            stats = st[:, 0:24].rearrange("p (c s) -> p c s", s=6)
            mv, rstd, nmr = st[:, 24:26], st[:, 26:27], st[:, 27:28]
            dve.wait(e, smt.free_evs())
            e = ln_stats(k, xs.ap, mv, rstd, nmr, POST_EPS, stats)
            act.wait(e)
            e = act.do(lambda a, xs=xs, fs=fs, rstd=rstd, nmr=nmr: a.activation(out=fs.ap[:], in_=xs.ap[:], func=AF.Identity, bias=nmr, scale=rstd), sig=True)
            xs.readers.append(e)
            smt.readers.append(e)
            dve.wait(e)
            e = dve.do(lambda v, fs=fs: v.tensor_tensor(out=fs.ap[:], in0=fs.ap[:], in1=lng[:], op=ALU.mult), sig=True)
            os_ = orr.next()
            pool.wait(e, os_.free_evs())
            e = pool.do(lambda g, fs=fs, os_=os_: g.tensor_tensor(out=os_.ap[:], in0=fs.ap[:], in1=lnb[:], op=ALU.add), sig=True)
            fs.readers.append(e)
            if final:
                if i < NLT:
                    k.store(sp, os_, k.out.ap()[i * 128:(i + 1) * 128, :], os_.ap[:], e)
            else:
                k.store(sp, os_, D_["xres"].ap()[i * 128:(i + 1) * 128, :], os_.ap[:], e)
        k.flush()


def phase_ffn_up(k, l, hT, ntq):
    pe, act, dve, pool, sp = k.pe, k.act, k.dve, k.pool, k.sp
    D_ = k.dram
    Wu = D_["w_up"].ap()[l]
    ntok = ntq * 128
    has_ctx = ntq == NT
    GW = T + 4
    with ExitStack() as es:
        pj = Proj(k, es, hT, 16, 512, 2, 4, "fu")
        Gb = Ring([Slot(k.sb(es, "fu_g%d" % i, [128, GW], F32)) for i in range(2)])
        Vb = Ring([Slot(k.sb(es, "fu_v%d" % i, [128, T], F32)) for i in range(2)])
        Cb = Ring([Slot(k.sb(es, "fu_c%d" % i, [128, T], F32)) for i in range(2)])
        Ho = Ring([Slot(k.sb(es, "fu_h%d" % i, [128, T], BF16), k.dsem()) for i in range(2)])
        cp = k.sb(es, "fu_cp", [128, NFC, 4], F32)
        d0 = k.dsem()
        evcp = sp.dma(d0, cp[:], D_["convp"].ap()[l])
        ev_ms = None
        for s in Gb.slots:
            ev_ms = pool.do(lambda g, s=s: g.memset(s.ap[:], 0.0), sig=True)
        groups = tok_groups(ntok)
        segs = [(0, S, 0)] + ([(S, CL, S + 2)] if has_ctx else [])

        def gcol(g0):
            return 1 + g0 if g0 < S else S + 3 + (g0 - S)

        def mk_run(sp_):
            def run(slab):
                for cc in range(2):
                    ch = sp_ * 2 + cc
                    gb, vb, cb, ho = Gb.next(), Vb.next(), Cb.next(), Ho.next()
                    gfe, vfe = gb.free_evs(), vb.free_evs()
                    gevs, vevs = [], []
                    for (g0, n) in groups:
                        b, ev = pj.fm_group(slab, cc, g0, n)
                        act.wait(ev, gfe, ev_ms)
                        e = evac_copy(k, act, gb.ap[:, gcol(g0):gcol(g0) + n], b.ap[:, 0:n])
                        b.readers.append(e)
                        gevs.append(e)
                    for (g0, n) in groups:
                        b, ev = pj.fm_group(slab, 2 + cc, g0, n)
                        act.wait(ev, vfe)
                        e = evac_copy(k, act, vb.ap[:, g0:g0 + n], b.ap[:, 0:n])
                        b.readers.append(e)
                        vevs.append(e)
                    w = [cp[:, ch, j:j + 1] for j in range(4)]
                    dve.wait(gevs, evcp, cb.free_evs())
                    for (t0, n, c0) in segs:
                        e = dve.do(lambda v, gb=gb, cb=cb, t0=t0, n=n, c0=c0, w=w: v.tensor_scalar(out=cb.ap[:, t0:t0 + n], in0=gb.ap[:, c0:c0 + n], scalar1=w[0], scalar2=w[3], op0=ALU.mult, op1=ALU.add), sig=True)
                    for j in (1, 2):
                        dve.wait(e)
                        for (t0, n, c0) in segs:
                            e = dve.do(lambda v, gb=gb, cb=cb, t0=t0, n=n, c0=c0, w=w, j=j: v.scalar_tensor_tensor(out=cb.ap[:, t0:t0 + n], in0=gb.ap[:, c0 + j:c0 + j + n], scalar=w[j], in1=cb.ap[:, t0:t0 + n],
                                                                                                                     op0=ALU.mult, op1=ALU.add), sig=True)
                    gb.readers.append(e)
                    act.wait(e)
                    e = act.do(lambda a, cb=cb: a.activation(out=cb.ap[:, 0:ntok], in_=cb.ap[:, 0:ntok], func=AF.Silu), sig=True)
                    pool.wait(e, vevs, ho.free_evs())
                    e = pool.do(lambda g, cb=cb, vb=vb, ho=ho: g.tensor_tensor(out=ho.ap[:, 0:ntok], in0=cb.ap[:, 0:ntok], in1=vb.ap[:, 0:ntok], op=ALU.mult), sig=True)
                    cb.readers.append(e)
                    vb.readers.append(e)
                    k.store(sp, ho, D_["hidT"].ap()[ch * 128:(ch + 1) * 128, 0:ntok], ho.ap[:, 0:ntok], e)
            return run

        for sp_ in range(NFC // 2):
            pj.add("fm", (lambda s, sp_=sp_: [(s.ap[:, :, 0:256], Wu[:, sp_ * 256:(sp_ + 1) * 256].rearrange("(k p) n -> p k n", p=128)),
                                              (s.ap[:, :, 256:512], Wu[:, DFF + sp_ * 256:DFF + (sp_ + 1) * 256].rearrange("(k p) n -> p k n", p=128))]), mk_run(sp_))
        pj.run()
        k.flush()
```

```python
import numpy as np
from contextlib import ExitStack
import concourse.bass as bass
import concourse.mybir as mybir
from concourse.bass_utils import run_bass_kernel_spmd

F32 = mybir.dt.float32
BF16 = mybir.dt.bfloat16
AF = mybir.ActivationFunctionType
ALU = mybir.AluOpType

D = 2048
S = 2048
CL = 256
T = S + CL
NT = T // 128
NLT = S // 128
L = 2
H = 16
DFF = 5632
NFC = DFF // 128
N_IN = 19008
O_MQ, O_CKV, O_KR, O_NA, O_GQ, O_GKV, O_GATE = 0, 3072, 3584, 3648, 9792, 11840, 12864
ALPHA = (2 * L) ** 0.25
MLA_SCALE = 192 ** -0.5
NA_SCALE = 128 ** -0.5
GQA_SCALE = 128 ** -0.5
ADA_EPS, POST_EPS, RMS_EPS = 1e-6, 1e-5, 1e-6
NEG = -1.0e4


class DSem:
    def __init__(self, h, key):
        self.h, self.key, self.cnt = h, key, 0


class Eng:
    def __init__(self, name, sem):
        self.name, self.sem, self.cnt, self.ops, self.waited = name, sem, 0, [], {}

    def wait(self, *evs):
        for ev in evs:
            if ev is None:
                continue
            if isinstance(ev, list):
                self.wait(*ev)
                continue
            key, sem, val = ev
            if self.waited.get(key, 0) >= val:
                continue
            self.waited[key] = val
            self.ops.append(lambda e, sem=sem, val=val: e.wait_ge(sem, val))

    def do(self, fn, sig=False):
        if sig:
            self.cnt += 1
            n, sem = self.cnt, self.sem
            self.ops.append(lambda e: fn(e).then_inc(sem, 1))
            return (self.name, sem, n)
        self.ops.append(fn)
        return None

    def dma(self, dsem, out, in_):
        dsem.cnt += 16
        n, h = dsem.cnt, dsem.h
        self.ops.append(lambda e: e.dma_start(out=out, in_=in_).then_inc(h, 16))
        return (dsem.key, h, n)


class Slot:
    def __init__(self, ap, dsem=None):
        self.ap, self.dsem = ap, dsem
        self.ready = None
        self.readers = []

    def free_evs(self):
        r = self.readers
        self.readers = []
        return r


class Ring:
    def __init__(self, slots):
        self.slots, self.i = slots, 0

    def next(self):
        s = self.slots[self.i % len(self.slots)]
        self.i += 1
        return s


class K:
    def __init__(self, dbg=(), stop_after=None):
        self.dbg, self.stop_after = set(dbg), stop_after
        self.branches = (0, 1, 2)
        self.nc = nc = bass.Bass("TRN2", target_bir_lowering=False)
        self.es = ExitStack()
        self.ds_pool, self.ds_i = [], 0
        self.uid = 0
        self.engs = {}
        for n in ("pe", "act", "dve", "pool", "sp"):
            self.engs[n] = Eng(n, self.es.enter_context(nc.semaphore("sem_" + n)))
        self.pe, self.act, self.dve, self.pool, self.sp = (self.engs[n] for n in ("pe", "act", "dve", "pool", "sp"))
        self.dram = {}
        self.pending = []

    def inp(self, name, shape, dt=F32):
        t = self.nc.dram_tensor(name, list(shape), dt, kind="ExternalInput")
        self.dram[name] = t
        return t

    def scr(self, name, shape, dt=BF16):
        if name in self.dbg:
            t = self.nc.dram_tensor(name, list(shape), dt, kind="ExternalOutput")
        else:
            t = self.nc.dram_tensor(name, list(shape), dt)
        self.dram[name] = t
        return t

    def dsem(self):
        if self.ds_i >= len(self.ds_pool):
            n = len(self.ds_pool)
            self.ds_pool.append(DSem(self.es.enter_context(self.nc.semaphore("ds%d" % n)), "ds%d" % n))
        d = self.ds_pool[self.ds_i]
        self.ds_i += 1
        return d

    def sb(self, es, name, shape, dt):
        self.uid += 1
        return es.enter_context(self.nc.sbuf_tensor("%s_%d" % (name, self.uid), list(shape), dt))

    def ps(self, es, name, shape, dt):
        self.uid += 1
        return es.enter_context(self.nc.psum_tensor("%s_%d" % (name, self.uid), list(shape), dt))

    def flush(self, waiter=None):
        w = waiter or self.sp
        w.wait(self.pending)
        self.pending = []
        with self.nc.Block() as block:
            ops = {n: list(e.ops) for n, e in self.engs.items()}

            @block.tensor
            def _(t):
                for f in ops["pe"]:
                    f(t)

            @block.scalar
            def _(a):
                for f in ops["act"]:
                    f(a)

            @block.vector
            def _(v):
                for f in ops["dve"]:
                    f(v)

            @block.gpsimd
            def _(g):
                for f in ops["pool"]:
                    f(g)

            @block.sync
            def _(s):
                for f in ops["sp"]:
                    f(s)
        for e in self.engs.values():
            e.ops = []
        self.ds_i = 0

    def store(self, eng, slot, out, in_, after):
        eng.wait(after)
        ev = eng.dma(slot.dsem, out, in_)
        slot.readers.append(ev)
        self.pending.append(ev)
        return ev


def bc_ap(t, off, n, parts=128):
    return bass.AP(t, off, [[0, parts], [1, n]])


def ada_steps(k, es, l, nbanks=2, sw=512):
    pe, act, dve, pool, sp = k.pe, k.act, k.dve, k.pool, k.sp
    csb = k.sb(es, "ada_c", [128, 32], F32)
    scT = k.sb(es, "ada_sc", [128, 16, 2], BF16)
    bs = Ring([Slot(k.sb(es, "ada_b%d" % i, [2, sw], F32), k.dsem()) for i in range(2)])
    ms = Ring([Slot(k.sb(es, "ada_m%d" % i, [2, sw], F32), k.dsem()) for i in range(2)])
    ws = Ring([Slot(k.sb(es, "ada_w%d" % i, [128, 16, sw], BF16), k.dsem()) for i in range(2)])
    pb = Ring([Slot(k.ps(es, "ada_p%d" % i, [128, 512], F32)) for i in range(nbanks)])
    d0 = k.dsem()
    e2 = sp.dma(d0, csb[:], k.dram["cT"].ap().rearrange("p k j -> p (k j)"))
    act.wait(e2)
    ev_sc = act.do(lambda a: a.activation(out=scT[:].rearrange("p k j -> p (k j)"), in_=csb[:], func=AF.Silu), sig=True)
    wada = k.dram["w_ada"].ap()
    mv = k.dram["mvec"].ap()

    def load(n):
        s = ws.next()
        pool.wait(s.free_evs())
        s.ready = pool.dma(s.dsem, s.ap[:], wada[l, :, n * sw:(n + 1) * sw].rearrange("(k p) n -> p k n", p=128))
        b = bs.next()
        sp.wait(b.free_evs())
        b.ready = sp.dma(b.dsem, b.ap[:], bc_ap(k.dram["b_ada"], l * 12288 + n * sw, sw, parts=2))
        return s, b

    def mm(b, cur, kk):
        return lambda t: t.matmul(b.ap[0:2, 0:sw], lhsT=scT[:, kk, :], rhs=cur.ap[:, kk, :], start=(kk == 0), stop=(kk == 15))

    ns = 12288 // sw
    nxt = load(0)
    for n in range(ns):
        cur, bsl = nxt
        if n + 1 < ns:
            nxt = load(n + 1)
        b = pb.next()
        pe.wait(cur.ready, ev_sc, b.free_evs())
        for kk in range(16):
            ev = pe.do(mm(b, cur, kk), sig=(kk == 15))
        cur.readers.append(ev)
        m = ms.next()
        dve.wait(ev, bsl.ready, m.free_evs())
        ev2 = dve.do(lambda v, b=b, m=m, bsl=bsl: v.tensor_tensor(out=m.ap[:], in0=b.ap[0:2, 0:sw], in1=bsl.ap[:], op=ALU.add), sig=True)
        b.readers.append(ev2)
        bsl.readers.append(ev2)
        k.store(sp, m, mv[l][:, n * sw:(n + 1) * sw], m.ap[:], ev2)
        yield n


def phase_ada(k, l):
    with ExitStack() as es:
        for _ in ada_steps(k, es, l):
            pass
        k.flush()


def ln_stats(k, xin_ap, mv, rstd, nmr, eps, stats):
    dve, act = k.dve, k.act
    for c in range(4):
        ev = dve.do(lambda v, c=c: v.bn_stats(out=stats[:, c, :], in_=xin_ap[:, c * 512:(c + 1) * 512]), sig=(c == 3))
    dve.wait(ev)
    ev = dve.do(lambda v: v.bn_aggr(out=mv, in_=stats), sig=True)
    act.wait(ev)
    ev = act.do(lambda a: a.activation(out=rstd, in_=mv[:, 1:2], func=AF.Sqrt, bias=eps, scale=1.0), sig=True)
    dve.wait(ev)
    ev = dve.do(lambda v: v.reciprocal(out=rstd, in_=rstd), sig=True)
    dve.wait(ev)
    return dve.do(lambda v: v.tensor_scalar(out=nmr, in0=mv[:, 0:1], scalar1=-1.0, scalar2=rstd, op0=ALU.mult, op1=ALU.mult), sig=True)


def phase_mod(k, l, sub, src, hT, ident, ntiles):
    pe, act, dve, pool, sp = k.pe, k.act, k.dve, k.pool, k.sp
    mv_t = k.dram["mvec"]
    with ExitStack() as es:
        bcs = {}
        d0 = k.dsem()
        evb = None
        for j, nm in ((0, "lat"), (1, "ctx")):
            sh = k.sb(es, "mod_sh" + nm, [128, 2048], F32)
            s1 = k.sb(es, "mod_s1" + nm, [128, 2048], F32)
            base = (l * 2 + j) * 12288 + sub * 3 * 2048
            sp.dma(d0, sh[:], bc_ap(mv_t, base, 2048))
            evb = sp.dma(d0, s1[:], bc_ap(mv_t, base + 2048, 2048))
            bcs[j] = (sh, s1)
        dve.wait(evb)
        for j in (0, 1):
            s1 = bcs[j][1]
            evb2 = dve.do(lambda v, s1=s1: v.tensor_scalar(out=s1[:], in0=s1[:], scalar1=1.0, scalar2=None, op0=ALU.add), sig=True)
        xr = Ring([Slot(k.sb(es, "mod_x%d" % i, [128, 2048], F32), k.dsem()) for i in range(2)])
        xn = Ring([Slot(k.sb(es, "mod_xn%d" % i, [128, 2048], F32)) for i in range(2)])
        hb = Ring([Slot(k.sb(es, "mod_hb%d" % i, [128, 2048], BF16)) for i in range(3)])
        sm = Ring([Slot(k.sb(es, "mod_sm%d" % i, [128, 32], F32)) for i in range(2)])
        pt = Ring([Slot(k.ps(es, "mod_pt%d" % i, [128, 512], BF16)) for i in range(4)])

        def load(i):
            s = xr.next()
            sp.wait(s.free_evs())
            s.ready = sp.dma(s.dsem, s.ap[:], src(i))
            return s

        def emit_tr(hs, ev, i):
            for jj in range(4):
                p = pt.next()
                pe.wait(ev, p.free_evs())
                for a in range(4):
                    kk = 4 * jj + a
                    evp = pe.do(lambda t, p=p, hs=hs, kk=kk, a=a: t.transpose(out=p.ap[:, a * 128:(a + 1) * 128], in_=hs.ap[:, kk * 128:(kk + 1) * 128], identity=ident[:]), sig=(a == 3))
                eng = act if jj % 2 == 0 else dve
                eng.wait(evp)
                evc = evac_copy(k, eng, hT[:, 4 * jj:4 * jj + 4, i * 128:(i + 1) * 128], p.ap[:].rearrange("p (a b) -> p a b", a=4))
                p.readers.append(evc)
            hs.readers.append(evp)

        pend = []
        nxt = load(0)
        for i in range(ntiles):
            cur = nxt
            if i + 1 < ntiles:
                nxt = load(i + 1)
            j = 0 if i < NLT else 1
            sh, s1 = bcs[j]
            smt = sm.next()
            st = smt.ap
            stats = st[:, 0:24].rearrange("p (c s) -> p c s", s=6)
            mv, rstd, nmr = st[:, 24:26], st[:, 26:27], st[:, 27:28]
            dve.wait(cur.ready, smt.free_evs())
            ev = ln_stats(k, cur.ap, mv, rstd, nmr, ADA_EPS, stats)
            xs = xn.next()
            act.wait(ev, xs.free_evs())
            ev = act.do(lambda a, xs=xs, cur=cur, rstd=rstd, nmr=nmr: a.activation(out=xs.ap[:], in_=cur.ap[:], func=AF.Identity, bias=nmr, scale=rstd), sig=True)
            cur.readers.append(ev)
            smt.readers.append(ev)
            dve.wait(ev, evb2)
            ev = dve.do(lambda v, xs=xs, s1=s1: v.tensor_tensor(out=xs.ap[:], in0=xs.ap[:], in1=s1[:], op=ALU.mult), sig=True)
            hs = hb.next()
            pool.wait(ev, hs.free_evs())
            ev = pool.do(lambda g, xs=xs, hs=hs, sh=sh: g.tensor_tensor(out=hs.ap[:], in0=xs.ap[:], in1=sh[:], op=ALU.add), sig=True)
            xs.readers.append(ev)
            pend.append((hs, ev, i))
            if len(pend) > 1:
                emit_tr(*pend.pop(0))
        while pend:
            emit_tr(*pend.pop(0))
        k.flush()


def declare(k):
    k.inp("x", [S, D])
    k.inp("ctx", [CL, D])
    k.inp("cT", [128, 16, 2])
    k.inp("w_ada", [L, D, 6 * D])
    k.inp("b_ada", [L, 6 * D])
    k.inp("w_in", [L, D, N_IN])
    k.inp("mla_kv_norm", [L, 512])
    k.inp("w_mla_ukv", [L, 512, 4096])
    k.inp("gqa_q_norm", [L, 128])
    k.inp("gqa_k_norm", [L, 128])
    k.inp("na_bias", [L, H, 128, 7680])
    k.inp("w_branch", [L, 3, D, D])
    k.inp("w_out", [L, D, D])
    k.inp("ln_a_g", [L, D])
    k.inp("ln_a_b", [L, D])
    k.inp("w_up", [L, D, 2 * DFF])
    k.inp("convp", [L, 128, NFC, 4])
    k.inp("w_down", [L, DFF, D])
    k.inp("ln_f_g", [L, D])
    k.inp("ln_f_b", [L, D])
    k.inp("ropeM", [2, 128, NLT, 64])
    k.inp("ropeG", [2, 128, NLT, 128])
    k.out = k.nc.dram_tensor("out", [S, D], F32, kind="ExternalOutput")
    k.scr("mvec", [L, 2, 6 * D], F32)
    k.scr("xres", [T, D], F32)
    k.scr("fout", [T, D], F32)
    k.scr("hT_dbg", [D, T], BF16)
    k.scr("qa_nT", [H, 128, T])
    k.scr("qa_rT", [H // 2, 128, T])
    k.scr("ka_nT", [H, 128, T])
    k.scr("ka_rT", [128, T])
    k.scr("cgT", [512, T])
    k.scr("va", [T, D])
    k.scr("qbT", [H, 128, T])
    k.scr("kbT", [H, 128, T])
    k.scr("vb", [T, D])
    k.scr("qcT", [H, 128, T])
    k.scr("kcT", [4, 128, T])
    k.scr("vc", [T, 512])
    k.scr("gateT", [3, D, T])
    k.scr("yT", [3, H, 128, T])
    k.scr("accT", [D, T])
    k.scr("hidT", [DFF, T])
    k.scr("rstd_dbg", [128, NT], F32)


def make_ident(k, es):
    ident = k.sb(es, "ident", [128, 128], BF16)
    k.pool.do(lambda g: g.memset(ident[:], 0.0))
    k.pool.do(lambda g: g.affine_select(out=ident[:], in_=ident[:], pattern=[[-1, 128]], compare_op=ALU.not_equal, fill=1.0, base=0, channel_multiplier=1))
    return ident


def build(dbg=(), stop_after=None):
    k = K(dbg, stop_after)
    declare(k)
    sp = k.sp
    with ExitStack() as es:
        ident = make_ident(k, es)
        k.flush()
        for l in range(L):
            phase_ada(k, l)
        if stop_after == "ada":
            return k
        xin, cin, xres = k.dram["x"].ap(), k.dram["ctx"].ap(), k.dram["xres"].ap()
        for l in range(L):
            ntq = NT if l < L - 1 else NLT
            if l == 0:
                src = lambda i: (xin[i * 128:(i + 1) * 128, :] if i < NLT else cin[(i - NLT) * 128:(i - NLT + 1) * 128, :])
            else:
                src = lambda i: xres[i * 128:(i + 1) * 128, :]
            with ExitStack() as es2:
                hT = k.sb(es2, "hT", [128, 16, T], BF16)
                phase_mod(k, l, 0, src, hT, ident, NT)
                if "hT_dbg" in k.dbg and l == 0:
                    st = Slot(None, k.dsem())
                    k.store(sp, st, k.dram["hT_dbg"].ap().rearrange("(k p) t -> p k t", p=128), hT[:], None)
                    k.flush()
                if stop_after == "mod":
                    return k
                rms_all = k.sb(es2, "rms_all", [128, NT], F32)
                rstd_all = k.sb(es2, "rstd_all", [128, NT], F32)
                phase_win(k, l, hT, ident, ntq, rms_all, rstd_all)
                if stop_after == "win":
                    return k
                phase_ukv(k, l, rstd_all)
                if stop_after == "ukv":
                    return k
                for br in k.branches:
                    phase_attn(k, l, br, ident, ntq, rstd_all)
                if stop_after == "attn":
                    return k
            phase_merge(k, l, ntq)
            if stop_after == "merge":
                return k
            phase_out_resid(k, l, src, ntq)
            if stop_after == "mixer":
                return k
            srcx = lambda i: xres[i * 128:(i + 1) * 128, :]
            with ExitStack() as es2:
                hT = k.sb(es2, "hT", [128, 16, T], BF16)
                phase_mod(k, l, 1, srcx, hT, ident, ntq)
                phase_ffn_up(k, l, hT, ntq)
            if stop_after == "ffnup":
                return k
            phase_tm_out(k, l, "down", ntq)
            phase_resid(k, l, 1, srcx, ntq, l == L - 1)
            if stop_after == "layer%d" % l:
                return k
    return k


def host_inputs(inputs):
    f32 = np.float32
    common = {}
    for n in ("w_ada", "b_ada", "w_in", "mla_kv_norm", "w_mla_ukv", "gqa_q_norm", "gqa_k_norm", "w_branch",
              "w_out", "ln_a_g", "ln_a_b", "w_up", "w_down", "ln_f_g", "ln_f_b"):
        common[n] = np.ascontiguousarray(inputs[n], dtype=f32)
    cw = np.asarray(inputs["conv_w"], f32)
    cb = np.asarray(inputs["conv_b"], f32)
    cp = np.concatenate([cw, cb[:, None, :]], axis=1)
    common["convp"] = np.ascontiguousarray(cp.reshape(L, 4, NFC, 128).transpose(0, 3, 2, 1))
    rpb = np.asarray(inputs["na_rpb"], f32)
    a_ = np.arange(2)[:, None, None, None]
    kc = np.arange(64)[None, :, None, None]
    t_ = np.arange(24)[None, None, :, None]
    qc = np.arange(64)[None, None, None, :]
    dr = a_ - t_ + 11
    cs = np.clip(qc - 8, 0, 48)
    colv = (kc >= cs) & (kc < cs + 16)
    shp = (2, 64, 24, 64)
    dri = np.broadcast_to(np.clip(dr + 7, 0, 14), shp)
    dci = np.broadcast_to(np.clip(kc - qc + 15, 0, 30), shp)
    nb = rpb[:, :, dri, dci]
    valid = np.broadcast_to(((dr >= -4) & (dr <= 3)) & colv, shp)
    tab_int = np.where(valid[None, None], nb, f32(NEG)).astype(f32).reshape(L, H, 128, 1536)
    tabs = [tab_int]
    bq = np.arange(8)[None, None, :, None]
    for J, i0 in ((0, 0), (3, 10)):
        per_tile = []
        for ii in range(6):
            r = 8 * J + bq
            kr = 2 * (i0 + ii) + a_
            rs_ = np.clip(r - 4, 0, 24)
            vrow = (kr >= rs_) & (kr < rs_ + 8)
            drb = kr - r
            shp2 = (2, 64, 8, 64)
            v2 = np.broadcast_to(vrow & colv, shp2)
            g = rpb[:, :, np.broadcast_to(np.clip(drb + 7, 0, 14), shp2), np.broadcast_to(np.clip(kc - qc + 15, 0, 30), shp2)]
            per_tile.append(np.where(v2[None, None], g, f32(NEG)).astype(f32).reshape(L, H, 128, 512))
        tabs.append(np.concatenate(per_tile, axis=-1))
    common["na_bias"] = np.ascontiguousarray(np.concatenate(tabs, axis=-1))
    tok = np.arange(S)
    rows, cols = (tok // 64).astype(f32), (tok % 64).astype(f32)

    def tables(quarter):
        fr = (f32(10000.0) ** (-np.arange(quarter, dtype=f32) / f32(quarter))).astype(f32)
        ar, ac = rows[:, None] * fr[None, :], cols[:, None] * fr[None, :]
        C = np.concatenate([np.cos(ar), np.cos(ar), np.cos(ac), np.cos(ac)], 1)
        Sg = np.concatenate([-np.sin(ar), np.sin(ar), -np.sin(ac), np.sin(ac)], 1)
        tb = np.stack([C, Sg], 0).astype(f32)
        return np.ascontiguousarray(tb.reshape(2, NLT, 128, 4 * quarter).transpose(0, 2, 1, 3))
    common["ropeM"] = tables(16)
    common["ropeG"] = tables(32)
    x = np.asarray(inputs["x"], f32)
    ctx = np.asarray(inputs["ctx"], f32)
    c = np.asarray(inputs["c"], f32)
    cc = np.asarray(inputs["c_ctx"], f32)
    maps = []
    for b in range(8):
        m = dict(common)
        m["x"] = np.ascontiguousarray(x[b])
        m["ctx"] = np.ascontiguousarray(ctx[b])
        c2 = np.stack([c[b], cc], 0)
        m["cT"] = np.ascontiguousarray(c2.reshape(2, 16, 128).transpose(2, 1, 0))
        maps.append(m)
    return maps


def kernel(**inputs):
    k = build()
    maps = host_inputs(inputs)
    res = run_bass_kernel_spmd(k.nc, maps, core_ids=list(range(8)))
    return np.stack([np.asarray(r["out"], np.float32) for r in res.results], 0)


def tok_groups(ntok):
    g, t0 = [], 0
    while t0 < ntok:
        n = min(512, ntok - t0)
        g.append((t0, n))
        t0 += n
    return g


class Proj:
    def __init__(self, k, es, A, KC, wcols, nslots, nbanks, pfx):
        self.k, self.A, self.KC = k, A, KC
        self.ws = Ring([Slot(k.sb(es, pfx + "_w%d" % i, [128, KC, wcols], BF16), k.dsem()) for i in range(nslots)])
        self.pb = Ring([Slot(k.ps(es, pfx + "_p%d" % i, [128, 512], F32)) for i in range(nbanks)])
        self.items = []

    def add(self, mode, src_fn, run_fn):
        self.items.append((mode, src_fn, run_fn))

    def load(self, idx):
        _, src_fn, _ = self.items[idx]
        s = self.ws.next()
        self.k.pool.wait(s.free_evs())
        pairs = src_fn(s)
        if isinstance(pairs, tuple):
            pairs = [pairs]
        for dst, src in pairs:
            s.ready = self.k.pool.dma(s.dsem, dst, src)
        return s

    def run(self):
        n = len(self.items)
        depth = len(self.ws.slots) - 1
        loaded = []
        for j in range(min(depth, n)):
            loaded.append(self.load(j))
        for idx in range(n):
            if idx + depth < n:
                loaded.append(self.load(idx + depth))
            self.items[idx][2](loaded[idx])
        self.items = []

    def tm_tile(self, slab, width, i, extra_wait=None):
        pe = self.k.pe
        b = self.pb.next()
        pe.wait(slab.ready, b.free_evs(), extra_wait)
        A, KC = self.A, self.KC
        for kk in range(KC):
            ev = pe.do(lambda t, b=b, kk=kk: t.matmul(b.ap[:, 0:width], lhsT=A[:, kk, i * 128:(i + 1) * 128], rhs=slab.ap[:, kk, 0:width],
                                                     start=(kk == 0), stop=(kk == KC - 1)), sig=(kk == KC - 1))
        slab.readers.append(ev)
        return b, ev

    def fm_group(self, slab, c, g0, n, kcs=None, extra_wait=None):
        pe = self.k.pe
        b = self.pb.next()
        pe.wait(slab.ready, b.free_evs(), extra_wait)
        A = self.A
        kcs = list(range(self.KC)) if kcs is None else kcs
        for j, kk in enumerate(kcs):
            ev = pe.do(lambda t, b=b, kk=kk, j=j: t.matmul(b.ap[:, 0:n], lhsT=slab.ap[:, j, c * 128:(c + 1) * 128], rhs=A[:, kk, g0:g0 + n],
                                                           start=(j == 0), stop=(j == len(kcs) - 1)), sig=(j == len(kcs) - 1))
        slab.readers.append(ev)
        return b, ev


def evac_copy(k, eng, out, in_, func=None, scale=None):
    if eng is k.act:
        f = func if func is not None else (AF.Copy if scale is None else AF.Identity)
        if scale is None:
            return eng.do(lambda a: a.activation(out=out, in_=in_, func=f), sig=True)
        return eng.do(lambda a: a.activation(out=out, in_=in_, func=f, scale=scale), sig=True)
    assert func is None
    if scale is None:
        return eng.do(lambda v: v.tensor_copy(out=out, in_=in_), sig=True)
    return eng.do(lambda v: v.tensor_scalar(out=out, in0=in_, scalar1=scale, scalar2=None, op0=ALU.mult), sig=True)


class FMOut:
    def __init__(self, k, es, nslots, pfx):
        self.k = k
        self.ring = Ring([Slot(k.sb(es, pfx + "_fs%d" % i, [128, T], BF16), k.dsem()) for i in range(nslots)])
        self.cur, self.evs, self.tog = None, [], 0

    def evac(self, b, ev, g0, n, first, last, dst, ntok, func=None):
        k = self.k
        if first:
            self.cur = self.ring.next()
            self.evs = []
            self.fe = self.cur.free_evs()
        eng = k.act if (func is not None or self.tog % 2 == 0) else k.dve
        self.tog += 1
        eng.wait(ev, self.fe)
        e2 = evac_copy(k, eng, self.cur.ap[:, g0:g0 + n], b.ap[:, 0:n], func=func)
        b.readers.append(e2)
        self.evs.append(e2)
        if last:
            k.store(k.sp, self.cur, dst[:, 0:ntok], self.cur.ap[:, 0:ntok], self.evs)


class TOut:
    def __init__(self, k, es, ident, nslots, pfx):
        self.k, self.ident = k, ident
        self.ring = Ring([Slot(k.sb(es, pfx + "_ts%d" % i, [128, 4, T], BF16), k.dsem()) for i in range(nslots)])
        self.pt = Ring([Slot(k.ps(es, pfx + "_tp%d" % i, [128, 512], BF16)) for i in range(2)])
        self.tog = 0
        self.q = []
        self.DELAY = 2

    def begin(self):
        self.cur = self.ring.next()
        self.fe = self.cur.free_evs()
        self.evs = []

    def tile(self, o_slot, o_ev, nblk, i):
        self.q.append((o_slot, o_ev, nblk, i))
        while len(self.q) > self.DELAY:
            self._emit(*self.q.pop(0))

    def _emit(self, o_slot, o_ev, nblk, i):
        k, pe, ident = self.k, self.k.pe, self.ident
        p = self.pt.next()
        pe.wait(o_ev, p.free_evs())
        for j in range(nblk):
            evp = pe.do(lambda t, j=j, p=p: t.transpose(out=p.ap[:, j * 128:(j + 1) * 128], in_=o_slot.ap[:, j * 128:(j + 1) * 128], identity=ident[:]), sig=(j == nblk - 1))
        o_slot.readers.append(evp)
        eng = k.act if self.tog % 2 == 0 else k.dve
        self.tog += 1
        eng.wait(evp, self.fe)
        e2 = evac_copy(k, eng, self.cur.ap[:, 0:nblk, i * 128:(i + 1) * 128], p.ap[:, 0:nblk * 128].rearrange("p (a b) -> p a b", b=128))
        p.readers.append(e2)
        self.evs.append(e2)

    def finish(self, dsts, ntok):
        k = self.k
        while self.q:
            self._emit(*self.q.pop(0))
        for j, d in enumerate(dsts):
            k.store(k.sp, self.cur, d[:, 0:ntok], self.cur.ap[:, j, 0:ntok], self.evs)


def tab_bc(tab, i, nh, Dh, sub=None):
    pstep = NLT * Dh
    if sub is None:
        return bass.AP(tab, i * Dh, [[pstep, 128], [0, nh], [1, Dh]])
    b, q = sub
    return bass.AP(tab, i * Dh + b * q, [[pstep, 128], [0, nh], [2 * q, 2], [1, q]])


def rope_emit(k, x_ap, x_ev, nh, Dh, tabC, tabS, i, t1, t2, o_ap, o_free, sc=None):
    dve, pool = k.dve, k.pool
    q = Dh // 4
    W = nh * Dh
    xv = x_ap.rearrange("p (h a b q) -> p h a b q", h=nh, a=2, b=2)
    t1v = t1.ap[:, 0:W].rearrange("p (h d) -> p h d", h=nh)
    t2v = t2.ap[:, 0:W].rearrange("p (h a b q) -> p h a b q", h=nh, a=2, b=2)
    dve.wait(x_ev, t1.free_evs(), t2.free_evs())
    if sc is None:
        dve.do(lambda v: v.tensor_tensor(out=t1v, in0=x_ap.rearrange("p (h d) -> p h d", h=nh), in1=tab_bc(tabC, i, nh, Dh), op=ALU.mult))
        dve.do(lambda v: v.tensor_tensor(out=t2v[:, :, :, 0, :], in0=xv[:, :, :, 1, :], in1=tab_bc(tabS, i, nh, Dh, (0, q)), op=ALU.mult))
        ev = dve.do(lambda v: v.tensor_tensor(out=t2v[:, :, :, 1, :], in0=xv[:, :, :, 0, :], in1=tab_bc(tabS, i, nh, Dh, (1, q)), op=ALU.mult), sig=True)
    else:
        assert nh == 1
        Cv = bass.AP(tabC, i * Dh, [[NLT * Dh, 128], [1, Dh]])
        dve.do(lambda v: v.scalar_tensor_tensor(out=t1.ap[:, 0:W], in0=x_ap, scalar=sc, in1=Cv, op0=ALU.mult, op1=ALU.mult))
        for b in (0, 1):
            Sv = bass.AP(tabS, i * Dh + b * q, [[NLT * Dh, 128], [2 * q, 2], [1, q]])
            ev = dve.do(lambda v, b=b, Sv=Sv: v.scalar_tensor_tensor(out=t2v[:, 0, :, b, :], in0=xv[:, 0, :, 1 - b, :], scalar=sc, in1=Sv, op0=ALU.mult, op1=ALU.mult), sig=(b == 1))
    pool.wait(ev, o_free)
    ev2 = pool.do(lambda g: g.tensor_tensor(out=o_ap, in0=t1.ap[:, 0:W], in1=t2.ap[:, 0:W], op=ALU.add), sig=True)
    t1.readers.append(ev2)
    t2.readers.append(ev2)
    return ev2, [ev]


def phase_win(k, l, hT, ident, ntq, rms_all, rstd_all):
    pe, act, dve, pool, sp = k.pe, k.act, k.dve, k.pool, k.sp
    W = k.dram["w_in"].ap()[l]
    ntok_q = ntq * 128
    with ExitStack() as es:
        pj = Proj(k, es, hT, 16, 512, 2, 4, "win")
        fmo = FMOut(k, es, 2, "win")
        tout = TOut(k, es, ident, 2, "win")
        vst = Ring([Slot(k.sb(es, "win_vst%d" % i, [128, 512], BF16), k.dsem()) for i in range(2)])
        t1r = Ring([Slot(k.sb(es, "win_t1%d" % i, [128, 512], F32)) for i in range(2)])
        t2r = Ring([Slot(k.sb(es, "win_t2%d" % i, [128, 512], F32)) for i in range(2)])
        xgr = Ring([Slot(k.sb(es, "win_xg%d" % i, [128, 512], F32)) for i in range(3)])
        orr = Ring([Slot(k.sb(es, "win_o%d" % i, [128, 512], BF16)) for i in range(4)])
        smr = Ring([Slot(k.sb(es, "win_sm%d" % i, [128, 16], F32)) for i in range(4)])
        junk = k.sb(es, "win_junk", [128, 512], BF16)
        rM = [k.sb(es, "win_rM%d" % j, [128, NLT, 64], F32) for j in range(2)]
        rG = [k.sb(es, "win_rG%d" % j, [128, NLT, 128], F32) for j in range(2)]
        g512 = k.sb(es, "win_g512", [128, 512], F32)
        gqk = [k.sb(es, "win_gqk%d" % j, [128, 128], F32) for j in range(2)]
        d0 = k.dsem()
        for j in range(2):
            sp.dma(d0, rM[j][:], k.dram["ropeM"].ap()[j])
            sp.dma(d0, rG[j][:], k.dram["ropeG"].ap()[j])
        sp.dma(d0, g512[:], bc_ap(k.dram["mla_kv_norm"], l * 512, 512))
        sp.dma(d0, gqk[0][:], bc_ap(k.dram["gqa_q_norm"], l * 128, 128))
        ev_tab = sp.dma(d0, gqk[1][:], bc_ap(k.dram["gqa_k_norm"], l * 128, 128))
        for e in (act, dve, pool):
            e.wait(ev_tab)

        def src2d(c0, w):
            return lambda s: (s.ap[:, :, 0:w], W[:, c0:c0 + w].rearrange("(k p) n -> p k n", p=128))

        def src_mq(h0, nh, d0_, dw):
            v = W[:, 0:3072].rearrange("(k p) (h d) -> p k h d", p=128, d=192)
            return lambda s: [(s.ap[:, :, hh * dw:(hh + 1) * dw], v[:, :, h0 + hh, d0_:d0_ + dw]) for hh in range(nh)]

        def fm_run(nchunks, dsts, ntok, func=None):
            groups = tok_groups(ntok)

            def run(slab):
                for c in range(nchunks):
                    for gi, (g0, n) in enumerate(groups):
                        b, ev = pj.fm_group(slab, c, g0, n)
                        fmo.evac(b, ev, g0, n, gi == 0, gi == len(groups) - 1, dsts[c], ntok, func=func)
            return run

        def v_run(dst, width, ntiles):
            def run(slab):
                for i in range(ntiles):
                    b, ev = pj.tm_tile(slab, width, i)
                    st = vst.next()
                    act.wait(ev, st.free_evs())
                    e2 = evac_copy(k, act, st.ap[:, 0:width], b.ap[:, 0:width])
                    b.readers.append(e2)
                    k.store(sp, st, dst[i * 128:(i + 1) * 128, :], st.ap[:, 0:width], e2)
            return run

        def mqrope_run(h0):
            def run(slab):
                tout.begin()
                for i in range(ntq):
                    b, ev = pj.tm_tile(slab, 512, i)
                    o = orr.next()
                    if i < NLT:
                        e2, xr = rope_emit(k, b.ap[:, 0:512], ev, 8, 64, rM[0], rM[1], i, t1r.next(), t2r.next(), o.ap[:, 0:512], o.free_evs())
                        b.readers.extend(xr)
                    else:
                        act.wait(ev, o.free_evs())
                        e2 = evac_copy(k, act, o.ap[:, 0:512], b.ap[:, 0:512])
                        b.readers.append(e2)
                    tout.tile(o, e2, 4, i)
                tout.finish([k.dram["qa_rT"].ap()[h0 // 2 + j] for j in range(4)], ntok_q)
            return run

        def ckv_run(slab):
            tout.begin()
            for i in range(NT):
                b, ev = pj.tm_tile(slab, 512, i)
                sm = smr.next()
                act.wait(ev, sm.free_evs())
                e1 = act.do(lambda a, b=b, sm=sm: a.activation(out=junk[:], in_=b.ap[:, 0:512], func=AF.Square, accum_out=sm.ap[:, 0:1]), sig=True)
                act.wait(e1)
                e1 = act.do(lambda a, sm=sm, i=i: a.activation(out=rms_all[:, i:i + 1], in_=sm.ap[:, 0:1], func=AF.Sqrt, bias=RMS_EPS, scale=1.0 / 512), sig=True)
                sm.readers.append(e1)
                dve.wait(e1)
                e3 = dve.do(lambda v, i=i: v.reciprocal(out=rstd_all[:, i:i + 1], in_=rms_all[:, i:i + 1]), sig=True)
                o = orr.next()
                dve.wait(o.free_evs())
                e2 = dve.do(lambda v, b=b, o=o: v.tensor_tensor(out=o.ap[:, 0:512], in0=b.ap[:, 0:512], in1=g512[:], op=ALU.mult), sig=True)
                b.readers.extend([e1, e2])
                tout.tile(o, e2, 4, i)
            k.ev_rstd = e3
            tout.finish([k.dram["cgT"].ap()[j * 128:(j + 1) * 128, :] for j in range(4)], T)

        def kr_run(slab):
            tout.begin()
            for i in range(NT):
                b, ev = pj.tm_tile(slab, 64, i)
                o = orr.next()
                sc = rms_all[:, i:i + 1]
                dve.wait(k.ev_rstd)
                if i < NLT:
                    e2, xr = rope_emit(k, b.ap[:, 0:64], ev, 1, 64, rM[0], rM[1], i, t1r.next(), t2r.next(), o.ap[:, 0:64], o.free_evs(), sc=sc)
                    b.readers.extend(xr)
                    pool.wait(e2)
                    e2 = pool.do(lambda g, o=o: g.tensor_copy(out=o.ap[:, 64:128], in_=o.ap[:, 0:64]), sig=True)
                else:
                    dve.wait(ev, o.free_evs())
                    dve.do(lambda v, b=b, o=o, sc=sc: v.tensor_scalar(out=o.ap[:, 0:64], in0=b.ap[:, 0:64], scalar1=sc, scalar2=None, op0=ALU.mult))
                    e2 = dve.do(lambda v, b=b, o=o, sc=sc: v.tensor_scalar(out=o.ap[:, 64:128], in0=b.ap[:, 0:64], scalar1=sc, scalar2=None, op0=ALU.mult), sig=True)
                    b.readers.append(e2)
                tout.tile(o, e2, 1, i)
            tout.finish([k.dram["ka_rT"].ap()], T)

        def gqa_run(gvec, dsts, ntiles):
            def run(slab):
                tout.begin()
                for i in range(ntiles):
                    b, ev = pj.tm_tile(slab, 512, i)
                    sm = smr.next()
                    act.wait(ev, sm.free_evs())
                    for hh in range(4):
                        e1 = act.do(lambda a, b=b, sm=sm, hh=hh: a.activation(out=junk[:, 0:128], in_=b.ap[:, hh * 128:(hh + 1) * 128], func=AF.Square, accum_out=sm.ap[:, hh:hh + 1]), sig=(hh == 3))
                    act.wait(e1)
                    e1 = act.do(lambda a, sm=sm: a.activation(out=sm.ap[:, 4:8], in_=sm.ap[:, 0:4], func=AF.Sqrt, bias=RMS_EPS, scale=1.0 / 128), sig=True)
                    dve.wait(e1)
                    e3 = dve.do(lambda v, sm=sm: v.reciprocal(out=sm.ap[:, 8:12], in_=sm.ap[:, 4:8]), sig=True)
                    xg = xgr.next()
                    dve.wait(e3, xg.free_evs())
                    for hh in range(4):
                        e4 = dve.do(lambda v, b=b, sm=sm, hh=hh, xg=xg: v.scalar_tensor_tensor(out=xg.ap[:, hh * 128:(hh + 1) * 128], in0=b.ap[:, hh * 128:(hh + 1) * 128],
                                                                                              scalar=sm.ap[:, 8 + hh:9 + hh], in1=gvec[:], op0=ALU.mult, op1=ALU.mult), sig=(hh == 3))
                    b.readers.extend([e1, e4])
                    sm.readers.append(e4)
                    o = orr.next()
                    if i < NLT:
                        e2, xr = rope_emit(k, xg.ap[:, 0:512], e4, 4, 128, rG[0], rG[1], i, t1r.next(), t2r.next(), o.ap[:, 0:512], o.free_evs())
                        xg.readers.extend(xr)
                    else:
                        pool.wait(e4, o.free_evs())
                        e2 = pool.do(lambda g, xg=xg, o=o: g.tensor_copy(out=o.ap[:, 0:512], in_=xg.ap[:, 0:512]), sig=True)
                        xg.readers.append(e2)
                    tout.tile(o, e2, 4, i)
                tout.finish(dsts, ntiles * 128)
            return run

        D_ = k.dram
        pj.add("tm", src2d(O_CKV, 512), ckv_run)
        pj.add("tm", src2d(O_KR, 64), kr_run)
        for h0 in (0, 8):
            pj.add("tm", src_mq(h0, 8, 128, 64), mqrope_run(h0))
        for h0 in range(0, H, 4):
            pj.add("fm", src_mq(h0, 4, 0, 128), fm_run(4, [D_["qa_nT"].ap()[h0 + j] for j in range(4)], ntok_q))
        for h0 in range(0, H, 4):
            pj.add("fm", src2d(O_NA + h0 * 128, 512), fm_run(4, [D_["qbT"].ap()[h0 + j] for j in range(4)], ntok_q))
        for h0 in range(0, H, 4):
            pj.add("fm", src2d(O_NA + 2048 + h0 * 128, 512), fm_run(4, [D_["kbT"].ap()[h0 + j] for j in range(4)], T))
        for c0 in range(0, 2048, 512):
            pj.add("tm", src2d(O_NA + 4096 + c0, 512), v_run(D_["vb"].ap()[:, c0:c0 + 512], 512, NT))
        for h0 in range(0, H, 4):
            pj.add("tm", src2d(O_GQ + h0 * 128, 512), gqa_run(gqk[0], [D_["qcT"].ap()[h0 + j] for j in range(4)], ntq))
        pj.add("tm", src2d(O_GKV, 512), gqa_run(gqk[1], [D_["kcT"].ap()[j] for j in range(4)], NT))
        pj.add("tm", src2d(O_GKV + 512, 512), v_run(D_["vc"].ap(), 512, NT))
        for br in range(3):
            for c0 in range(0, 2048, 512):
                pj.add("fm", src2d(O_GATE + br * 2048 + c0, 512),
                       fm_run(4, [D_["gateT"].ap()[br, c0 + j * 128:c0 + (j + 1) * 128, :] for j in range(4)], ntok_q, func=AF.Sigmoid))
        pj.run()
        if "rstd_dbg" in k.dbg:
            st = Slot(None, k.dsem())
            k.store(sp, st, k.dram["rstd_dbg"].ap(), rstd_all[:], k.ev_rstd)
        k.flush()


def phase_ukv(k, l, rstd_all):
    pe, act, dve, pool, sp = k.pe, k.act, k.dve, k.pool, k.sp
    with ExitStack() as es:
        A2 = k.sb(es, "ukv_A", [128, 4, T], BF16)
        Wu = k.sb(es, "ukv_W", [128, 4, 4096], BF16)
        pb = Ring([Slot(k.ps(es, "ukv_p%d" % i, [128, 512], F32)) for i in range(4)])
        fmo = FMOut(k, es, 2, "ukv")
        vst = Ring([Slot(k.sb(es, "ukv_vst%d" % i, [128, 512], BF16), k.dsem()) for i in range(2)])
        d0, d1 = k.dsem(), k.dsem()
        evA = sp.dma(d0, A2[:], k.dram["cgT"].ap().rearrange("(k p) t -> p k t", p=128))
        evW = pool.dma(d1, Wu[:], k.dram["w_mla_ukv"].ap()[l].rearrange("(k p) n -> p k n", p=128))
        groups = tok_groups(T)
        for h in range(H):
            for gi, (g0, n) in enumerate(groups):
                b = pb.next()
                pe.wait(evA, evW, b.free_evs())
                for kk in range(4):
                    ev = pe.do(lambda t, b=b, kk=kk, h=h, g0=g0, n=n: t.matmul(b.ap[:, 0:n], lhsT=Wu[:, kk, h * 256:h * 256 + 128], rhs=A2[:, kk, g0:g0 + n],
                                                                              start=(kk == 0), stop=(kk == 3)), sig=(kk == 3))
                fmo.evac(b, ev, g0, n, gi == 0, gi == len(groups) - 1, k.dram["ka_nT"].ap()[h], T)
        for i in range(NT):
            for hg in range(4):
                b = pb.next()
                pe.wait(b.free_evs())
                for hh in range(4):
                    c0 = (hg * 4 + hh) * 256 + 128
                    for kk in range(4):
                        ev = pe.do(lambda t, b=b, kk=kk, hh=hh, c0=c0, i=i: t.matmul(b.ap[:, hh * 128:(hh + 1) * 128], lhsT=A2[:, kk, i * 128:(i + 1) * 128], rhs=Wu[:, kk, c0:c0 + 128],
                                                                                     start=(kk == 0), stop=(kk == 3)), sig=(kk == 3 and hh == 3))
                st = vst.next()
                act.wait(ev, st.free_evs())
                e2 = evac_copy(k, act, st.ap[:, :], b.ap[:, :], scale=rstd_all[:, i:i + 1])
                b.readers.append(e2)
                k.store(sp, st, k.dram["va"].ap()[i * 128:(i + 1) * 128, hg * 512:(hg + 1) * 512], st.ap[:, :], e2)
        k.flush()


def na_groups():
    gs = []
    for J in range(4):
        rs = [min(max(r - 4, 0), 24) for r in range(8 * J, 8 * J + 8)]
        lo, hi = min(rs), max(rs) + 8
        kts = []
        for ii, i in enumerate(range(lo // 2, (hi + 1) // 2)):
            if J in (0, 3):
                c0 = 1536 + (0 if J == 0 else 3072) + ii * 512
            else:
                c0 = (7 - 2 * (i - 4 * J) + 4) * 64
            kts.append((i, c0, None))
        kts += [(16, None, None), (17, None, None)]
        gs.append((J * 512, 512, kts))
    return gs


def phase_attn(k, l, br, ident, ntq, rstd_all, ada_l=None):
    pe, act, dve, pool, sp = k.pe, k.act, k.dve, k.pool, k.sp
    D_ = k.dram
    ntok_q = ntq * 128
    mla, na = br == 0, br == 1
    with ExitStack() as es:
        slots = []
        for i in range(2):
            s = Slot(None, k.dsem())
            s.QN = k.sb(es, "at_qn%d" % i, [128, T], BF16)
            s.KN = k.sb(es, "at_kn%d" % i, [128, T], BF16)
            s.V = k.sb(es, "at_v%d" % i, [128, NT, 129], BF16)
            if mla:
                s.QR = k.sb(es, "at_qr%d" % i, [128, T], BF16)
            if na:
                s.EE = k.sb(es, "at_ee%d" % i, [128, 7680], BF16)
            slots.append(s)
        EEraw = Slot(k.sb(es, "at_eeraw", [128, 7680], F32), k.dsem()) if na else None
        ev_init = None
        for s in slots:
            ev_init = pool.do(lambda g, s=s: g.memset(s.V[:, :, 128:129], 1.0), sig=True)
        sp.wait(ev_init)
        hring = Ring(slots)
        ev_kr = None
        if mla:
            KRb = k.sb(es, "at_krb", [128, T], BF16)
            sc_all = k.sb(es, "at_sc", [128, NT], F32)
            dk = k.dsem()
            ev_kr = sp.dma(dk, KRb[:], D_["ka_rT"].ap())
            ev_sc = dve.do(lambda v: v.tensor_scalar(out=sc_all[:], in0=rstd_all[:], scalar1=MLA_SCALE, scalar2=None, op0=ALU.mult), sig=True)
            act.wait(ev_sc)
        la = 3 if na else 2
        Sb = Ring([Slot(k.ps(es, "at_s%d" % i, [128, 512], F32)) for i in range(la + 1)])
        Yb = Ring([Slot([k.ps(es, "at_y%d_%d" % (i, j), [128, 512], F32) for j in range(2)]) for i in range(1 if na else 2)])
        ada = ada_steps(k, es, ada_l, nbanks=1) if ada_l is not None else None
        Tb = Ring([Slot(k.ps(es, "at_t%d" % i, [128, 512], BF16)) for i in range(1)])
        PT = Ring([Slot(k.sb(es, "at_pt%d" % i, [128, 512], BF16)) for i in range(la + 2)])
        PR = Ring([Slot(k.sb(es, "at_pr%d" % i, [128, 512], BF16)) for i in range(la + 1)]) if na else None
        Yn = Ring([Slot(k.sb(es, "at_yn%d" % i, [128, 4, 128], BF16)) for i in range(2)])
        Rc = Ring([Slot(k.sb(es, "at_rc%d" % i, [128, 4], F32)) for i in range(2)])
        YT = Ring([Slot(k.sb(es, "at_yt%d" % i, [128, T], BF16), k.dsem()) for i in range(2)])

        def load_head(h):
            s = hring.next()
            fe = s.free_evs()
            sp.wait(fe)
            if br == 0:
                base = (h % 2) * 64
                sp.dma(s.dsem, s.QN[:, 0:ntok_q], D_["qa_nT"].ap()[h][:, 0:ntok_q])
                sp.dma(s.dsem, s.QR[base:base + 64, 0:ntok_q], D_["qa_rT"].ap()[h // 2][base:base + 64, 0:ntok_q])
                sp.dma(s.dsem, s.KN[:], D_["ka_nT"].ap()[h])
                vsrc = D_["va"].ap()[:, h * 128:(h + 1) * 128]
            elif br == 1:
                sp.dma(s.dsem, s.QN[:, 0:ntok_q], D_["qbT"].ap()[h][:, 0:ntok_q])
                sp.dma(s.dsem, s.KN[:], D_["kbT"].ap()[h])
                sp.wait(EEraw.free_evs())
                ev_raw = sp.dma(EEraw.dsem, EEraw.ap[:], D_["na_bias"].ap()[l, h])
                vsrc = D_["vb"].ap()[:, h * 128:(h + 1) * 128]
            else:
                sp.dma(s.dsem, s.QN[:, 0:ntok_q], D_["qcT"].ap()[h][:, 0:ntok_q])
                sp.dma(s.dsem, s.KN[:], D_["kcT"].ap()[h // 4])
                vsrc = D_["vc"].ap()[:, (h // 4) * 128:(h // 4 + 1) * 128]
            s.ready = sp.dma(s.dsem, s.V[:, :, 0:128], vsrc.rearrange("(i p) e -> p i e", p=128))
            s.ee_ev = None
            if na:
                s.ee_ev = None
                for c in range(15):
                    ee_chunks.append((s, c, ev_raw, fe))
            return s

        def head_groups():
            gs = []
            if na:
                gs.extend(na_groups())
            else:
                for g in range(4):
                    gs.append((g * 512, 512, [(i, None, None) for i in range(NT)]))
            if ntq == NT:
                gs.append((S, 256, [(16, None, None), (17, None, None)]))
            return gs

        steps = []
        for h in range(H):
            for gidx, (q0, n, kts) in enumerate(head_groups()):
                for ki, (kt, t0, inv) in enumerate(kts):
                    steps.append(dict(h=h, g=gidx, q0=q0, n=n, kt=kt, t0=t0, inv=inv, first=(ki == 0), last=(ki == len(kts) - 1),
                                      lastg=(gidx == len(head_groups()) - 1)))
        hs = {}
        state = dict(Y=None, yt=None)
        scale_c = NA_SCALE if na else GQA_SCALE

        def emit_qk(st):
            h = st["h"]
            if h not in hs:
                hs[h] = load_head(h)
            s = hs[h]
            sb_ = Sb.next()
            st["S"] = sb_
            q0, n, kt = st["q0"], st["n"], st["kt"]
            pe.wait(s.ready, ev_kr, sb_.free_evs())
            if mla:
                base = (h % 2) * 64
                pe.do(lambda t: t.matmul(sb_.ap[:, 0:n], lhsT=s.KN[:, kt * 128:(kt + 1) * 128], rhs=s.QN[:, q0:q0 + n], start=True, stop=False))
                st["qk_ev"] = pe.do(lambda t: t.matmul(sb_.ap[:, 0:n], lhsT=KRb[base:base + 64, kt * 128:(kt + 1) * 128], rhs=s.QR[base:base + 64, q0:q0 + n], start=False, stop=True), sig=True)
            else:
                st["qk_ev"] = pe.do(lambda t: t.matmul(sb_.ap[:, 0:n], lhsT=s.KN[:, kt * 128:(kt + 1) * 128], rhs=s.QN[:, q0:q0 + n], start=True, stop=True), sig=True)

        ee_chunks = []

        def emit_ee_chunk():
            s, c, ev_raw, fe = ee_chunks.pop(0)
            act.wait(ev_raw, fe)
            e = act.do(lambda a: a.activation(out=s.EE[:, c * 512:(c + 1) * 512], in_=EEraw.ap[:, c * 512:(c + 1) * 512], func=AF.Exp), sig=True)
            if c == 14:
                s.ee_ev = e
                EEraw.readers.append(e)

        def emit_exp(st):
            s, sb_, n, kt = hs[st["h"]], st["S"], st["n"], st["kt"]
            while ee_chunks and ee_chunks[0][0] is s:
                emit_ee_chunk()
            pt = PT.next()
            st["PT"] = pt
            if st["t0"] is None:
                act.wait(st["qk_ev"], pt.free_evs())
                sc = sc_all[:, kt:kt + 1] if mla else scale_c
                e = act.do(lambda a: a.activation(out=pt.ap[:, 0:n], in_=sb_.ap[:, 0:n], func=AF.Exp, scale=sc), sig=True)
                sb_.readers.append(e)
                st["pt_ev"] = e
            else:
                pr = PR.next()
                act.wait(st["qk_ev"], pr.free_evs())
                e = act.do(lambda a: a.activation(out=pr.ap[:, 0:n], in_=sb_.ap[:, 0:n], func=AF.Exp, scale=scale_c), sig=True)
                sb_.readers.append(e)
                c0 = st["t0"]
                dve.wait(e, s.ee_ev, pt.free_evs())
                e2 = dve.do(lambda v: v.tensor_tensor(out=pt.ap[:, 0:n], in0=pr.ap[:, 0:n], in1=s.EE[:, c0:c0 + n], op=ALU.mult), sig=True)
                pr.readers.append(e2)
                st["pt_ev"] = e2

        def emit_pv(st):
            s, n, kt, pt = hs[st["h"]], st["n"], st["kt"], st["PT"]
            if st["first"]:
                y = Yb.next()
                state["Y"] = y
                pe.wait(y.free_evs())
            y = state["Y"]
            pe.wait(st["pt_ev"])
            nqs = n // 128
            for qs in range(nqs):
                yap = y.ap[qs // 2][:, (qs % 2) * 256:(qs % 2) * 256 + 129]
                ev = pe.do(lambda t, qs=qs, yap=yap: t.matmul(yap, lhsT=pt.ap[:, qs * 128:(qs + 1) * 128], rhs=s.V[:, kt, :], start=(st["first"] and qs % 2 == 0), stop=st["last"]), sig=(qs == nqs - 1))
            pt.readers.append(ev)
            st["pv_ev"] = ev
            if st["lastg"] and st["last"]:
                s.readers.append(ev)

        def epilogue(st):
            y, n, q0, h = state["Y"], st["n"], st["q0"], st["h"]
            nqs = n // 128
            rc, yn = Rc.next(), Yn.next()
            dve.wait(st["pv_ev"], rc.free_evs(), yn.free_evs())
            for qs in range(nqs):
                e = dve.do(lambda v, qs=qs: v.reciprocal(out=rc.ap[:, qs:qs + 1], in_=y.ap[qs // 2][:, (qs % 2) * 256 + 128:(qs % 2) * 256 + 129]), sig=(qs == nqs - 1))
            dve.wait(e)
            for qs in range(nqs):
                e = dve.do(lambda v, qs=qs: v.tensor_scalar(out=yn.ap[:, qs, :], in0=y.ap[qs // 2][:, (qs % 2) * 256:(qs % 2) * 256 + 128], scalar1=rc.ap[:, qs:qs + 1], scalar2=None, op0=ALU.mult), sig=(qs == nqs - 1))
            y.readers.append(e)
            rc.readers.append(e)
            if st["g"] == 0:
                state["yt"] = YT.next()
                state["yt_fe"] = state["yt"].free_evs()
                state["yt_evs"] = []
            yt, yt_fe, yt_evs = state["yt"], state["yt_fe"], state["yt_evs"]
            lastg = st["lastg"]

            def pe_part():
                tb = Tb.next()
                pe.wait(e, tb.free_evs())
                for qs in range(nqs):
                    ep = pe.do(lambda t, qs=qs: t.transpose(out=tb.ap[:, qs * 128:(qs + 1) * 128], in_=yn.ap[:, qs, :], identity=ident[:]), sig=(qs == nqs - 1))
                yn.readers.append(ep)
                dve.wait(ep, yt_fe)
                ec = dve.do(lambda v: v.tensor_copy(out=yt.ap[:, q0:q0 + n], in_=tb.ap[:, 0:n]), sig=True)
                tb.readers.append(ec)
                yt_evs.append(ec)
                if lastg:
                    k.store(sp, yt, D_["yT"].ap()[br, h][:, 0:ntok_q], yt.ap[:, 0:ntok_q], list(yt_evs))
            return pe_part

        deferred = []
        for si in range(min(la, len(steps))):
            emit_qk(steps[si])
        for si, st in enumerate(steps):
            emit_exp(st)
            if si + la < len(steps):
                emit_qk(steps[si + la])
            emit_pv(st)
            if ada is not None and si % 48 == 47:
                next(ada, None)
            if ee_chunks:
                emit_ee_chunk()
            deferred = [(c - 1, f) for (c, f) in deferred]
            for c, f in [d for d in deferred if d[0] <= 0]:
                f()
            deferred = [d for d in deferred if d[0] > 0]
            if st["last"]:
                deferred.append((la + 1, epilogue(st)))
            if st["first"] and st["g"] == 0 and st["h"] + 1 < H and (st["h"] + 1) not in hs:
                hs[st["h"] + 1] = load_head(st["h"] + 1)
        for c, f in deferred:
            f()
        if ada is not None:
            for _ in ada:
                pass
        k.flush()


def phase_merge(k, l, ntq):
    pe, act, dve, pool, sp = k.pe, k.act, k.dve, k.pool, k.sp
    D_ = k.dram
    ngt = ntq // 2
    ntg = ngt * 128
    WB = D_["w_branch"].ap()[l]
    with ExitStack() as es:
        A3 = k.sb(es, "mg_A", [128, 48, ntg], BF16)
        ws = Ring([Slot(k.sb(es, "mg_w%d" % i, [128, 3, 16, 256], BF16), k.dsem()) for i in range(2)])
        gs = Ring([Slot(k.sb(es, "mg_g%d" % i, [128, 3, ntg], BF16), k.dsem()) for i in range(2)])
        accs = Ring([Slot(k.sb(es, "mg_acc%d" % i, [128, ntg], F32)) for i in range(2)])
        tmps = Ring([Slot(k.sb(es, "mg_tmp%d" % i, [128, 512], F32)) for i in range(3)])
        outs = Ring([Slot(k.sb(es, "mg_o%d" % i, [128, ntg], BF16), k.dsem()) for i in range(2)])
        pb = Ring([Slot(k.ps(es, "mg_p%d" % i, [128, 512], F32)) for i in range(6)])
        dA = k.dsem()
        a_readers = []
        groups = tok_groups(ntg)

        def wload(cs):
            s = ws.next()
            pool.wait(s.free_evs())
            for i in range(3):
                s.ready = pool.dma(s.dsem, s.ap[:, i, :, :], WB[i][:, cs * 256:(cs + 1) * 256].rearrange("(k p) n -> p k n", p=128))
            return s

        for tg in range(2):
            tok0 = tg * ntg
            sp.wait(a_readers)
            a_readers = []
            for i in range(3):
                evA = sp.dma(dA, A3[:, i * 16:(i + 1) * 16, :], D_["yT"].ap()[i][:, :, tok0:tok0 + ntg].rearrange("h p t -> p h t"))
            nxt = wload(0)
            for cs in range(8):
                cur = nxt
                if cs + 1 < 8:
                    nxt = wload(cs + 1)
                for cc in range(2):
                    c = cs * 2 + cc
                    g = gs.next()
                    sp.wait(g.free_evs())
                    g.ready = sp.dma(g.dsem, g.ap[:], D_["gateT"].ap()[:, c * 128:(c + 1) * 128, tok0:tok0 + ntg].rearrange("i p t -> p i t"))
                    acc, o = accs.next(), outs.next()
                    acc_fe, o_fe = acc.free_evs(), o.free_evs()
                    last = {}
                    oevs = []
                    for i in range(3):
                        for (g0, n) in groups:
                            b = pb.next()
                            pe.wait(cur.ready, evA, b.free_evs())
                            for hc in range(16):
                                ev = pe.do(lambda t, b=b, i=i, hc=hc, cc=cc, g0=g0, n=n, cur=cur: t.matmul(b.ap[:, 0:n], lhsT=cur.ap[:, i, hc, cc * 128:(cc + 1) * 128], rhs=A3[:, i * 16 + hc, g0:g0 + n],
                                                                                                          start=(hc == 0), stop=(hc == 15)), sig=(hc == 15))
                            cur.readers.append(ev)
                            a_readers.append(ev)
                            if i == 0:
                                dve.wait(ev, g.ready, acc_fe)
                                e = dve.do(lambda v, b=b, g=g, acc=acc, g0=g0, n=n: v.tensor_tensor(out=acc.ap[:, g0:g0 + n], in0=b.ap[:, 0:n], in1=g.ap[:, 0, g0:g0 + n], op=ALU.mult), sig=True)
                                b.readers.append(e)
                                last[g0] = e
                            else:
                                tmp = tmps.next()
                                dve.wait(ev, g.ready, tmp.free_evs())
                                e = dve.do(lambda v, b=b, g=g, tmp=tmp, i=i, g0=g0, n=n: v.tensor_tensor(out=tmp.ap[:, 0:n], in0=b.ap[:, 0:n], in1=g.ap[:, i, g0:g0 + n], op=ALU.mult), sig=True)
                                b.readers.append(e)
                                pool.wait(e, last[g0])
                                if i == 1:
                                    e2 = pool.do(lambda p_, acc=acc, tmp=tmp, g0=g0, n=n: p_.tensor_tensor(out=acc.ap[:, g0:g0 + n], in0=acc.ap[:, g0:g0 + n], in1=tmp.ap[:, 0:n], op=ALU.add), sig=True)
                                    last[g0] = e2
                                else:
                                    pool.wait(o_fe)
                                    e2 = pool.do(lambda p_, acc=acc, tmp=tmp, o=o, g0=g0, n=n: p_.tensor_tensor(out=o.ap[:, g0:g0 + n], in0=acc.ap[:, g0:g0 + n], in1=tmp.ap[:, 0:n], op=ALU.add), sig=True)
                                    oevs.append(e2)
                                    acc.readers.append(e2)
                                tmp.readers.append(e2)
                    g.readers.append(e)
                    k.store(sp, o, D_["accT"].ap()[c * 128:(c + 1) * 128, tok0:tok0 + ntg], o.ap[:, :], oevs)
        k.flush()


def phase_tm_out(k, l, which, ntq):
    pe, act, dve, pool, sp = k.pe, k.act, k.dve, k.pool, k.sp
    D_ = k.dram
    with ExitStack() as es:
        if which == "out":
            KC, wc, Wd, ngroups, ngt = 16, 512, D_["w_out"].ap()[l], 1, ntq
            Asrc = D_["accT"].ap()
        else:
            KC, wc, Wd, ngroups, ngt = NFC, 256, D_["w_down"].ap()[l], 2, ntq // 2
            Asrc = D_["hidT"].ap()
        ntg = ngt * 128
        A = k.sb(es, "to_A", [128, KC, ntg], BF16)
        pj = Proj(k, es, A, KC, wc, 2, 4, "to")
        stg = Ring([Slot(k.sb(es, "to_st%d" % i, [128, 512], F32), k.dsem()) for i in range(3)])
        dA = k.dsem()
        tog = [0]
        a_readers = []
        for tg in range(ngroups):
            tok0 = tg * ntg
            sp.wait(a_readers)
            evA = sp.dma(dA, A[:], Asrc[:, tok0:tok0 + ntg].rearrange("(k p) t -> p k t", p=128))

            def mk_run(c0, evA=evA, tok0=tok0):
                def run(slab):
                    for i in range(ngt):
                        b, ev = pj.tm_tile(slab, wc, i, extra_wait=evA)
                        a_readers.append(ev)
                        st = stg.next()
                        eng = act if tog[0] % 2 == 0 else dve
                        tog[0] += 1
                        eng.wait(ev, st.free_evs())
                        e2 = evac_copy(k, eng, st.ap[:, 0:wc], b.ap[:, 0:wc])
                        b.readers.append(e2)
                        k.store(sp, st, D_["fout"].ap()[tok0 + i * 128:tok0 + (i + 1) * 128, c0:c0 + wc], st.ap[:, 0:wc], e2)
                return run

            for c0 in range(0, D, wc):
                pj.add("tm", (lambda s, c0=c0: (s.ap[:, :, 0:wc], Wd[:, c0:c0 + wc].rearrange("(k p) n -> p k n", p=128))), mk_run(c0))
            pj.run()
        k.flush()


def phase_resid(k, l, sub, src, ntiles, final):
    pe, act, dve, pool, sp = k.pe, k.act, k.dve, k.pool, k.sp
    D_ = k.dram
    mv_t = D_["mvec"]
    with ExitStack() as es:
        d0 = k.dsem()
        gate = {}
        for j, nm in ((0, "lat"), (1, "ctx")):
            gate[j] = k.sb(es, "rs_gate" + nm, [128, 2048], F32)
            sp.dma(d0, gate[j][:], bc_ap(mv_t, (l * 2 + j) * 12288 + (2 + 3 * sub) * 2048, 2048))
        lng = k.sb(es, "rs_lng", [128, 2048], F32)
        lnb = k.sb(es, "rs_lnb", [128, 2048], F32)
        sp.dma(d0, lng[:], bc_ap(D_["ln_f_g" if sub else "ln_a_g"], l * 2048, 2048))
        evc = sp.dma(d0, lnb[:], bc_ap(D_["ln_f_b" if sub else "ln_a_b"], l * 2048, 2048))
        xr = Ring([Slot(k.sb(es, "rs_x%d" % i, [128, 2048], F32), k.dsem()) for i in range(2)])
        fr = Ring([Slot(k.sb(es, "rs_f%d" % i, [128, 2048], F32), k.dsem()) for i in range(2)])
        orr = Ring([Slot(k.sb(es, "rs_o%d" % i, [128, 2048], F32), k.dsem()) for i in range(2)])
        sm = Ring([Slot(k.sb(es, "rs_sm%d" % i, [128, 32], F32)) for i in range(2)])
        fout = D_["fout"].ap()

        def load(i):
            xs, fs = xr.next(), fr.next()
            sp.wait(xs.free_evs(), fs.free_evs())
            xs.ready = sp.dma(xs.dsem, xs.ap[:], src(i))
            fs.ready = sp.dma(fs.dsem, fs.ap[:], fout[i * 128:(i + 1) * 128, :])
            return xs, fs

        nxt = load(0)
        for i in range(ntiles):
            xs, fs = nxt
            if i + 1 < ntiles:
                nxt = load(i + 1)
            gt = gate[0 if i < NLT else 1]
            dve.wait(xs.ready, fs.ready, evc)
            e = dve.do(lambda v, fs=fs, gt=gt: v.tensor_tensor(out=fs.ap[:], in0=fs.ap[:], in1=gt[:], op=ALU.mult), sig=True)
            dve.wait(e)
            e = dve.do(lambda v, xs=xs, fs=fs: v.scalar_tensor_tensor(out=xs.ap[:], in0=xs.ap[:], scalar=ALPHA, in1=fs.ap[:], op0=ALU.mult, op1=ALU.add), sig=True)
            smt = sm.next()
            st = smt.ap
            stats = st[:, 0:24].rearrange("p (c s) -> p c s", s=6)
            mv, rstd, nmr = st[:, 24:26], st[:, 26:27], st[:, 27:28]
            dve.wait(e, smt.free_evs())
            e = ln_stats(k, xs.ap, mv, rstd, nmr, POST_EPS, stats)
            act.wait(e)
            e = act.do(lambda a, xs=xs, fs=fs, rstd=rstd, nmr=nmr: a.activation(out=fs.ap[:], in_=xs.ap[:], func=AF.Identity, bias=nmr, scale=rstd), sig=True)
            xs.readers.append(e)
            smt.readers.append(e)
            dve.wait(e)
            e = dve.do(lambda v, fs=fs: v.tensor_tensor(out=fs.ap[:], in0=fs.ap[:], in1=lng[:], op=ALU.mult), sig=True)
            os_ = orr.next()
            pool.wait(e, os_.free_evs())
            e = pool.do(lambda g, fs=fs, os_=os_: g.tensor_tensor(out=os_.ap[:], in0=fs.ap[:], in1=lnb[:], op=ALU.add), sig=True)
            fs.readers.append(e)
            if final:
                if i < NLT:
                    k.store(sp, os_, k.out.ap()[i * 128:(i + 1) * 128, :], os_.ap[:], e)
            else:
                k.store(sp, os_, D_["xres"].ap()[i * 128:(i + 1) * 128, :], os_.ap[:], e)
        k.flush()


def phase_ffn_up(k, l, hT, ntq, ada_l=None):
    pe, act, dve, pool, sp = k.pe, k.act, k.dve, k.pool, k.sp
    D_ = k.dram
    Wu = D_["w_up"].ap()[l]
    ntok = ntq * 128
    has_ctx = ntq == NT
    GW = T + 4
    with ExitStack() as es:
        pj = Proj(k, es, hT, 16, 512, 2, 4, "fu")
        Gb = Ring([Slot(k.sb(es, "fu_g%d" % i, [128, GW], F32)) for i in range(2)])
        Vb = Ring([Slot(k.sb(es, "fu_v%d" % i, [128, T], F32)) for i in range(2)])
        Cb = Ring([Slot(k.sb(es, "fu_c%d" % i, [128, T], F32)) for i in range(2)])
        Ho = Ring([Slot(k.sb(es, "fu_h%d" % i, [128, T], BF16), k.dsem()) for i in range(2)])
        cp = k.sb(es, "fu_cp", [128, NFC, 4], F32)
        d0 = k.dsem()
        evcp = sp.dma(d0, cp[:], D_["convp"].ap()[l])
        ev_ms = None
        for s in Gb.slots:
            ev_ms = pool.do(lambda g, s=s: g.memset(s.ap[:], 0.0), sig=True)
        groups = tok_groups(ntok)
        segs = [(0, S, 0)] + ([(S, CL, S + 2)] if has_ctx else [])

        def gcol(g0):
            return 1 + g0 if g0 < S else S + 3 + (g0 - S)

        def mk_run(sp_):
            def run(slab):
                for cc in range(2):
                    ch = sp_ * 2 + cc
                    gb, vb, cb, ho = Gb.next(), Vb.next(), Cb.next(), Ho.next()
                    gfe, vfe = gb.free_evs(), vb.free_evs()
                    gevs, vevs = [], []
                    for (g0, n) in groups:
                        b, ev = pj.fm_group(slab, cc, g0, n)
                        act.wait(ev, gfe, ev_ms)
                        e = evac_copy(k, act, gb.ap[:, gcol(g0):gcol(g0) + n], b.ap[:, 0:n])
                        b.readers.append(e)
                        gevs.append(e)
                    for (g0, n) in groups:
                        b, ev = pj.fm_group(slab, 2 + cc, g0, n)
                        act.wait(ev, vfe)
                        e = evac_copy(k, act, vb.ap[:, g0:g0 + n], b.ap[:, 0:n])
                        b.readers.append(e)
                        vevs.append(e)
                    w = [cp[:, ch, j:j + 1] for j in range(4)]
                    dve.wait(gevs, evcp, cb.free_evs())
                    for (t0, n, c0) in segs:
                        e = dve.do(lambda v, gb=gb, cb=cb, t0=t0, n=n, c0=c0, w=w: v.tensor_scalar(out=cb.ap[:, t0:t0 + n], in0=gb.ap[:, c0:c0 + n], scalar1=w[0], scalar2=w[3], op0=ALU.mult, op1=ALU.add), sig=True)
                    for j in (1, 2):
                        dve.wait(e)
                        for (t0, n, c0) in segs:
                            e = dve.do(lambda v, gb=gb, cb=cb, t0=t0, n=n, c0=c0, w=w, j=j: v.scalar_tensor_tensor(out=cb.ap[:, t0:t0 + n], in0=gb.ap[:, c0 + j:c0 + j + n], scalar=w[j], in1=cb.ap[:, t0:t0 + n],
                                                                                                                     op0=ALU.mult, op1=ALU.add), sig=True)
                    gb.readers.append(e)
                    act.wait(e)
                    e = act.do(lambda a, cb=cb: a.activation(out=cb.ap[:, 0:ntok], in_=cb.ap[:, 0:ntok], func=AF.Silu), sig=True)
                    pool.wait(e, vevs, ho.free_evs())
                    e = pool.do(lambda g, cb=cb, vb=vb, ho=ho: g.tensor_tensor(out=ho.ap[:, 0:ntok], in0=cb.ap[:, 0:ntok], in1=vb.ap[:, 0:ntok], op=ALU.mult), sig=True)
                    cb.readers.append(e)
                    vb.readers.append(e)
                    k.store(sp, ho, D_["hidT"].ap()[ch * 128:(ch + 1) * 128, 0:ntok], ho.ap[:, 0:ntok], e)
                    if ada is not None and ch >= 2 and ch % 2 == 0:
                        next(ada, None)
                        next(ada, None)
                        if ch % 4 == 0:
                            next(ada, None)
            return run

        ada = ada_steps(k, es, ada_l, nbanks=1, sw=256) if ada_l is not None else None
        for sp_ in range(NFC // 2):
            pj.add("fm", (lambda s, sp_=sp_: [(s.ap[:, :, 0:256], Wu[:, sp_ * 256:(sp_ + 1) * 256].rearrange("(k p) n -> p k n", p=128)),
                                              (s.ap[:, :, 256:512], Wu[:, DFF + sp_ * 256:DFF + (sp_ + 1) * 256].rearrange("(k p) n -> p k n", p=128))]), mk_run(sp_))
        pj.run()
        if ada is not None:
            for _ in ada:
                pass
        k.flush()


def phase_out_resid(k, l, src, ntiles):
    pe, act, dve, pool, sp = k.pe, k.act, k.dve, k.pool, k.sp
    D_ = k.dram
    mv_t = D_["mvec"]
    Wd = D_["w_out"].ap()[l]
    accT = D_["accT"].ap()
    with ExitStack() as es:
        Wsb = k.sb(es, "or_w", [128, 16, D], BF16)
        dW = [k.dsem() for _ in range(4)]
        evW = [pool.dma(dW[n], Wsb[:, :, n * 512:(n + 1) * 512], Wd[:, n * 512:(n + 1) * 512].rearrange("(k p) n -> p k n", p=128)) for n in range(4)]
        d0 = k.dsem()
        gate = {}
        for j, nm in ((0, "lat"), (1, "ctx")):
            gate[j] = k.sb(es, "or_gate" + nm, [128, 2048], F32)
            sp.dma(d0, gate[j][:], bc_ap(mv_t, (l * 2 + j) * 12288 + 2 * 2048, 2048))
        lng = k.sb(es, "or_lng", [128, 2048], F32)
        lnb = k.sb(es, "or_lnb", [128, 2048], F32)
        sp.dma(d0, lng[:], bc_ap(D_["ln_a_g"], l * 2048, 2048))
        evc = sp.dma(d0, lnb[:], bc_ap(D_["ln_a_b"], l * 2048, 2048))
        ar = Ring([Slot(k.sb(es, "or_a%d" % i, [128, 16, 128], BF16), k.dsem()) for i in range(3)])
        xr = Ring([Slot(k.sb(es, "or_x%d" % i, [128, 2048], F32), k.dsem()) for i in range(3)])
        fr = Ring([Slot(k.sb(es, "or_f%d" % i, [128, 2048], F32)) for i in range(2)])
        orr = Ring([Slot(k.sb(es, "or_o%d" % i, [128, 2048], F32), k.dsem()) for i in range(2)])
        sm = Ring([Slot(k.sb(es, "or_sm%d" % i, [128, 32], F32)) for i in range(2)])
        pbs = Ring([Slot([k.ps(es, "or_p%d_%d" % (i, n), [128, 512], F32) for n in range(4)]) for i in range(2)])

        def load(i):
            a, xs = ar.next(), xr.next()
            sp.wait(a.free_evs(), xs.free_evs())
            a.ready = sp.dma(a.dsem, a.ap[:], accT[:, i * 128:(i + 1) * 128].rearrange("(k p) t -> p k t", p=128))
            xs.ready = sp.dma(xs.dsem, xs.ap[:], src(i))
            return a, xs

        q = [load(0)]
        if ntiles > 1:
            q.append(load(1))
        for i in range(ntiles):
            a, xs = q.pop(0)
            if i + 2 < ntiles:
                q.append(load(i + 2))
            pb = pbs.next()
            pe.wait(a.ready, pb.free_evs())
            for n in range(4):
                pe.wait(evW[n])
                for kk in range(16):
                    ev = pe.do(lambda t, pb=pb, a=a, n=n, kk=kk: t.matmul(pb.ap[n][:, :], lhsT=a.ap[:, kk, :], rhs=Wsb[:, kk, n * 512:(n + 1) * 512], start=(kk == 0), stop=(kk == 15)), sig=(kk == 15))
            a.readers.append(ev)
            gt = gate[0 if i < NLT else 1]
            fs = fr.next()
            dve.wait(ev, evc, fs.free_evs())
            for n in range(4):
                e = dve.do(lambda v, pb=pb, fs=fs, gt=gt, n=n: v.tensor_tensor(out=fs.ap[:, n * 512:(n + 1) * 512], in0=pb.ap[n][:, :], in1=gt[:, n * 512:(n + 1) * 512], op=ALU.mult), sig=(n == 3))
            pb.readers.append(e)
            dve.wait(e, xs.ready)
            e = dve.do(lambda v, xs=xs, fs=fs: v.scalar_tensor_tensor(out=xs.ap[:], in0=xs.ap[:], scalar=ALPHA, in1=fs.ap[:], op0=ALU.mult, op1=ALU.add), sig=True)
            smt = sm.next()
            st = smt.ap
            stats = st[:, 0:24].rearrange("p (c s) -> p c s", s=6)
            mv, rstd, nmr = st[:, 24:26], st[:, 26:27], st[:, 27:28]
            dve.wait(e, smt.free_evs())
            e = ln_stats(k, xs.ap, mv, rstd, nmr, POST_EPS, stats)
            act.wait(e)
            e = act.do(lambda a_, xs=xs, fs=fs, rstd=rstd, nmr=nmr: a_.activation(out=fs.ap[:], in_=xs.ap[:], func=AF.Identity, bias=nmr, scale=rstd), sig=True)
            xs.readers.append(e)
            smt.readers.append(e)
            dve.wait(e)
            e = dve.do(lambda v, fs=fs: v.tensor_tensor(out=fs.ap[:], in0=fs.ap[:], in1=lng[:], op=ALU.mult), sig=True)
            os_ = orr.next()
            pool.wait(e, os_.free_evs())
            e = pool.do(lambda g, fs=fs, os_=os_: g.tensor_tensor(out=os_.ap[:], in0=fs.ap[:], in1=lnb[:], op=ALU.add), sig=True)
            fs.readers.append(e)
            k.store(sp, os_, D_["xres"].ap()[i * 128:(i + 1) * 128, :], os_.ap[:], e)
        k.flush()
```

```python
import numpy as np
from contextlib import ExitStack
import concourse.bass as bass
import concourse.mybir as mybir
from concourse.bass_utils import run_bass_kernel_spmd

F32 = mybir.dt.float32
BF16 = mybir.dt.bfloat16
AF = mybir.ActivationFunctionType
ALU = mybir.AluOpType

D = 2048
S = 2048
CL = 256
T = S + CL
NT = T // 128
NLT = S // 128
L = 2
H = 16
DFF = 5632
NFC = DFF // 128
N_IN = 19008
O_MQ, O_CKV, O_KR, O_NA, O_GQ, O_GKV, O_GATE = 0, 3072, 3584, 3648, 9792, 11840, 12864
ALPHA = (2 * L) ** 0.25
MLA_SCALE = 192 ** -0.5
NA_SCALE = 128 ** -0.5
GQA_SCALE = 128 ** -0.5
ADA_EPS, POST_EPS, RMS_EPS = 1e-6, 1e-5, 1e-6
NEG = -1.0e4


class DSem:
    def __init__(self, h, key):
        self.h, self.key, self.cnt = h, key, 0


class Eng:
    def __init__(self, name, sem):
        self.name, self.sem, self.cnt, self.ops, self.waited = name, sem, 0, [], {}

    def wait(self, *evs):
        for ev in evs:
            if ev is None:
                continue
            if isinstance(ev, list):
                self.wait(*ev)
                continue
            key, sem, val = ev
            if self.waited.get(key, 0) >= val:
                continue
            self.waited[key] = val
            self.ops.append(lambda e, sem=sem, val=val: e.wait_ge(sem, val))

    def do(self, fn, sig=False):
        if sig:
            self.cnt += 1
            n, sem = self.cnt, self.sem
            self.ops.append(lambda e: fn(e).then_inc(sem, 1))
            return (self.name, sem, n)
        self.ops.append(fn)
        return None

    def dma(self, dsem, out, in_):
        dsem.cnt += 16
        n, h = dsem.cnt, dsem.h
        self.ops.append(lambda e: e.dma_start(out=out, in_=in_).then_inc(h, 16))
        return (dsem.key, h, n)


class Slot:
    def __init__(self, ap, dsem=None):
        self.ap, self.dsem = ap, dsem
        self.ready = None
        self.readers = []

    def free_evs(self):
        r = self.readers
        self.readers = []
        return r


class Ring:
    def __init__(self, slots):
        self.slots, self.i = slots, 0

    def next(self):
        s = self.slots[self.i % len(self.slots)]
        self.i += 1
        return s


class K:
    def __init__(self, dbg=(), stop_after=None):
        self.dbg, self.stop_after = set(dbg), stop_after
        self.branches = (0, 1, 2)
        self.nc = nc = bass.Bass("TRN2", target_bir_lowering=False)
        self.es = ExitStack()
        self.ds_pool, self.ds_i = [], 0
        self.uid = 0
        self.engs = {}
        for n in ("pe", "act", "dve", "pool", "sp"):
            self.engs[n] = Eng(n, self.es.enter_context(nc.semaphore("sem_" + n)))
        self.pe, self.act, self.dve, self.pool, self.sp = (self.engs[n] for n in ("pe", "act", "dve", "pool", "sp"))
        self.dram = {}
        self.pending = []

    def inp(self, name, shape, dt=F32):
        t = self.nc.dram_tensor(name, list(shape), dt, kind="ExternalInput")
        self.dram[name] = t
        return t

    def scr(self, name, shape, dt=BF16):
        if name in self.dbg:
            t = self.nc.dram_tensor(name, list(shape), dt, kind="ExternalOutput")
        else:
            t = self.nc.dram_tensor(name, list(shape), dt)
        self.dram[name] = t
        return t

    def dsem(self):
        if self.ds_i >= len(self.ds_pool):
            n = len(self.ds_pool)
            self.ds_pool.append(DSem(self.es.enter_context(self.nc.semaphore("ds%d" % n)), "ds%d" % n))
        d = self.ds_pool[self.ds_i]
        self.ds_i += 1
        return d

    def sb(self, es, name, shape, dt):
        self.uid += 1
        return es.enter_context(self.nc.sbuf_tensor("%s_%d" % (name, self.uid), list(shape), dt))

    def ps(self, es, name, shape, dt):
        self.uid += 1
        return es.enter_context(self.nc.psum_tensor("%s_%d" % (name, self.uid), list(shape), dt))

    def flush(self, waiter=None):
        w = waiter or self.sp
        w.wait(self.pending)
        self.pending = []
        with self.nc.Block() as block:
            ops = {n: list(e.ops) for n, e in self.engs.items()}

            @block.tensor
            def _(t):
                for f in ops["pe"]:
                    f(t)

            @block.scalar
            def _(a):
                for f in ops["act"]:
                    f(a)

            @block.vector
            def _(v):
                for f in ops["dve"]:
                    f(v)

            @block.gpsimd
            def _(g):
                for f in ops["pool"]:
                    f(g)

            @block.sync
            def _(s):
                for f in ops["sp"]:
                    f(s)
        for e in self.engs.values():
            e.ops = []
        self.ds_i = 0

    def store(self, eng, slot, out, in_, after):
        eng.wait(after)
        ev = eng.dma(slot.dsem, out, in_)
        slot.readers.append(ev)
        self.pending.append(ev)
        return ev


def bc_ap(t, off, n, parts=128):
    return bass.AP(t, off, [[0, parts], [1, n]])


def ada_steps(k, es, l, nbanks=2, sw=512):
    pe, act, dve, pool, sp = k.pe, k.act, k.dve, k.pool, k.sp
    csb = k.sb(es, "ada_c", [128, 32], F32)
    scT = k.sb(es, "ada_sc", [128, 16, 2], BF16)
    bs = Ring([Slot(k.sb(es, "ada_b%d" % i, [2, sw], F32), k.dsem()) for i in range(2)])
    ms = Ring([Slot(k.sb(es, "ada_m%d" % i, [2, sw], F32), k.dsem()) for i in range(2)])
    ws = Ring([Slot(k.sb(es, "ada_w%d" % i, [128, 16, sw], BF16), k.dsem()) for i in range(2)])
    pb = Ring([Slot(k.ps(es, "ada_p%d" % i, [128, 512], F32)) for i in range(nbanks)])
    d0 = k.dsem()
    e2 = sp.dma(d0, csb[:], k.dram["cT"].ap().rearrange("p k j -> p (k j)"))
    act.wait(e2)
    ev_sc = act.do(lambda a: a.activation(out=scT[:].rearrange("p k j -> p (k j)"), in_=csb[:], func=AF.Silu), sig=True)
    wada = k.dram["w_ada"].ap()
    mv = k.dram["mvec"].ap()

    def load(n):
        s = ws.next()
        pool.wait(s.free_evs())
        s.ready = pool.dma(s.dsem, s.ap[:], wada[l, :, n * sw:(n + 1) * sw].rearrange("(k p) n -> p k n", p=128))
        b = bs.next()
        sp.wait(b.free_evs())
        b.ready = sp.dma(b.dsem, b.ap[:], bc_ap(k.dram["b_ada"], l * 12288 + n * sw, sw, parts=2))
        return s, b

    def mm(b, cur, kk):
        return lambda t: t.matmul(b.ap[0:2, 0:sw], lhsT=scT[:, kk, :], rhs=cur.ap[:, kk, :], start=(kk == 0), stop=(kk == 15))

    ns = 12288 // sw
    nxt = load(0)
    for n in range(ns):
        cur, bsl = nxt
        if n + 1 < ns:
            nxt = load(n + 1)
        b = pb.next()
        pe.wait(cur.ready, ev_sc, b.free_evs())
        for kk in range(16):
            ev = pe.do(mm(b, cur, kk), sig=(kk == 15))
        cur.readers.append(ev)
        m = ms.next()
        dve.wait(ev, bsl.ready, m.free_evs())
        ev2 = dve.do(lambda v, b=b, m=m, bsl=bsl: v.tensor_tensor(out=m.ap[:], in0=b.ap[0:2, 0:sw], in1=bsl.ap[:], op=ALU.add), sig=True)
        b.readers.append(ev2)
        bsl.readers.append(ev2)
        k.store(sp, m, mv[l][:, n * sw:(n + 1) * sw], m.ap[:], ev2)
        yield n


def phase_ada(k, l):
    with ExitStack() as es:
        for _ in ada_steps(k, es, l):
            pass
        k.flush()


def ln_stats(k, xin_ap, mv, rstd, nmr, eps, stats):
    dve, act = k.dve, k.act
    for c in range(4):
        ev = dve.do(lambda v, c=c: v.bn_stats(out=stats[:, c, :], in_=xin_ap[:, c * 512:(c + 1) * 512]), sig=(c == 3))
    dve.wait(ev)
    ev = dve.do(lambda v: v.bn_aggr(out=mv, in_=stats), sig=True)
    act.wait(ev)
    ev = act.do(lambda a: a.activation(out=rstd, in_=mv[:, 1:2], func=AF.Sqrt, bias=eps, scale=1.0), sig=True)
    dve.wait(ev)
    ev = dve.do(lambda v: v.reciprocal(out=rstd, in_=rstd), sig=True)
    dve.wait(ev)
    return dve.do(lambda v: v.tensor_scalar(out=nmr, in0=mv[:, 0:1], scalar1=-1.0, scalar2=rstd, op0=ALU.mult, op1=ALU.mult), sig=True)


def phase_mod(k, l, sub, src, hT, ident, ntiles):
    pe, act, dve, pool, sp = k.pe, k.act, k.dve, k.pool, k.sp
    mv_t = k.dram["mvec"]
    with ExitStack() as es:
        bcs = {}
        d0 = k.dsem()
        evb = None
        for j, nm in ((0, "lat"), (1, "ctx")):
            sh = k.sb(es, "mod_sh" + nm, [128, 2048], F32)
            s1 = k.sb(es, "mod_s1" + nm, [128, 2048], F32)
            base = (l * 2 + j) * 12288 + sub * 3 * 2048
            sp.dma(d0, sh[:], bc_ap(mv_t, base, 2048))
            evb = sp.dma(d0, s1[:], bc_ap(mv_t, base + 2048, 2048))
            bcs[j] = (sh, s1)
        dve.wait(evb)
        for j in (0, 1):
            s1 = bcs[j][1]
            evb2 = dve.do(lambda v, s1=s1: v.tensor_scalar(out=s1[:], in0=s1[:], scalar1=1.0, scalar2=None, op0=ALU.add), sig=True)
        xr = Ring([Slot(k.sb(es, "mod_x%d" % i, [128, 2048], F32), k.dsem()) for i in range(2)])
        xn = Ring([Slot(k.sb(es, "mod_xn%d" % i, [128, 2048], F32)) for i in range(2)])
        hb = Ring([Slot(k.sb(es, "mod_hb%d" % i, [128, 2048], BF16)) for i in range(3)])
        sm = Ring([Slot(k.sb(es, "mod_sm%d" % i, [128, 32], F32)) for i in range(2)])
        pt = Ring([Slot(k.ps(es, "mod_pt%d" % i, [128, 512], BF16)) for i in range(4)])

        def load(i):
            s = xr.next()
            sp.wait(s.free_evs())
            s.ready = sp.dma(s.dsem, s.ap[:], src(i))
            return s

        def emit_tr(hs, ev, i):
            for jj in range(4):
                p = pt.next()
                pe.wait(ev, p.free_evs())
                for a in range(4):
                    kk = 4 * jj + a
                    evp = pe.do(lambda t, p=p, hs=hs, kk=kk, a=a: t.transpose(out=p.ap[:, a * 128:(a + 1) * 128], in_=hs.ap[:, kk * 128:(kk + 1) * 128], identity=ident[:]), sig=(a == 3))
                eng = act if jj % 2 == 0 else dve
                eng.wait(evp)
                evc = evac_copy(k, eng, hT[:, 4 * jj:4 * jj + 4, i * 128:(i + 1) * 128], p.ap[:].rearrange("p (a b) -> p a b", a=4))
                p.readers.append(evc)
            hs.readers.append(evp)

        pend = []
        nxt = load(0)
        for i in range(ntiles):
            cur = nxt
            if i + 1 < ntiles:
                nxt = load(i + 1)
            j = 0 if i < NLT else 1
            sh, s1 = bcs[j]
            smt = sm.next()
            st = smt.ap
            stats = st[:, 0:24].rearrange("p (c s) -> p c s", s=6)
            mv, rstd, nmr = st[:, 24:26], st[:, 26:27], st[:, 27:28]
            dve.wait(cur.ready, smt.free_evs())
            ev = ln_stats(k, cur.ap, mv, rstd, nmr, ADA_EPS, stats)
            xs = xn.next()
            act.wait(ev, xs.free_evs())
            ev = act.do(lambda a, xs=xs, cur=cur, rstd=rstd, nmr=nmr: a.activation(out=xs.ap[:], in_=cur.ap[:], func=AF.Identity, bias=nmr, scale=rstd), sig=True)
            cur.readers.append(ev)
            smt.readers.append(ev)
            dve.wait(ev, evb2)
            ev = dve.do(lambda v, xs=xs, s1=s1: v.tensor_tensor(out=xs.ap[:], in0=xs.ap[:], in1=s1[:], op=ALU.mult), sig=True)
            hs = hb.next()
            pool.wait(ev, hs.free_evs())
            ev = pool.do(lambda g, xs=xs, hs=hs, sh=sh: g.tensor_tensor(out=hs.ap[:], in0=xs.ap[:], in1=sh[:], op=ALU.add), sig=True)
            xs.readers.append(ev)
            pend.append((hs, ev, i))
            if len(pend) > 1:
                emit_tr(*pend.pop(0))
        while pend:
            emit_tr(*pend.pop(0))
        k.flush()


def declare(k):
    k.inp("x", [S, D])
    k.inp("ctx", [CL, D])
    k.inp("cT", [128, 16, 2])
    k.inp("w_ada", [L, D, 6 * D])
    k.inp("b_ada", [L, 6 * D])
    k.inp("w_in", [L, D, N_IN])
    k.inp("mla_kv_norm", [L, 512])
    k.inp("w_mla_ukv", [L, 512, 4096])
    k.inp("gqa_q_norm", [L, 128])
    k.inp("gqa_k_norm", [L, 128])
    k.inp("na_bias", [L, H, 128, 7680])
    k.inp("w_branch", [L, 3, D, D])
    k.inp("w_out", [L, D, D])
    k.inp("ln_a_g", [L, D])
    k.inp("ln_a_b", [L, D])
    k.inp("w_up", [L, D, 2 * DFF])
    k.inp("convp", [L, 128, NFC, 4])
    k.inp("w_down", [L, DFF, D])
    k.inp("ln_f_g", [L, D])
    k.inp("ln_f_b", [L, D])
    k.inp("ropeM", [2, 128, NLT, 64])
    k.inp("ropeG", [2, 128, NLT, 128])
    k.out = k.nc.dram_tensor("out", [S, D], F32, kind="ExternalOutput")
    k.scr("mvec", [L, 2, 6 * D], F32)
    k.scr("xres", [T, D], F32)
    k.scr("fout", [T, D], F32)
    k.scr("hT_dbg", [D, T], BF16)
    k.scr("qa_nT", [H, 128, T])
    k.scr("qa_rT", [H // 2, 128, T])
    k.scr("ka_nT", [H, 128, T])
    k.scr("ka_rT", [128, T])
    k.scr("cgT", [512, T])
    k.scr("va", [T, D])
    k.scr("qbT", [H, 128, T])
    k.scr("kbT", [H, 128, T])
    k.scr("vb", [T, D])
    k.scr("qcT", [H, 128, T])
    k.scr("kcT", [4, 128, T])
    k.scr("vc", [T, 512])
    k.scr("gateT", [3, D, T])
    k.scr("yT", [3, H, 128, T])
    k.scr("accT", [D, T])
    k.scr("hidT", [DFF, T])
    k.scr("rstd_dbg", [128, NT], F32)


def make_ident(k, es):
    ident = k.sb(es, "ident", [128, 128], BF16)
    k.pool.do(lambda g: g.memset(ident[:], 0.0))
    k.pool.do(lambda g: g.affine_select(out=ident[:], in_=ident[:], pattern=[[-1, 128]], compare_op=ALU.not_equal, fill=1.0, base=0, channel_multiplier=1))
    return ident


def build(dbg=(), stop_after=None):
    k = K(dbg, stop_after)
    declare(k)
    sp = k.sp
    with ExitStack() as es:
        ident = make_ident(k, es)
        k.flush()
        for l in range(L):
            phase_ada(k, l)
        if stop_after == "ada":
            return k
        xin, cin, xres = k.dram["x"].ap(), k.dram["ctx"].ap(), k.dram["xres"].ap()
        for l in range(L):
            ntq = NT if l < L - 1 else NLT
            if l == 0:
                src = lambda i: (xin[i * 128:(i + 1) * 128, :] if i < NLT else cin[(i - NLT) * 128:(i - NLT + 1) * 128, :])
            else:
                src = lambda i: xres[i * 128:(i + 1) * 128, :]
            with ExitStack() as es2:
                hT = k.sb(es2, "hT", [128, 16, T], BF16)
                phase_mod(k, l, 0, src, hT, ident, NT)
                if "hT_dbg" in k.dbg and l == 0:
                    st = Slot(None, k.dsem())
                    k.store(sp, st, k.dram["hT_dbg"].ap().rearrange("(k p) t -> p k t", p=128), hT[:], None)
                    k.flush()
                if stop_after == "mod":
                    return k
                rms_all = k.sb(es2, "rms_all", [128, NT], F32)
                rstd_all = k.sb(es2, "rstd_all", [128, NT], F32)
                phase_win(k, l, hT, ident, ntq, rms_all, rstd_all)
                if stop_after == "win":
                    return k
                phase_ukv(k, l, rstd_all)
                if stop_after == "ukv":
                    return k
                for br in k.branches:
                    phase_attn(k, l, br, ident, ntq, rstd_all)
                if stop_after == "attn":
                    return k
            phase_merge(k, l, ntq)
            if stop_after == "merge":
                return k
            phase_out_resid(k, l, src, ntq)
            if stop_after == "mixer":
                return k
            srcx = lambda i: xres[i * 128:(i + 1) * 128, :]
            with ExitStack() as es2:
                hT = k.sb(es2, "hT", [128, 16, T], BF16)
                phase_mod(k, l, 1, srcx, hT, ident, ntq)
                phase_ffn_up(k, l, hT, ntq)
            if stop_after == "ffnup":
                return k
            phase_tm_out(k, l, "down", ntq, xsrc=srcx)
            phase_resid(k, l, 1, srcx, ntq, l == L - 1, pre_added=True)
            if stop_after == "layer%d" % l:
                return k
    return k


def host_inputs(inputs):
    f32 = np.float32
    common = {}
    for n in ("w_ada", "b_ada", "w_in", "mla_kv_norm", "w_mla_ukv", "gqa_q_norm", "gqa_k_norm", "w_branch",
              "w_out", "ln_a_g", "ln_a_b", "w_up", "w_down", "ln_f_g", "ln_f_b"):
        common[n] = np.ascontiguousarray(inputs[n], dtype=f32)
    cw = np.asarray(inputs["conv_w"], f32)
    cb = np.asarray(inputs["conv_b"], f32)
    cp = np.concatenate([cw, cb[:, None, :]], axis=1)
    common["convp"] = np.ascontiguousarray(cp.reshape(L, 4, NFC, 128).transpose(0, 3, 2, 1))
    rpb = np.asarray(inputs["na_rpb"], f32)
    a_ = np.arange(2)[:, None, None, None]
    kc = np.arange(64)[None, :, None, None]
    t_ = np.arange(24)[None, None, :, None]
    qc = np.arange(64)[None, None, None, :]
    dr = a_ - t_ + 11
    cs = np.clip(qc - 8, 0, 48)
    colv = (kc >= cs) & (kc < cs + 16)
    shp = (2, 64, 24, 64)
    dri = np.broadcast_to(np.clip(dr + 7, 0, 14), shp)
    dci = np.broadcast_to(np.clip(kc - qc + 15, 0, 30), shp)
    nb = rpb[:, :, dri, dci]
    valid = np.broadcast_to(((dr >= -4) & (dr <= 3)) & colv, shp)
    tab_int = np.where(valid[None, None], nb, f32(NEG)).astype(f32).reshape(L, H, 128, 1536)
    tabs = [tab_int]
    bq = np.arange(8)[None, None, :, None]
    for J, i0 in ((0, 0), (3, 10)):
        per_tile = []
        for ii in range(6):
            r = 8 * J + bq
            kr = 2 * (i0 + ii) + a_
            rs_ = np.clip(r - 4, 0, 24)
            vrow = (kr >= rs_) & (kr < rs_ + 8)
            drb = kr - r
            shp2 = (2, 64, 8, 64)
            v2 = np.broadcast_to(vrow & colv, shp2)
            g = rpb[:, :, np.broadcast_to(np.clip(drb + 7, 0, 14), shp2), np.broadcast_to(np.clip(kc - qc + 15, 0, 30), shp2)]
            per_tile.append(np.where(v2[None, None], g, f32(NEG)).astype(f32).reshape(L, H, 128, 512))
        tabs.append(np.concatenate(per_tile, axis=-1))
    common["na_bias"] = np.ascontiguousarray(np.concatenate(tabs, axis=-1))
    tok = np.arange(S)
    rows, cols = (tok // 64).astype(f32), (tok % 64).astype(f32)

    def tables(quarter):
        fr = (f32(10000.0) ** (-np.arange(quarter, dtype=f32) / f32(quarter))).astype(f32)
        ar, ac = rows[:, None] * fr[None, :], cols[:, None] * fr[None, :]
        C = np.concatenate([np.cos(ar), np.cos(ar), np.cos(ac), np.cos(ac)], 1)
        Sg = np.concatenate([-np.sin(ar), np.sin(ar), -np.sin(ac), np.sin(ac)], 1)
        tb = np.stack([C, Sg], 0).astype(f32)
        return np.ascontiguousarray(tb.reshape(2, NLT, 128, 4 * quarter).transpose(0, 2, 1, 3))
    common["ropeM"] = tables(16)
    common["ropeG"] = tables(32)
    x = np.asarray(inputs["x"], f32)
    ctx = np.asarray(inputs["ctx"], f32)
    c = np.asarray(inputs["c"], f32)
    cc = np.asarray(inputs["c_ctx"], f32)
    maps = []
    for b in range(8):
        m = dict(common)
        m["x"] = np.ascontiguousarray(x[b])
        m["ctx"] = np.ascontiguousarray(ctx[b])
        c2 = np.stack([c[b], cc], 0)
        m["cT"] = np.ascontiguousarray(c2.reshape(2, 16, 128).transpose(2, 1, 0))
        maps.append(m)
    return maps


def kernel(**inputs):
    k = build()
    maps = host_inputs(inputs)
    res = run_bass_kernel_spmd(k.nc, maps, core_ids=list(range(8)))
    return np.stack([np.asarray(r["out"], np.float32) for r in res.results], 0)


def tok_groups(ntok):
    g, t0 = [], 0
    while t0 < ntok:
        n = min(512, ntok - t0)
        g.append((t0, n))
        t0 += n
    return g


class Proj:
    def __init__(self, k, es, A, KC, wcols, nslots, nbanks, pfx):
        self.k, self.A, self.KC = k, A, KC
        self.ws = Ring([Slot(k.sb(es, pfx + "_w%d" % i, [128, KC, wcols], BF16), k.dsem()) for i in range(nslots)])
        self.pb = Ring([Slot(k.ps(es, pfx + "_p%d" % i, [128, 512], F32)) for i in range(nbanks)])
        self.items = []

    def add(self, mode, src_fn, run_fn):
        self.items.append((mode, src_fn, run_fn))

    def load(self, idx):
        _, src_fn, _ = self.items[idx]
        s = self.ws.next()
        self.k.pool.wait(s.free_evs())
        pairs = src_fn(s)
        if isinstance(pairs, tuple):
            pairs = [pairs]
        for dst, src in pairs:
            s.ready = self.k.pool.dma(s.dsem, dst, src)
        return s

    def run(self):
        n = len(self.items)
        depth = len(self.ws.slots) - 1
        loaded = []
        for j in range(min(depth, n)):
            loaded.append(self.load(j))
        for idx in range(n):
            if idx + depth < n:
                loaded.append(self.load(idx + depth))
            self.items[idx][2](loaded[idx])
        self.items = []

    def tm_tile(self, slab, width, i, extra_wait=None):
        pe = self.k.pe
        b = self.pb.next()
        pe.wait(slab.ready, b.free_evs(), extra_wait)
        A, KC = self.A, self.KC
        for kk in range(KC):
            ev = pe.do(lambda t, b=b, kk=kk: t.matmul(b.ap[:, 0:width], lhsT=A[:, kk, i * 128:(i + 1) * 128], rhs=slab.ap[:, kk, 0:width],
                                                     start=(kk == 0), stop=(kk == KC - 1)), sig=(kk == KC - 1))
        slab.readers.append(ev)
        return b, ev

    def fm_group(self, slab, c, g0, n, kcs=None, extra_wait=None):
        pe = self.k.pe
        b = self.pb.next()
        pe.wait(slab.ready, b.free_evs(), extra_wait)
        A = self.A
        kcs = list(range(self.KC)) if kcs is None else kcs
        for j, kk in enumerate(kcs):
            ev = pe.do(lambda t, b=b, kk=kk, j=j: t.matmul(b.ap[:, 0:n], lhsT=slab.ap[:, j, c * 128:(c + 1) * 128], rhs=A[:, kk, g0:g0 + n],
                                                           start=(j == 0), stop=(j == len(kcs) - 1)), sig=(j == len(kcs) - 1))
        slab.readers.append(ev)
        return b, ev


def evac_copy(k, eng, out, in_, func=None, scale=None):
    if eng is k.act:
        f = func if func is not None else (AF.Copy if scale is None else AF.Identity)
        if scale is None:
            return eng.do(lambda a: a.activation(out=out, in_=in_, func=f), sig=True)
        return eng.do(lambda a: a.activation(out=out, in_=in_, func=f, scale=scale), sig=True)
    assert func is None
    if scale is None:
        return eng.do(lambda v: v.tensor_copy(out=out, in_=in_), sig=True)
    return eng.do(lambda v: v.tensor_scalar(out=out, in0=in_, scalar1=scale, scalar2=None, op0=ALU.mult), sig=True)


class FMOut:
    def __init__(self, k, es, nslots, pfx):
        self.k = k
        self.ring = Ring([Slot(k.sb(es, pfx + "_fs%d" % i, [128, T], BF16), k.dsem()) for i in range(nslots)])
        self.cur, self.evs, self.tog = None, [], 0

    def evac(self, b, ev, g0, n, first, last, dst, ntok, func=None):
        k = self.k
        if first:
            self.cur = self.ring.next()
            self.evs = []
            self.fe = self.cur.free_evs()
        eng = k.act if (func is not None or self.tog % 2 == 0) else k.dve
        self.tog += 1
        eng.wait(ev, self.fe)
        e2 = evac_copy(k, eng, self.cur.ap[:, g0:g0 + n], b.ap[:, 0:n], func=func)
        b.readers.append(e2)
        self.evs.append(e2)
        if last:
            k.store(k.sp, self.cur, dst[:, 0:ntok], self.cur.ap[:, 0:ntok], self.evs)


class TOut:
    def __init__(self, k, es, ident, nslots, pfx):
        self.k, self.ident = k, ident
        self.ring = Ring([Slot(k.sb(es, pfx + "_ts%d" % i, [128, 4, T], BF16), k.dsem()) for i in range(nslots)])
        self.pt = Ring([Slot(k.ps(es, pfx + "_tp%d" % i, [128, 512], BF16)) for i in range(2)])
        self.tog = 0
        self.q = []
        self.DELAY = 2

    def begin(self):
        self.cur = self.ring.next()
        self.fe = self.cur.free_evs()
        self.evs = []

    def tile(self, o_slot, o_ev, nblk, i):
        self.q.append((o_slot, o_ev, nblk, i))
        while len(self.q) > self.DELAY:
            self._emit(*self.q.pop(0))

    def _emit(self, o_slot, o_ev, nblk, i):
        k, pe, ident = self.k, self.k.pe, self.ident
        p = self.pt.next()
        pe.wait(o_ev, p.free_evs())
        for j in range(nblk):
            evp = pe.do(lambda t, j=j, p=p: t.transpose(out=p.ap[:, j * 128:(j + 1) * 128], in_=o_slot.ap[:, j * 128:(j + 1) * 128], identity=ident[:]), sig=(j == nblk - 1))
        o_slot.readers.append(evp)
        eng = k.act if self.tog % 2 == 0 else k.dve
        self.tog += 1
        eng.wait(evp, self.fe)
        e2 = evac_copy(k, eng, self.cur.ap[:, 0:nblk, i * 128:(i + 1) * 128], p.ap[:, 0:nblk * 128].rearrange("p (a b) -> p a b", b=128))
        p.readers.append(e2)
        self.evs.append(e2)

    def finish(self, dsts, ntok):
        k = self.k
        while self.q:
            self._emit(*self.q.pop(0))
        for j, d in enumerate(dsts):
            k.store(k.sp, self.cur, d[:, 0:ntok], self.cur.ap[:, j, 0:ntok], self.evs)


def tab_bc(tab, i, nh, Dh, sub=None):
    pstep = NLT * Dh
    if sub is None:
        return bass.AP(tab, i * Dh, [[pstep, 128], [0, nh], [1, Dh]])
    b, q = sub
    return bass.AP(tab, i * Dh + b * q, [[pstep, 128], [0, nh], [2 * q, 2], [1, q]])


def rope_emit(k, x_ap, x_ev, nh, Dh, tabC, tabS, i, t1, t2, o_ap, o_free, sc=None):
    dve, pool = k.dve, k.pool
    q = Dh // 4
    W = nh * Dh
    xv = x_ap.rearrange("p (h a b q) -> p h a b q", h=nh, a=2, b=2)
    t1v = t1.ap[:, 0:W].rearrange("p (h d) -> p h d", h=nh)
    t2v = t2.ap[:, 0:W].rearrange("p (h a b q) -> p h a b q", h=nh, a=2, b=2)
    dve.wait(x_ev, t1.free_evs(), t2.free_evs())
    if sc is None:
        dve.do(lambda v: v.tensor_tensor(out=t1v, in0=x_ap.rearrange("p (h d) -> p h d", h=nh), in1=tab_bc(tabC, i, nh, Dh), op=ALU.mult))
        dve.do(lambda v: v.tensor_tensor(out=t2v[:, :, :, 0, :], in0=xv[:, :, :, 1, :], in1=tab_bc(tabS, i, nh, Dh, (0, q)), op=ALU.mult))
        ev = dve.do(lambda v: v.tensor_tensor(out=t2v[:, :, :, 1, :], in0=xv[:, :, :, 0, :], in1=tab_bc(tabS, i, nh, Dh, (1, q)), op=ALU.mult), sig=True)
    else:
        assert nh == 1
        Cv = bass.AP(tabC, i * Dh, [[NLT * Dh, 128], [1, Dh]])
        dve.do(lambda v: v.scalar_tensor_tensor(out=t1.ap[:, 0:W], in0=x_ap, scalar=sc, in1=Cv, op0=ALU.mult, op1=ALU.mult))
        for b in (0, 1):
            Sv = bass.AP(tabS, i * Dh + b * q, [[NLT * Dh, 128], [2 * q, 2], [1, q]])
            ev = dve.do(lambda v, b=b, Sv=Sv: v.scalar_tensor_tensor(out=t2v[:, 0, :, b, :], in0=xv[:, 0, :, 1 - b, :], scalar=sc, in1=Sv, op0=ALU.mult, op1=ALU.mult), sig=(b == 1))
    pool.wait(ev, o_free)
    ev2 = pool.do(lambda g: g.tensor_tensor(out=o_ap, in0=t1.ap[:, 0:W], in1=t2.ap[:, 0:W], op=ALU.add), sig=True)
    t1.readers.append(ev2)
    t2.readers.append(ev2)
    return ev2, [ev]


def phase_win(k, l, hT, ident, ntq, rms_all, rstd_all):
    pe, act, dve, pool, sp = k.pe, k.act, k.dve, k.pool, k.sp
    W = k.dram["w_in"].ap()[l]
    ntok_q = ntq * 128
    with ExitStack() as es:
        pj = Proj(k, es, hT, 16, 512, 2, 4, "win")
        fmo = FMOut(k, es, 2, "win")
        tout = TOut(k, es, ident, 2, "win")
        vst = Ring([Slot(k.sb(es, "win_vst%d" % i, [128, 512], BF16), k.dsem()) for i in range(2)])
        t1r = Ring([Slot(k.sb(es, "win_t1%d" % i, [128, 512], F32)) for i in range(2)])
        t2r = Ring([Slot(k.sb(es, "win_t2%d" % i, [128, 512], F32)) for i in range(2)])
        xgr = Ring([Slot(k.sb(es, "win_xg%d" % i, [128, 512], F32)) for i in range(3)])
        orr = Ring([Slot(k.sb(es, "win_o%d" % i, [128, 512], BF16)) for i in range(4)])
        smr = Ring([Slot(k.sb(es, "win_sm%d" % i, [128, 16], F32)) for i in range(4)])
        junk = k.sb(es, "win_junk", [128, 512], BF16)
        rM = [k.sb(es, "win_rM%d" % j, [128, NLT, 64], F32) for j in range(2)]
        rG = [k.sb(es, "win_rG%d" % j, [128, NLT, 128], F32) for j in range(2)]
        g512 = k.sb(es, "win_g512", [128, 512], F32)
        gqk = [k.sb(es, "win_gqk%d" % j, [128, 128], F32) for j in range(2)]
        d0 = k.dsem()
        for j in range(2):
            sp.dma(d0, rM[j][:], k.dram["ropeM"].ap()[j])
            sp.dma(d0, rG[j][:], k.dram["ropeG"].ap()[j])
        sp.dma(d0, g512[:], bc_ap(k.dram["mla_kv_norm"], l * 512, 512))
        sp.dma(d0, gqk[0][:], bc_ap(k.dram["gqa_q_norm"], l * 128, 128))
        ev_tab = sp.dma(d0, gqk[1][:], bc_ap(k.dram["gqa_k_norm"], l * 128, 128))
        for e in (act, dve, pool):
            e.wait(ev_tab)

        def src2d(c0, w):
            return lambda s: (s.ap[:, :, 0:w], W[:, c0:c0 + w].rearrange("(k p) n -> p k n", p=128))

        def src_mq(h0, nh, d0_, dw):
            v = W[:, 0:3072].rearrange("(k p) (h d) -> p k h d", p=128, d=192)
            return lambda s: [(s.ap[:, :, hh * dw:(hh + 1) * dw], v[:, :, h0 + hh, d0_:d0_ + dw]) for hh in range(nh)]

        def fm_run(nchunks, dsts, ntok, func=None):
            groups = tok_groups(ntok)

            def run(slab):
                for c in range(nchunks):
                    for gi, (g0, n) in enumerate(groups):
                        b, ev = pj.fm_group(slab, c, g0, n)
                        fmo.evac(b, ev, g0, n, gi == 0, gi == len(groups) - 1, dsts[c], ntok, func=func)
            return run

        def v_run(dst, width, ntiles):
            def run(slab):
                for i in range(ntiles):
                    b, ev = pj.tm_tile(slab, width, i)
                    st = vst.next()
                    act.wait(ev, st.free_evs())
                    e2 = evac_copy(k, act, st.ap[:, 0:width], b.ap[:, 0:width])
                    b.readers.append(e2)
                    k.store(sp, st, dst[i * 128:(i + 1) * 128, :], st.ap[:, 0:width], e2)
            return run

        def mqrope_run(h0):
            def run(slab):
                tout.begin()
                for i in range(ntq):
                    b, ev = pj.tm_tile(slab, 512, i)
                    o = orr.next()
                    if i < NLT:
                        e2, xr = rope_emit(k, b.ap[:, 0:512], ev, 8, 64, rM[0], rM[1], i, t1r.next(), t2r.next(), o.ap[:, 0:512], o.free_evs())
                        b.readers.extend(xr)
                    else:
                        act.wait(ev, o.free_evs())
                        e2 = evac_copy(k, act, o.ap[:, 0:512], b.ap[:, 0:512])
                        b.readers.append(e2)
                    tout.tile(o, e2, 4, i)
                tout.finish([k.dram["qa_rT"].ap()[h0 // 2 + j] for j in range(4)], ntok_q)
            return run

        def ckv_run(slab):
            tout.begin()
            for i in range(NT):
                b, ev = pj.tm_tile(slab, 512, i)
                sm = smr.next()
                act.wait(ev, sm.free_evs())
                e1 = act.do(lambda a, b=b, sm=sm: a.activation(out=junk[:], in_=b.ap[:, 0:512], func=AF.Square, accum_out=sm.ap[:, 0:1]), sig=True)
                act.wait(e1)
                e1 = act.do(lambda a, sm=sm, i=i: a.activation(out=rms_all[:, i:i + 1], in_=sm.ap[:, 0:1], func=AF.Sqrt, bias=RMS_EPS, scale=1.0 / 512), sig=True)
                sm.readers.append(e1)
                dve.wait(e1)
                e3 = dve.do(lambda v, i=i: v.reciprocal(out=rstd_all[:, i:i + 1], in_=rms_all[:, i:i + 1]), sig=True)
                o = orr.next()
                dve.wait(o.free_evs())
                e2 = dve.do(lambda v, b=b, o=o: v.tensor_tensor(out=o.ap[:, 0:512], in0=b.ap[:, 0:512], in1=g512[:], op=ALU.mult), sig=True)
                b.readers.extend([e1, e2])
                tout.tile(o, e2, 4, i)
            k.ev_rstd = e3
            tout.finish([k.dram["cgT"].ap()[j * 128:(j + 1) * 128, :] for j in range(4)], T)

        def kr_run(slab):
            tout.begin()
            for i in range(NT):
                b, ev = pj.tm_tile(slab, 64, i)
                o = orr.next()
                sc = rms_all[:, i:i + 1]
                dve.wait(k.ev_rstd)
                if i < NLT:
                    e2, xr = rope_emit(k, b.ap[:, 0:64], ev, 1, 64, rM[0], rM[1], i, t1r.next(), t2r.next(), o.ap[:, 0:64], o.free_evs(), sc=sc)
                    b.readers.extend(xr)
                    pool.wait(e2)
                    e2 = pool.do(lambda g, o=o: g.tensor_copy(out=o.ap[:, 64:128], in_=o.ap[:, 0:64]), sig=True)
                else:
                    dve.wait(ev, o.free_evs())
                    dve.do(lambda v, b=b, o=o, sc=sc: v.tensor_scalar(out=o.ap[:, 0:64], in0=b.ap[:, 0:64], scalar1=sc, scalar2=None, op0=ALU.mult))
                    e2 = dve.do(lambda v, b=b, o=o, sc=sc: v.tensor_scalar(out=o.ap[:, 64:128], in0=b.ap[:, 0:64], scalar1=sc, scalar2=None, op0=ALU.mult), sig=True)
                    b.readers.append(e2)
                tout.tile(o, e2, 1, i)
            tout.finish([k.dram["ka_rT"].ap()], T)

        def gqa_run(gvec, dsts, ntiles):
            def run(slab):
                tout.begin()
                for i in range(ntiles):
                    b, ev = pj.tm_tile(slab, 512, i)
                    sm = smr.next()
                    act.wait(ev, sm.free_evs())
                    for hh in range(4):
                        e1 = act.do(lambda a, b=b, sm=sm, hh=hh: a.activation(out=junk[:, 0:128], in_=b.ap[:, hh * 128:(hh + 1) * 128], func=AF.Square, accum_out=sm.ap[:, hh:hh + 1]), sig=(hh == 3))
                    act.wait(e1)
                    e1 = act.do(lambda a, sm=sm: a.activation(out=sm.ap[:, 4:8], in_=sm.ap[:, 0:4], func=AF.Sqrt, bias=RMS_EPS, scale=1.0 / 128), sig=True)
                    dve.wait(e1)
                    e3 = dve.do(lambda v, sm=sm: v.reciprocal(out=sm.ap[:, 8:12], in_=sm.ap[:, 4:8]), sig=True)
                    xg = xgr.next()
                    dve.wait(e3, xg.free_evs())
                    for hh in range(4):
                        e4 = dve.do(lambda v, b=b, sm=sm, hh=hh, xg=xg: v.scalar_tensor_tensor(out=xg.ap[:, hh * 128:(hh + 1) * 128], in0=b.ap[:, hh * 128:(hh + 1) * 128],
                                                                                              scalar=sm.ap[:, 8 + hh:9 + hh], in1=gvec[:], op0=ALU.mult, op1=ALU.mult), sig=(hh == 3))
                    b.readers.extend([e1, e4])
                    sm.readers.append(e4)
                    o = orr.next()
                    if i < NLT:
                        e2, xr = rope_emit(k, xg.ap[:, 0:512], e4, 4, 128, rG[0], rG[1], i, t1r.next(), t2r.next(), o.ap[:, 0:512], o.free_evs())
                        xg.readers.extend(xr)
                    else:
                        pool.wait(e4, o.free_evs())
                        e2 = pool.do(lambda g, xg=xg, o=o: g.tensor_copy(out=o.ap[:, 0:512], in_=xg.ap[:, 0:512]), sig=True)
                        xg.readers.append(e2)
                    tout.tile(o, e2, 4, i)
                tout.finish(dsts, ntiles * 128)
            return run

        D_ = k.dram
        pj.add("tm", src2d(O_CKV, 512), ckv_run)
        pj.add("tm", src2d(O_KR, 64), kr_run)
        for h0 in (0, 8):
            pj.add("tm", src_mq(h0, 8, 128, 64), mqrope_run(h0))
        for h0 in range(0, H, 4):
            pj.add("fm", src_mq(h0, 4, 0, 128), fm_run(4, [D_["qa_nT"].ap()[h0 + j] for j in range(4)], ntok_q))
        for h0 in range(0, H, 4):
            pj.add("fm", src2d(O_NA + h0 * 128, 512), fm_run(4, [D_["qbT"].ap()[h0 + j] for j in range(4)], ntok_q))
        for h0 in range(0, H, 4):
            pj.add("fm", src2d(O_NA + 2048 + h0 * 128, 512), fm_run(4, [D_["kbT"].ap()[h0 + j] for j in range(4)], T))
        for c0 in range(0, 2048, 512):
            pj.add("tm", src2d(O_NA + 4096 + c0, 512), v_run(D_["vb"].ap()[:, c0:c0 + 512], 512, NT))
        for h0 in range(0, H, 4):
            pj.add("tm", src2d(O_GQ + h0 * 128, 512), gqa_run(gqk[0], [D_["qcT"].ap()[h0 + j] for j in range(4)], ntq))
        pj.add("tm", src2d(O_GKV, 512), gqa_run(gqk[1], [D_["kcT"].ap()[j] for j in range(4)], NT))
        pj.add("tm", src2d(O_GKV + 512, 512), v_run(D_["vc"].ap(), 512, NT))
        for br in range(3):
            for c0 in range(0, 2048, 512):
                pj.add("fm", src2d(O_GATE + br * 2048 + c0, 512),
                       fm_run(4, [D_["gateT"].ap()[br, c0 + j * 128:c0 + (j + 1) * 128, :] for j in range(4)], ntok_q, func=AF.Sigmoid))
        pj.run()
        if "rstd_dbg" in k.dbg:
            st = Slot(None, k.dsem())
            k.store(sp, st, k.dram["rstd_dbg"].ap(), rstd_all[:], k.ev_rstd)
        k.flush()


def phase_ukv(k, l, rstd_all):
    pe, act, dve, pool, sp = k.pe, k.act, k.dve, k.pool, k.sp
    with ExitStack() as es:
        A2 = k.sb(es, "ukv_A", [128, 4, T], BF16)
        Wu = k.sb(es, "ukv_W", [128, 4, 4096], BF16)
        pb = Ring([Slot(k.ps(es, "ukv_p%d" % i, [128, 512], F32)) for i in range(4)])
        fmo = FMOut(k, es, 2, "ukv")
        vst = Ring([Slot(k.sb(es, "ukv_vst%d" % i, [128, 512], BF16), k.dsem()) for i in range(2)])
        d0, d1 = k.dsem(), k.dsem()
        evA = sp.dma(d0, A2[:], k.dram["cgT"].ap().rearrange("(k p) t -> p k t", p=128))
        evW = pool.dma(d1, Wu[:], k.dram["w_mla_ukv"].ap()[l].rearrange("(k p) n -> p k n", p=128))
        groups = tok_groups(T)
        for h in range(H):
            for gi, (g0, n) in enumerate(groups):
                b = pb.next()
                pe.wait(evA, evW, b.free_evs())
                for kk in range(4):
                    ev = pe.do(lambda t, b=b, kk=kk, h=h, g0=g0, n=n: t.matmul(b.ap[:, 0:n], lhsT=Wu[:, kk, h * 256:h * 256 + 128], rhs=A2[:, kk, g0:g0 + n],
                                                                              start=(kk == 0), stop=(kk == 3)), sig=(kk == 3))
                fmo.evac(b, ev, g0, n, gi == 0, gi == len(groups) - 1, k.dram["ka_nT"].ap()[h], T)
        for i in range(NT):
            for hg in range(4):
                b = pb.next()
                pe.wait(b.free_evs())
                for hh in range(4):
                    c0 = (hg * 4 + hh) * 256 + 128
                    for kk in range(4):
                        ev = pe.do(lambda t, b=b, kk=kk, hh=hh, c0=c0, i=i: t.matmul(b.ap[:, hh * 128:(hh + 1) * 128], lhsT=A2[:, kk, i * 128:(i + 1) * 128], rhs=Wu[:, kk, c0:c0 + 128],
                                                                                     start=(kk == 0), stop=(kk == 3)), sig=(kk == 3 and hh == 3))
                st = vst.next()
                act.wait(ev, st.free_evs())
                e2 = evac_copy(k, act, st.ap[:, :], b.ap[:, :], scale=rstd_all[:, i:i + 1])
                b.readers.append(e2)
                k.store(sp, st, k.dram["va"].ap()[i * 128:(i + 1) * 128, hg * 512:(hg + 1) * 512], st.ap[:, :], e2)
        k.flush()


def na_groups():
    gs = []
    for J in range(4):
        rs = [min(max(r - 4, 0), 24) for r in range(8 * J, 8 * J + 8)]
        lo, hi = min(rs), max(rs) + 8
        kts = []
        for ii, i in enumerate(range(lo // 2, (hi + 1) // 2)):
            if J in (0, 3):
                c0 = 1536 + (0 if J == 0 else 3072) + ii * 512
            else:
                c0 = (7 - 2 * (i - 4 * J) + 4) * 64
            kts.append((i, c0, None))
        kts += [(16, None, None), (17, None, None)]
        gs.append((J * 512, 512, kts))
    return gs


def phase_attn(k, l, br, ident, ntq, rstd_all, ada_l=None):
    pe, act, dve, pool, sp = k.pe, k.act, k.dve, k.pool, k.sp
    D_ = k.dram
    ntok_q = ntq * 128
    mla, na = br == 0, br == 1
    with ExitStack() as es:
        slots = []
        for i in range(2):
            s = Slot(None, k.dsem())
            s.QN = k.sb(es, "at_qn%d" % i, [128, T], BF16)
            s.KN = k.sb(es, "at_kn%d" % i, [128, T], BF16)
            s.V = k.sb(es, "at_v%d" % i, [128, NT, 129], BF16)
            if mla:
                s.QR = k.sb(es, "at_qr%d" % i, [128, T], BF16)
            if na:
                s.EE = k.sb(es, "at_ee%d" % i, [128, 7680], BF16)
            slots.append(s)
        EEraw = Slot(k.sb(es, "at_eeraw", [128, 7680], F32), k.dsem()) if na else None
        ev_init = None
        for s in slots:
            ev_init = pool.do(lambda g, s=s: g.memset(s.V[:, :, 128:129], 1.0), sig=True)
        sp.wait(ev_init)
        hring = Ring(slots)
        ev_kr = None
        if mla:
            KRb = k.sb(es, "at_krb", [128, T], BF16)
            sc_all = k.sb(es, "at_sc", [128, NT], F32)
            dk = k.dsem()
            ev_kr = sp.dma(dk, KRb[:], D_["ka_rT"].ap())
            ev_sc = dve.do(lambda v: v.tensor_scalar(out=sc_all[:], in0=rstd_all[:], scalar1=MLA_SCALE, scalar2=None, op0=ALU.mult), sig=True)
            act.wait(ev_sc)
        la = 3 if na else 2
        Sb = Ring([Slot(k.ps(es, "at_s%d" % i, [128, 512], F32)) for i in range(la + 1)])
        Yb = Ring([Slot([k.ps(es, "at_y%d_%d" % (i, j), [128, 512], F32) for j in range(2)]) for i in range(1 if na else 2)])
        ada = ada_steps(k, es, ada_l, nbanks=1) if ada_l is not None else None
        Tb = Ring([Slot(k.ps(es, "at_t%d" % i, [128, 512], BF16)) for i in range(1)])
        PT = Ring([Slot(k.sb(es, "at_pt%d" % i, [128, 512], BF16)) for i in range(la + 2)])
        PR = Ring([Slot(k.sb(es, "at_pr%d" % i, [128, 512], BF16)) for i in range(la + 1)]) if na else None
        Yn = Ring([Slot(k.sb(es, "at_yn%d" % i, [128, 4, 128], BF16)) for i in range(2)])
        Rc = Ring([Slot(k.sb(es, "at_rc%d" % i, [128, 4], F32)) for i in range(2)])
        YT = Ring([Slot(k.sb(es, "at_yt%d" % i, [128, T], BF16), k.dsem()) for i in range(2)])

        def load_head(h):
            s = hring.next()
            fe = s.free_evs()
            sp.wait(fe)
            if br == 0:
                base = (h % 2) * 64
                sp.dma(s.dsem, s.QN[:, 0:ntok_q], D_["qa_nT"].ap()[h][:, 0:ntok_q])
                sp.dma(s.dsem, s.QR[base:base + 64, 0:ntok_q], D_["qa_rT"].ap()[h // 2][base:base + 64, 0:ntok_q])
                sp.dma(s.dsem, s.KN[:], D_["ka_nT"].ap()[h])
                vsrc = D_["va"].ap()[:, h * 128:(h + 1) * 128]
            elif br == 1:
                sp.dma(s.dsem, s.QN[:, 0:ntok_q], D_["qbT"].ap()[h][:, 0:ntok_q])
                sp.dma(s.dsem, s.KN[:], D_["kbT"].ap()[h])
                sp.wait(EEraw.free_evs())
                ev_raw = sp.dma(EEraw.dsem, EEraw.ap[:], D_["na_bias"].ap()[l, h])
                vsrc = D_["vb"].ap()[:, h * 128:(h + 1) * 128]
            else:
                sp.dma(s.dsem, s.QN[:, 0:ntok_q], D_["qcT"].ap()[h][:, 0:ntok_q])
                sp.dma(s.dsem, s.KN[:], D_["kcT"].ap()[h // 4])
                vsrc = D_["vc"].ap()[:, (h // 4) * 128:(h // 4 + 1) * 128]
            s.ready = sp.dma(s.dsem, s.V[:, :, 0:128], vsrc.rearrange("(i p) e -> p i e", p=128))
            s.ee_ev = None
            if na:
                s.ee_ev = None
                for c in range(15):
                    ee_chunks.append((s, c, ev_raw, fe))
            return s

        def head_groups():
            gs = []
            if na:
                gs.extend(na_groups())
            else:
                for g in range(4):
                    gs.append((g * 512, 512, [(i, None, None) for i in range(NT)]))
            if ntq == NT:
                gs.append((S, 256, [(16, None, None), (17, None, None)]))
            return gs

        steps = []
        for h in range(H):
            for gidx, (q0, n, kts) in enumerate(head_groups()):
                for ki, (kt, t0, inv) in enumerate(kts):
                    steps.append(dict(h=h, g=gidx, q0=q0, n=n, kt=kt, t0=t0, inv=inv, first=(ki == 0), last=(ki == len(kts) - 1),
                                      lastg=(gidx == len(head_groups()) - 1)))
        hs = {}
        state = dict(Y=None, yt=None)
        scale_c = NA_SCALE if na else GQA_SCALE

        def emit_qk(st):
            h = st["h"]
            if h not in hs:
                hs[h] = load_head(h)
            s = hs[h]
            sb_ = Sb.next()
            st["S"] = sb_
            q0, n, kt = st["q0"], st["n"], st["kt"]
            pe.wait(s.ready, ev_kr, sb_.free_evs())
            if mla:
                base = (h % 2) * 64
                pe.do(lambda t: t.matmul(sb_.ap[:, 0:n], lhsT=s.KN[:, kt * 128:(kt + 1) * 128], rhs=s.QN[:, q0:q0 + n], start=True, stop=False))
                st["qk_ev"] = pe.do(lambda t: t.matmul(sb_.ap[:, 0:n], lhsT=KRb[base:base + 64, kt * 128:(kt + 1) * 128], rhs=s.QR[base:base + 64, q0:q0 + n], start=False, stop=True), sig=True)
            else:
                st["qk_ev"] = pe.do(lambda t: t.matmul(sb_.ap[:, 0:n], lhsT=s.KN[:, kt * 128:(kt + 1) * 128], rhs=s.QN[:, q0:q0 + n], start=True, stop=True), sig=True)

        ee_chunks = []

        def emit_ee_chunk():
            s, c, ev_raw, fe = ee_chunks.pop(0)
            act.wait(ev_raw, fe)
            e = act.do(lambda a: a.activation(out=s.EE[:, c * 512:(c + 1) * 512], in_=EEraw.ap[:, c * 512:(c + 1) * 512], func=AF.Exp), sig=True)
            if c == 14:
                s.ee_ev = e
                EEraw.readers.append(e)

        def emit_exp(st):
            s, sb_, n, kt = hs[st["h"]], st["S"], st["n"], st["kt"]
            while ee_chunks and ee_chunks[0][0] is s:
                emit_ee_chunk()
            pt = PT.next()
            st["PT"] = pt
            if st["t0"] is None:
                act.wait(st["qk_ev"], pt.free_evs())
                sc = sc_all[:, kt:kt + 1] if mla else scale_c
                e = act.do(lambda a: a.activation(out=pt.ap[:, 0:n], in_=sb_.ap[:, 0:n], func=AF.Exp, scale=sc), sig=True)
                sb_.readers.append(e)
                st["pt_ev"] = e
            else:
                pr = PR.next()
                act.wait(st["qk_ev"], pr.free_evs())
                e = act.do(lambda a: a.activation(out=pr.ap[:, 0:n], in_=sb_.ap[:, 0:n], func=AF.Exp, scale=scale_c), sig=True)
                sb_.readers.append(e)
                c0 = st["t0"]
                dve.wait(e, s.ee_ev, pt.free_evs())
                e2 = dve.do(lambda v: v.tensor_tensor(out=pt.ap[:, 0:n], in0=pr.ap[:, 0:n], in1=s.EE[:, c0:c0 + n], op=ALU.mult), sig=True)
                pr.readers.append(e2)
                st["pt_ev"] = e2

        def emit_pv(st):
            s, n, kt, pt = hs[st["h"]], st["n"], st["kt"], st["PT"]
            if st["first"]:
                y = Yb.next()
                state["Y"] = y
                pe.wait(y.free_evs())
            y = state["Y"]
            pe.wait(st["pt_ev"])
            nqs = n // 128
            for qs in range(nqs):
                yap = y.ap[qs // 2][:, (qs % 2) * 256:(qs % 2) * 256 + 129]
                ev = pe.do(lambda t, qs=qs, yap=yap: t.matmul(yap, lhsT=pt.ap[:, qs * 128:(qs + 1) * 128], rhs=s.V[:, kt, :], start=(st["first"] and qs % 2 == 0), stop=st["last"]), sig=(qs == nqs - 1))
            pt.readers.append(ev)
            st["pv_ev"] = ev
            if st["lastg"] and st["last"]:
                s.readers.append(ev)

        def epilogue(st):
            y, n, q0, h = state["Y"], st["n"], st["q0"], st["h"]
            nqs = n // 128
            rc, yn = Rc.next(), Yn.next()
            dve.wait(st["pv_ev"], rc.free_evs(), yn.free_evs())
            for qs in range(nqs):
                e = dve.do(lambda v, qs=qs: v.reciprocal(out=rc.ap[:, qs:qs + 1], in_=y.ap[qs // 2][:, (qs % 2) * 256 + 128:(qs % 2) * 256 + 129]), sig=(qs == nqs - 1))
            dve.wait(e)
            for qs in range(nqs):
                e = dve.do(lambda v, qs=qs: v.tensor_scalar(out=yn.ap[:, qs, :], in0=y.ap[qs // 2][:, (qs % 2) * 256:(qs % 2) * 256 + 128], scalar1=rc.ap[:, qs:qs + 1], scalar2=None, op0=ALU.mult), sig=(qs == nqs - 1))
            y.readers.append(e)
            rc.readers.append(e)
            if st["g"] == 0:
                state["yt"] = YT.next()
                state["yt_fe"] = state["yt"].free_evs()
                state["yt_evs"] = []
            yt, yt_fe, yt_evs = state["yt"], state["yt_fe"], state["yt_evs"]
            lastg = st["lastg"]

            def pe_part():
                tb = Tb.next()
                pe.wait(e, tb.free_evs())
                for qs in range(nqs):
                    ep = pe.do(lambda t, qs=qs: t.transpose(out=tb.ap[:, qs * 128:(qs + 1) * 128], in_=yn.ap[:, qs, :], identity=ident[:]), sig=(qs == nqs - 1))
                yn.readers.append(ep)
                dve.wait(ep, yt_fe)
                ec = dve.do(lambda v: v.tensor_copy(out=yt.ap[:, q0:q0 + n], in_=tb.ap[:, 0:n]), sig=True)
                tb.readers.append(ec)
                yt_evs.append(ec)
                if lastg:
                    k.store(sp, yt, D_["yT"].ap()[br, h][:, 0:ntok_q], yt.ap[:, 0:ntok_q], list(yt_evs))
            return pe_part

        deferred = []
        for si in range(min(la, len(steps))):
            emit_qk(steps[si])
        for si, st in enumerate(steps):
            emit_exp(st)
            if si + la < len(steps):
                emit_qk(steps[si + la])
            emit_pv(st)
            if ada is not None and si % 48 == 47:
                next(ada, None)
            if ee_chunks:
                emit_ee_chunk()
            deferred = [(c - 1, f) for (c, f) in deferred]
            for c, f in [d for d in deferred if d[0] <= 0]:
                f()
            deferred = [d for d in deferred if d[0] > 0]
            if st["last"]:
                deferred.append((la + 1, epilogue(st)))
            if st["first"] and st["g"] == 0 and st["h"] + 1 < H and (st["h"] + 1) not in hs:
                hs[st["h"] + 1] = load_head(st["h"] + 1)
        for c, f in deferred:
            f()
        if ada is not None:
            for _ in ada:
                pass
        k.flush()


def phase_merge(k, l, ntq):
    pe, act, dve, pool, sp = k.pe, k.act, k.dve, k.pool, k.sp
    D_ = k.dram
    ngt = ntq // 2
    ntg = ngt * 128
    WB = D_["w_branch"].ap()[l]
    with ExitStack() as es:
        A3 = k.sb(es, "mg_A", [128, 48, ntg], BF16)
        ws = Ring([Slot(k.sb(es, "mg_w%d" % i, [128, 3, 16, 256], BF16), k.dsem()) for i in range(2)])
        gs = Ring([Slot(k.sb(es, "mg_g%d" % i, [128, 3, ntg], BF16), k.dsem()) for i in range(2)])
        accs = Ring([Slot(k.sb(es, "mg_acc%d" % i, [128, ntg], F32)) for i in range(2)])
        tmps = Ring([Slot(k.sb(es, "mg_tmp%d" % i, [128, 512], F32)) for i in range(3)])
        outs = Ring([Slot(k.sb(es, "mg_o%d" % i, [128, ntg], BF16), k.dsem()) for i in range(2)])
        pb = Ring([Slot(k.ps(es, "mg_p%d" % i, [128, 512], F32)) for i in range(6)])
        dA = k.dsem()
        a_readers = []
        groups = tok_groups(ntg)

        def wload(cs):
            s = ws.next()
            pool.wait(s.free_evs())
            for i in range(3):
                s.ready = pool.dma(s.dsem, s.ap[:, i, :, :], WB[i][:, cs * 256:(cs + 1) * 256].rearrange("(k p) n -> p k n", p=128))
            return s

        for tg in range(2):
            tok0 = tg * ntg
            sp.wait(a_readers)
            a_readers = []
            for i in range(3):
                evA = sp.dma(dA, A3[:, i * 16:(i + 1) * 16, :], D_["yT"].ap()[i][:, :, tok0:tok0 + ntg].rearrange("h p t -> p h t"))
            nxt = wload(0)
            for cs in range(8):
                cur = nxt
                if cs + 1 < 8:
                    nxt = wload(cs + 1)
                for cc in range(2):
                    c = cs * 2 + cc
                    g = gs.next()
                    sp.wait(g.free_evs())
                    g.ready = sp.dma(g.dsem, g.ap[:], D_["gateT"].ap()[:, c * 128:(c + 1) * 128, tok0:tok0 + ntg].rearrange("i p t -> p i t"))
                    acc, o = accs.next(), outs.next()
                    acc_fe, o_fe = acc.free_evs(), o.free_evs()
                    last = {}
                    oevs = []
                    for i in range(3):
                        for (g0, n) in groups:
                            b = pb.next()
                            pe.wait(cur.ready, evA, b.free_evs())
                            for hc in range(16):
                                ev = pe.do(lambda t, b=b, i=i, hc=hc, cc=cc, g0=g0, n=n, cur=cur: t.matmul(b.ap[:, 0:n], lhsT=cur.ap[:, i, hc, cc * 128:(cc + 1) * 128], rhs=A3[:, i * 16 + hc, g0:g0 + n],
                                                                                                          start=(hc == 0), stop=(hc == 15)), sig=(hc == 15))
                            cur.readers.append(ev)
                            a_readers.append(ev)
                            if i == 0:
                                dve.wait(ev, g.ready, acc_fe)
                                e = dve.do(lambda v, b=b, g=g, acc=acc, g0=g0, n=n: v.tensor_tensor(out=acc.ap[:, g0:g0 + n], in0=b.ap[:, 0:n], in1=g.ap[:, 0, g0:g0 + n], op=ALU.mult), sig=True)
                                b.readers.append(e)
                                last[g0] = e
                            else:
                                tmp = tmps.next()
                                dve.wait(ev, g.ready, tmp.free_evs())
                                e = dve.do(lambda v, b=b, g=g, tmp=tmp, i=i, g0=g0, n=n: v.tensor_tensor(out=tmp.ap[:, 0:n], in0=b.ap[:, 0:n], in1=g.ap[:, i, g0:g0 + n], op=ALU.mult), sig=True)
                                b.readers.append(e)
                                pool.wait(e, last[g0])
                                if i == 1:
                                    e2 = pool.do(lambda p_, acc=acc, tmp=tmp, g0=g0, n=n: p_.tensor_tensor(out=acc.ap[:, g0:g0 + n], in0=acc.ap[:, g0:g0 + n], in1=tmp.ap[:, 0:n], op=ALU.add), sig=True)
                                    last[g0] = e2
                                else:
                                    pool.wait(o_fe)
                                    e2 = pool.do(lambda p_, acc=acc, tmp=tmp, o=o, g0=g0, n=n: p_.tensor_tensor(out=o.ap[:, g0:g0 + n], in0=acc.ap[:, g0:g0 + n], in1=tmp.ap[:, 0:n], op=ALU.add), sig=True)
                                    oevs.append(e2)
                                    acc.readers.append(e2)
                                tmp.readers.append(e2)
                    g.readers.append(e)
                    k.store(sp, o, D_["accT"].ap()[c * 128:(c + 1) * 128, tok0:tok0 + ntg], o.ap[:, :], oevs)
        k.flush()


def phase_tm_out(k, l, which, ntq, xsrc=None):
    pe, act, dve, pool, sp = k.pe, k.act, k.dve, k.pool, k.sp
    D_ = k.dram
    with ExitStack() as es:
        if which == "out":
            KC, wc, Wd, ngroups, ngt = 16, 512, D_["w_out"].ap()[l], 1, ntq
            Asrc = D_["accT"].ap()
        else:
            KC, wc, Wd, ngroups, ngt = NFC, 256, D_["w_down"].ap()[l], 2, ntq // 2
            Asrc = D_["hidT"].ap()
        ntg = ngt * 128
        A = k.sb(es, "to_A", [128, KC, ntg], BF16)
        pj = Proj(k, es, A, KC, wc, 2, 4, "to")
        stg = Ring([Slot(k.sb(es, "to_st%d" % i, [128, 512], F32), k.dsem()) for i in range(3)])
        dA = k.dsem()
        tog = [0]
        gate, xsl, ev_g = None, None, None
        if xsrc is not None:
            mv_t = D_["mvec"]
            gate = {}
            dg = k.dsem()
            for j, nm in ((0, "lat"), (1, "ctx")):
                gate[j] = k.sb(es, "to_gate" + nm, [128, 2048], F32)
                ev_g = sp.dma(dg, gate[j][:], bc_ap(mv_t, (l * 2 + j) * 12288 + 5 * 2048, 2048))
            xsl = Ring([Slot(k.sb(es, "to_x%d" % i, [128, 256], F32), k.dsem()) for i in range(4)])
        a_readers = []
        for tg in range(ngroups):
            tok0 = tg * ntg
            sp.wait(a_readers)
            evA = sp.dma(dA, A[:], Asrc[:, tok0:tok0 + ntg].rearrange("(k p) t -> p k t", p=128))

            def mk_run(c0, evA=evA, tok0=tok0):
                def xload(i):
                    xs = xsl.next()
                    sp.wait(xs.free_evs())
                    xs.ready = sp.dma(xs.dsem, xs.ap[:, 0:wc], xsrc(tok0 // 128 + i)[:, c0:c0 + wc])
                    return xs

                def run(slab):
                    xq = [xload(i) for i in range(min(2, ngt))] if xsrc is not None else None
                    for i in range(ngt):
                        b, ev = pj.tm_tile(slab, wc, i, extra_wait=evA)
                        a_readers.append(ev)
                        st = stg.next()
                        if xsrc is None:
                            eng = act if tog[0] % 2 == 0 else dve
                            tog[0] += 1
                            eng.wait(ev, st.free_evs())
                            e2 = evac_copy(k, eng, st.ap[:, 0:wc], b.ap[:, 0:wc])
                            b.readers.append(e2)
                        else:
                            xs = xq.pop(0)
                            if i + 2 < ngt:
                                xq.append(xload(i + 2))
                            gt = gate[0 if (tok0 // 128 + i) < NLT else 1]
                            dve.wait(ev, ev_g, st.free_evs())
                            e1 = dve.do(lambda v, b=b, st=st, gt=gt: v.tensor_tensor(out=st.ap[:, 0:wc], in0=b.ap[:, 0:wc], in1=gt[:, c0:c0 + wc], op=ALU.mult), sig=True)
                            b.readers.append(e1)
                            dve.wait(e1, xs.ready)
                            e2 = dve.do(lambda v, st=st, xs=xs: v.scalar_tensor_tensor(out=st.ap[:, 0:wc], in0=xs.ap[:, 0:wc], scalar=ALPHA, in1=st.ap[:, 0:wc], op0=ALU.mult, op1=ALU.add), sig=True)
                            xs.readers.append(e2)
                        k.store(sp, st, D_["fout"].ap()[tok0 + i * 128:tok0 + (i + 1) * 128, c0:c0 + wc], st.ap[:, 0:wc], e2)
                return run

            for c0 in range(0, D, wc):
                pj.add("tm", (lambda s, c0=c0: (s.ap[:, :, 0:wc], Wd[:, c0:c0 + wc].rearrange("(k p) n -> p k n", p=128))), mk_run(c0))
            pj.run()
        k.flush()


def phase_resid(k, l, sub, src, ntiles, final, pre_added=False):
    pe, act, dve, pool, sp = k.pe, k.act, k.dve, k.pool, k.sp
    D_ = k.dram
    mv_t = D_["mvec"]
    with ExitStack() as es:
        d0 = k.dsem()
        gate = {}
        for j, nm in ((0, "lat"), (1, "ctx")):
            gate[j] = k.sb(es, "rs_gate" + nm, [128, 2048], F32)
            sp.dma(d0, gate[j][:], bc_ap(mv_t, (l * 2 + j) * 12288 + (2 + 3 * sub) * 2048, 2048))
        lng = k.sb(es, "rs_lng", [128, 2048], F32)
        lnb = k.sb(es, "rs_lnb", [128, 2048], F32)
        sp.dma(d0, lng[:], bc_ap(D_["ln_f_g" if sub else "ln_a_g"], l * 2048, 2048))
        evc = sp.dma(d0, lnb[:], bc_ap(D_["ln_f_b" if sub else "ln_a_b"], l * 2048, 2048))
        xr = Ring([Slot(k.sb(es, "rs_x%d" % i, [128, 2048], F32), k.dsem()) for i in range(2)])
        fr = Ring([Slot(k.sb(es, "rs_f%d" % i, [128, 2048], F32), k.dsem()) for i in range(2)])
        orr = Ring([Slot(k.sb(es, "rs_o%d" % i, [128, 2048], F32), k.dsem()) for i in range(2)])
        sm = Ring([Slot(k.sb(es, "rs_sm%d" % i, [128, 32], F32)) for i in range(2)])
        fout = D_["fout"].ap()

        def load(i):
            xs, fs = xr.next(), fr.next()
            sp.wait(xs.free_evs(), fs.free_evs())
            if pre_added:
                xs.ready = sp.dma(xs.dsem, xs.ap[:], fout[i * 128:(i + 1) * 128, :])
                fs.ready = None
            else:
                xs.ready = sp.dma(xs.dsem, xs.ap[:], src(i))
                fs.ready = sp.dma(fs.dsem, fs.ap[:], fout[i * 128:(i + 1) * 128, :])
            return xs, fs

        nxt = load(0)
        for i in range(ntiles):
            xs, fs = nxt
            if i + 1 < ntiles:
                nxt = load(i + 1)
            gt = gate[0 if i < NLT else 1]
            dve.wait(xs.ready, fs.ready, evc)
            e = None
            if not pre_added:
                e = dve.do(lambda v, fs=fs, gt=gt: v.tensor_tensor(out=fs.ap[:], in0=fs.ap[:], in1=gt[:], op=ALU.mult), sig=True)
                dve.wait(e)
                e = dve.do(lambda v, xs=xs, fs=fs: v.scalar_tensor_tensor(out=xs.ap[:], in0=xs.ap[:], scalar=ALPHA, in1=fs.ap[:], op0=ALU.mult, op1=ALU.add), sig=True)
            smt = sm.next()
            st = smt.ap
            stats = st[:, 0:24].rearrange("p (c s) -> p c s", s=6)
            mv, rstd, nmr = st[:, 24:26], st[:, 26:27], st[:, 27:28]
            dve.wait(e, smt.free_evs())
            e = ln_stats(k, xs.ap, mv, rstd, nmr, POST_EPS, stats)
            act.wait(e)
            e = act.do(lambda a, xs=xs, fs=fs, rstd=rstd, nmr=nmr: a.activation(out=fs.ap[:], in_=xs.ap[:], func=AF.Identity, bias=nmr, scale=rstd), sig=True)
            xs.readers.append(e)
            smt.readers.append(e)
            dve.wait(e)
            e = dve.do(lambda v, fs=fs: v.tensor_tensor(out=fs.ap[:], in0=fs.ap[:], in1=lng[:], op=ALU.mult), sig=True)
            os_ = orr.next()
            pool.wait(e, os_.free_evs())
            e = pool.do(lambda g, fs=fs, os_=os_: g.tensor_tensor(out=os_.ap[:], in0=fs.ap[:], in1=lnb[:], op=ALU.add), sig=True)
            fs.readers.append(e)
            if final:
                if i < NLT:
                    k.store(sp, os_, k.out.ap()[i * 128:(i + 1) * 128, :], os_.ap[:], e)
            else:
                k.store(sp, os_, D_["xres"].ap()[i * 128:(i + 1) * 128, :], os_.ap[:], e)
        k.flush()


def phase_ffn_up(k, l, hT, ntq, ada_l=None):
    pe, act, dve, pool, sp = k.pe, k.act, k.dve, k.pool, k.sp
    D_ = k.dram
    Wu = D_["w_up"].ap()[l]
    ntok = ntq * 128
    has_ctx = ntq == NT
    GW = T + 4
    with ExitStack() as es:
        pj = Proj(k, es, hT, 16, 512, 2, 4, "fu")
        Gb = Ring([Slot(k.sb(es, "fu_g%d" % i, [128, GW], F32)) for i in range(2)])
        Vb = Ring([Slot(k.sb(es, "fu_v%d" % i, [128, T], F32)) for i in range(2)])
        Cb = Ring([Slot(k.sb(es, "fu_c%d" % i, [128, T], F32)) for i in range(2)])
        Ho = Ring([Slot(k.sb(es, "fu_h%d" % i, [128, T], BF16), k.dsem()) for i in range(2)])
        cp = k.sb(es, "fu_cp", [128, NFC, 4], F32)
        d0 = k.dsem()
        evcp = sp.dma(d0, cp[:], D_["convp"].ap()[l])
        ev_ms = None
        for s in Gb.slots:
            ev_ms = pool.do(lambda g, s=s: g.memset(s.ap[:], 0.0), sig=True)
        groups = tok_groups(ntok)
        segs = [(0, S, 0)] + ([(S, CL, S + 2)] if has_ctx else [])

        def gcol(g0):
            return 1 + g0 if g0 < S else S + 3 + (g0 - S)

        def mk_run(sp_):
            def run(slab):
                for cc in range(2):
                    ch = sp_ * 2 + cc
                    gb, vb, cb, ho = Gb.next(), Vb.next(), Cb.next(), Ho.next()
                    gfe, vfe = gb.free_evs(), vb.free_evs()
                    gevs, vevs = [], []
                    for (g0, n) in groups:
                        b, ev = pj.fm_group(slab, cc, g0, n)
                        act.wait(ev, gfe, ev_ms)
                        e = evac_copy(k, act, gb.ap[:, gcol(g0):gcol(g0) + n], b.ap[:, 0:n])
                        b.readers.append(e)
                        gevs.append(e)
                    for (g0, n) in groups:
                        b, ev = pj.fm_group(slab, 2 + cc, g0, n)
                        act.wait(ev, vfe)
                        e = evac_copy(k, act, vb.ap[:, g0:g0 + n], b.ap[:, 0:n])
                        b.readers.append(e)
                        vevs.append(e)
                    w = [cp[:, ch, j:j + 1] for j in range(4)]
                    dve.wait(gevs, evcp, cb.free_evs())
                    for (t0, n, c0) in segs:
                        e = dve.do(lambda v, gb=gb, cb=cb, t0=t0, n=n, c0=c0, w=w: v.tensor_scalar(out=cb.ap[:, t0:t0 + n], in0=gb.ap[:, c0:c0 + n], scalar1=w[0], scalar2=w[3], op0=ALU.mult, op1=ALU.add), sig=True)
                    for j in (1, 2):
                        dve.wait(e)
                        for (t0, n, c0) in segs:
                            e = dve.do(lambda v, gb=gb, cb=cb, t0=t0, n=n, c0=c0, w=w, j=j: v.scalar_tensor_tensor(out=cb.ap[:, t0:t0 + n], in0=gb.ap[:, c0 + j:c0 + j + n], scalar=w[j], in1=cb.ap[:, t0:t0 + n],
                                                                                                                     op0=ALU.mult, op1=ALU.add), sig=True)
                    gb.readers.append(e)
                    act.wait(e)
                    e = act.do(lambda a, cb=cb: a.activation(out=cb.ap[:, 0:ntok], in_=cb.ap[:, 0:ntok], func=AF.Silu), sig=True)
                    pool.wait(e, vevs, ho.free_evs())
                    e = pool.do(lambda g, cb=cb, vb=vb, ho=ho: g.tensor_tensor(out=ho.ap[:, 0:ntok], in0=cb.ap[:, 0:ntok], in1=vb.ap[:, 0:ntok], op=ALU.mult), sig=True)
                    cb.readers.append(e)
                    vb.readers.append(e)
                    k.store(sp, ho, D_["hidT"].ap()[ch * 128:(ch + 1) * 128, 0:ntok], ho.ap[:, 0:ntok], e)
                    if ada is not None and ch >= 2 and ch % 2 == 0:
                        next(ada, None)
                        next(ada, None)
                        if ch % 4 == 0:
                            next(ada, None)
            return run

        ada = ada_steps(k, es, ada_l, nbanks=1, sw=256) if ada_l is not None else None
        for sp_ in range(NFC // 2):
            pj.add("fm", (lambda s, sp_=sp_: [(s.ap[:, :, 0:256], Wu[:, sp_ * 256:(sp_ + 1) * 256].rearrange("(k p) n -> p k n", p=128)),
                                              (s.ap[:, :, 256:512], Wu[:, DFF + sp_ * 256:DFF + (sp_ + 1) * 256].rearrange("(k p) n -> p k n", p=128))]), mk_run(sp_))
        pj.run()
        if ada is not None:
            for _ in ada:
                pass
        k.flush()


def phase_out_resid(k, l, src, ntiles):
    pe, act, dve, pool, sp = k.pe, k.act, k.dve, k.pool, k.sp
    D_ = k.dram
    mv_t = D_["mvec"]
    Wd = D_["w_out"].ap()[l]
    accT = D_["accT"].ap()
    with ExitStack() as es:
        Wsb = k.sb(es, "or_w", [128, 16, D], BF16)
        dW = [k.dsem() for _ in range(4)]
        evW = [pool.dma(dW[n], Wsb[:, :, n * 512:(n + 1) * 512], Wd[:, n * 512:(n + 1) * 512].rearrange("(k p) n -> p k n", p=128)) for n in range(4)]
        d0 = k.dsem()
        gate = {}
        for j, nm in ((0, "lat"), (1, "ctx")):
            gate[j] = k.sb(es, "or_gate" + nm, [128, 2048], F32)
            sp.dma(d0, gate[j][:], bc_ap(mv_t, (l * 2 + j) * 12288 + 2 * 2048, 2048))
        lng = k.sb(es, "or_lng", [128, 2048], F32)
        lnb = k.sb(es, "or_lnb", [128, 2048], F32)
        sp.dma(d0, lng[:], bc_ap(D_["ln_a_g"], l * 2048, 2048))
        evc = sp.dma(d0, lnb[:], bc_ap(D_["ln_a_b"], l * 2048, 2048))
        ar = Ring([Slot(k.sb(es, "or_a%d" % i, [128, 16, 128], BF16), k.dsem()) for i in range(3)])
        xr = Ring([Slot(k.sb(es, "or_x%d" % i, [128, 2048], F32), k.dsem()) for i in range(3)])
        fr = Ring([Slot(k.sb(es, "or_f%d" % i, [128, 2048], F32)) for i in range(2)])
        orr = Ring([Slot(k.sb(es, "or_o%d" % i, [128, 2048], F32), k.dsem()) for i in range(2)])
        sm = Ring([Slot(k.sb(es, "or_sm%d" % i, [128, 32], F32)) for i in range(2)])
        pbs = Ring([Slot([k.ps(es, "or_p%d_%d" % (i, n), [128, 512], F32) for n in range(4)]) for i in range(2)])

        def load(i):
            a, xs = ar.next(), xr.next()
            sp.wait(a.free_evs(), xs.free_evs())
            a.ready = sp.dma(a.dsem, a.ap[:], accT[:, i * 128:(i + 1) * 128].rearrange("(k p) t -> p k t", p=128))
            xs.ready = sp.dma(xs.dsem, xs.ap[:], src(i))
            return a, xs

        q = [load(0)]
        if ntiles > 1:
            q.append(load(1))
        for i in range(ntiles):
            a, xs = q.pop(0)
            if i + 2 < ntiles:
                q.append(load(i + 2))
            pb = pbs.next()
            pe.wait(a.ready, pb.free_evs())
            for n in range(4):
                pe.wait(evW[n])
                for kk in range(16):
                    ev = pe.do(lambda t, pb=pb, a=a, n=n, kk=kk: t.matmul(pb.ap[n][:, :], lhsT=a.ap[:, kk, :], rhs=Wsb[:, kk, n * 512:(n + 1) * 512], start=(kk == 0), stop=(kk == 15)), sig=(kk == 15))
            a.readers.append(ev)
            gt = gate[0 if i < NLT else 1]
            fs = fr.next()
            dve.wait(ev, evc, fs.free_evs())
            for n in range(4):
                e = dve.do(lambda v, pb=pb, fs=fs, gt=gt, n=n: v.tensor_tensor(out=fs.ap[:, n * 512:(n + 1) * 512], in0=pb.ap[n][:, :], in1=gt[:, n * 512:(n + 1) * 512], op=ALU.mult), sig=(n == 3))
            pb.readers.append(e)
            dve.wait(e, xs.ready)
            e = dve.do(lambda v, xs=xs, fs=fs: v.scalar_tensor_tensor(out=xs.ap[:], in0=xs.ap[:], scalar=ALPHA, in1=fs.ap[:], op0=ALU.mult, op1=ALU.add), sig=True)
            smt = sm.next()
            st = smt.ap
            stats = st[:, 0:24].rearrange("p (c s) -> p c s", s=6)
            mv, rstd, nmr = st[:, 24:26], st[:, 26:27], st[:, 27:28]
            dve.wait(e, smt.free_evs())
            e = ln_stats(k, xs.ap, mv, rstd, nmr, POST_EPS, stats)
            act.wait(e)
            e = act.do(lambda a_, xs=xs, fs=fs, rstd=rstd, nmr=nmr: a_.activation(out=fs.ap[:], in_=xs.ap[:], func=AF.Identity, bias=nmr, scale=rstd), sig=True)
            xs.readers.append(e)
            smt.readers.append(e)
            dve.wait(e)
            e = dve.do(lambda v, fs=fs: v.tensor_tensor(out=fs.ap[:], in0=fs.ap[:], in1=lng[:], op=ALU.mult), sig=True)
            os_ = orr.next()
            pool.wait(e, os_.free_evs())
            e = pool.do(lambda g, fs=fs, os_=os_: g.tensor_tensor(out=os_.ap[:], in0=fs.ap[:], in1=lnb[:], op=ALU.add), sig=True)
            fs.readers.append(e)
            k.store(sp, os_, D_["xres"].ap()[i * 128:(i + 1) * 128, :], os_.ap[:], e)
        k.flush()
```
